# Optimizing a Trainium2 kernel written in Bass

```python
import jax, jax.numpy as jnp
from jax import lax
import numpy as np

D_MODEL = 1024
BATCH = 2
SEQ = 16384
DEPTH = 4

D_MIX = D_MODEL
GROUP_W = D_MIX // 4
HEAD_DIM = 64
N_HEADS_G = GROUP_W // HEAD_DIM
CONV_W = 4
LRU_C = 8.0
DIL_PAIRS = ((128, 1), (512, 4), (2048, 16))
ROPE_THETA = 10000.0
DN_CHUNK = 64
N_EXPERTS = 16
EC_FACTOR = 2
D_FF_EXPERT = 2688
DN_ALPHA = (2.0 * DEPTH) ** 0.25
DN_BETA = (8.0 * DEPTH) ** -0.25
LN_EPS = 1e-5
RMS_EPS = 1e-6
NEG = -1e30

IN_SIZES = (GROUP_W,) * 10 + (2 * N_HEADS_G, 2 * N_HEADS_G)
D_IN = int(sum(IN_SIZES))
SPLIT_POINTS = tuple(int(v) for v in np.cumsum(IN_SIZES)[:-1])

kernel_name = 'hybrid_fourier_lru_dilattn_deltanet_ec_moe'


def layer_norm(x, g, b):
    xf = x.astype(jnp.float32)
    mu = jnp.mean(xf, -1, keepdims=True)
    var = jnp.mean(jnp.square(xf - mu), -1, keepdims=True)
    return ((xf - mu) * lax.rsqrt(var + LN_EPS) * g + b).astype(x.dtype)


def centred_dwconv(x, w, b):
    c = x.shape[-1]
    y = lax.conv_general_dilated(
        x, w[:, None, :].astype(x.dtype), window_strides=(1,),
        padding=[(CONV_W // 2, CONV_W - 1 - CONV_W // 2)],
        dimension_numbers=('NWC', 'WIO', 'NWC'), feature_group_count=c)
    return y + b.astype(x.dtype)


def fourier_mixer(xa, w_f, b_f):
    bsz, slen, _ = xa.shape
    xh = xa.astype(jnp.float32).reshape(bsz, slen, N_HEADS_G, HEAD_DIM)
    f = jnp.fft.fft2(xh, axes=(1, 3), norm='ortho').real
    y = jnp.einsum('bshc,hcd->bshd', f, w_f) + b_f
    return y.reshape(bsz, slen, GROUP_W)


def rglru_scan(xc, w_a, b_a, w_x, b_x, lam, reverse):
    bsz, slen, _ = xc.shape
    xh = xc.reshape(bsz, slen, N_HEADS_G, HEAD_DIM)
    r = jax.nn.sigmoid(jnp.einsum('bshc,hcd->bshd', xh, w_a).reshape(bsz, slen, GROUP_W) + b_a)
    i = jax.nn.sigmoid(jnp.einsum('bshc,hcd->bshd', xh, w_x).reshape(bsz, slen, GROUP_W) + b_x)
    log_a = -LRU_C * r * jax.nn.softplus(-lam)
    a = jnp.exp(log_a)
    u = jnp.sqrt(-jnp.expm1(2.0 * log_a)) * (i * xc)

    def combine(lhs, rhs):
        a1, u1 = lhs
        a2, u2 = rhs
        return a1 * a2, a2 * u1 + u2

    _, h = lax.associative_scan(combine, (a, u), reverse=reverse, axis=1)
    return h


def rglru_mixer(rec_in, rec_gate, conv_w, conv_b, w_a, b_a, w_x, b_x, lam):
    xc = centred_dwconv(rec_in.astype(jnp.float32), conv_w, conv_b).astype(jnp.float32)
    h = (rglru_scan(xc, w_a[0], b_a[0], w_x[0], b_x[0], lam[0], False)
         + rglru_scan(xc, w_a[1], b_a[1], w_x[1], b_x[1], lam[1], True))
    return h * jax.nn.gelu(rec_gate.astype(jnp.float32))


def rope(x, pos):
    half = HEAD_DIM // 2
    inv = ROPE_THETA ** (-jnp.arange(half, dtype=jnp.float32) / half)
    ang = pos.astype(jnp.float32)[..., None] * inv
    cos = jnp.cos(ang)[:, :, None, :]
    sin = jnp.sin(ang)[:, :, None, :]
    x1, x2 = x[..., :half], x[..., half:]
    return jnp.concatenate([x1 * cos - x2 * sin, x2 * cos + x1 * sin], axis=-1)


def dilated_window_attn(q, k, v, dil, radius):
    bsz, slen, nh, hd = q.shape
    sub_len = slen // dil
    nb = -(-sub_len // radius)
    sub_pad = nb * radius

    def to_blocks(t):
        t = t.reshape(bsz, sub_len, dil, nh, hd).transpose(0, 2, 1, 3, 4)
        t = jnp.pad(t, ((0, 0), (0, 0), (0, sub_pad - sub_len), (0, 0), (0, 0)))
        return t.reshape(bsz, dil, nb, radius, nh, hd)

    def neighbours(t):
        tp = jnp.pad(t, ((0, 0), (0, 0), (1, 1), (0, 0), (0, 0), (0, 0)))
        return jnp.concatenate([tp[:, :, :-2], tp[:, :, 1:-1], tp[:, :, 2:]], axis=3)

    qb = to_blocks(q)
    kn = neighbours(to_blocks(k))
    vn = neighbours(to_blocks(v))
    s = jnp.einsum('bgnqhe,bgnkhe->bgnhqk', qb, kn) * (hd ** -0.5)
    qi = jnp.arange(radius)[:, None]
    kj = jnp.arange(3 * radius)[None, :]
    band = jnp.abs(kj - radius - qi) <= radius
    kglob = (jnp.arange(nb)[:, None] - 1) * radius + jnp.arange(3 * radius)[None, :]
    inrange = (kglob >= 0) & (kglob < sub_len)
    mask = band[None, :, :] & inrange[:, None, :]
    s = jnp.where(mask[None, None, :, None], s, NEG)
    lse = jax.nn.logsumexp(s, axis=-1)
    p = jnp.exp(s - lse[..., None])
    o = jnp.einsum('bgnhqk,bgnkhe->bgnqhe', p, vn)
    o = o.reshape(bsz, dil, sub_pad, nh, hd)[:, :, :sub_len]
    o = o.transpose(0, 2, 1, 3, 4).reshape(bsz, slen, nh, hd)
    lse = lse.transpose(0, 1, 2, 4, 3).reshape(bsz, dil, sub_pad, nh)[:, :, :sub_len]
    lse = lse.transpose(0, 2, 1, 3).reshape(bsz, slen, nh)
    return o, lse


def dilated_attention_mixer(cq, ck, cv, positions):
    bsz, slen, _ = cq.shape

    def heads(t):
        return t.astype(jnp.float32).reshape(bsz, slen, N_HEADS_G, HEAD_DIM)

    q = rope(heads(cq), positions)
    k = rope(heads(ck), positions)
    v = heads(cv)
    outs, lses = [], []
    for window, dil in DIL_PAIRS:
        o, lse = dilated_window_attn(q, k, v, dil, (window // 2) // dil)
        outs.append(o)
        lses.append(lse)
    wts = jax.nn.softmax(jnp.stack(lses, axis=0), axis=0)
    o = jnp.einsum('gbsh,gbshe->bshe', wts, jnp.stack(outs, axis=0))
    return o.reshape(bsz, slen, GROUP_W)


def chunk_gated_delta(q, k, v, g, beta):
    bsz, slen, nh, dk = q.shape
    dv = v.shape[-1]
    cl = DN_CHUNK
    nc = slen // cl

    def chunks(t):
        t = t.reshape((bsz, nc, cl, nh) + t.shape[3:])
        return jnp.moveaxis(t, 3, 1)

    q = chunks(q) * (dk ** -0.5)
    k = chunks(k)
    v = chunks(v)
    g = jnp.cumsum(chunks(g), axis=-1)
    beta = chunks(beta)
    kb = k * beta[..., None]
    vb = v * beta[..., None]
    incl = jnp.tril(jnp.ones((cl, cl), dtype=bool))
    strict = jnp.tril(jnp.ones((cl, cl), dtype=bool), -1)
    diff = g[..., :, None] - g[..., None, :]
    decay = jnp.where(incl, jnp.exp(jnp.where(incl, diff, 0.0)), 0.0)
    m = jnp.einsum('bhnid,bhnjd->bhnij', kb, k) * jnp.where(strict, decay, 0.0)
    eye = jnp.eye(cl, dtype=q.dtype)
    tmat = lax.linalg.triangular_solve(eye + m, jnp.broadcast_to(eye, m.shape),
                                       left_side=True, lower=True)
    w = tmat @ vb
    u = tmat @ (kb * jnp.exp(g)[..., None])
    a_intra = jnp.einsum('bhnid,bhnjd->bhnij', q, k) * decay
    q_dec = q * jnp.exp(g)[..., None]
    g_last = g[..., -1]
    k_dec = k * jnp.exp(g_last[..., None] - g)[..., None]

    def step(state, xs):
        w_i, u_i, qd_i, kd_i, a_i, gl_i = xs
        v_new = w_i - u_i @ state
        o_i = qd_i @ state + a_i @ v_new
        state = state * jnp.exp(gl_i)[..., None, None] + jnp.einsum('bhck,bhcv->bhkv', kd_i, v_new)
        return state, o_i

    xs = tuple(jnp.moveaxis(t, 2, 0) for t in (w, u, q_dec, k_dec, a_intra, g_last))
    s0 = jnp.zeros((bsz, nh, dk, dv), q.dtype)
    _, o = lax.scan(step, s0, xs)
    o = jnp.moveaxis(o, 0, 2)
    return jnp.moveaxis(o, 1, 3).reshape(bsz, slen, nh, dv)


def l2norm(t):
    return t * lax.rsqrt(jnp.sum(t * t, axis=-1, keepdims=True) + RMS_EPS)


def gated_deltanet_mixer(dq, dk, dv, dg, dbeta, ddecay, conv_w, conv_b, a_log, dt_bias, norm_w):
    bsz, slen, _ = dq.shape
    f32 = jnp.float32
    qkv = jax.nn.silu(centred_dwconv(jnp.concatenate([dq, dk, dv], axis=-1).astype(f32), conv_w, conv_b))

    def heads(t):
        return t.astype(f32).reshape(bsz, slen, N_HEADS_G, HEAD_DIM)

    q, k, v = (heads(t) for t in jnp.split(qkv, 3, axis=-1))
    q = l2norm(q)
    k = l2norm(k)
    beta = jax.nn.sigmoid(dbeta.astype(f32))
    g = -jnp.exp(a_log.reshape(-1).astype(f32)) * jax.nn.softplus(
        ddecay.astype(f32) + dt_bias.reshape(-1))
    nh = N_HEADS_G

    def flip(t):
        return jnp.flip(t, axis=1)

    o_f = chunk_gated_delta(q, k, v, g[..., :nh], beta[..., :nh])
    o_b = flip(chunk_gated_delta(flip(q), flip(k), flip(v), flip(g[..., nh:]), flip(beta[..., nh:])))
    o = o_f + o_b
    o = o * lax.rsqrt(jnp.mean(o * o, axis=-1, keepdims=True) + RMS_EPS) * norm_w
    o = o * jax.nn.silu(heads(dg))
    return o.reshape(bsz, slen, GROUP_W)


def hybrid_mixer(x, positions, w_in, w_out, fno_w, fno_b, lru_conv_w, lru_conv_b, lru_wa, lru_ba,
                 lru_wx, lru_bx, lru_lam, dn_conv_w, dn_conv_b, dn_a_log, dn_dt_bias, dn_norm_w):
    parts = jnp.split(x @ w_in, SPLIT_POINTS, axis=-1)
    (a_x, rec_in, rec_gate, c_q, c_k, c_v, d_q, d_k, d_v, d_g, d_beta, d_decay) = parts
    y_a = fourier_mixer(a_x, fno_w, fno_b).astype(x.dtype)
    y_b = rglru_mixer(rec_in, rec_gate, lru_conv_w, lru_conv_b, lru_wa, lru_ba,
                      lru_wx, lru_bx, lru_lam).astype(x.dtype)
    y_c = dilated_attention_mixer(c_q, c_k, c_v, positions).astype(x.dtype)
    y_d = gated_deltanet_mixer(d_q, d_k, d_v, d_g, d_beta, d_decay, dn_conv_w, dn_conv_b,
                               dn_a_log, dn_dt_bias, dn_norm_w).astype(x.dtype)
    y = jnp.concatenate([y_a, y_b, y_c, y_d], axis=-1)
    return y @ w_out


def expert_choice_ffn(x, router_w, w1, w3, w2):
    bsz, slen, dm = x.shape
    cap = EC_FACTOR * slen // N_EXPERTS
    aff = jax.nn.softmax(jnp.einsum('bsd,de->bse', x, router_w).astype(jnp.float32), axis=-1)
    gate, idx = lax.top_k(jnp.swapaxes(aff, 1, 2), cap)
    idx_flat = idx.reshape(bsz, -1)
    xg = jax.vmap(lambda xb, ib: xb[ib])(x, idx_flat).reshape(bsz, N_EXPERTS, cap, dm)
    h = jax.nn.silu(jnp.einsum('becd,edf->becf', xg, w1)) * jnp.einsum('becd,edf->becf', xg, w3)
    y = jnp.einsum('becf,efd->becd', h, w2) * gate[..., None].astype(x.dtype)
    out = jax.vmap(lambda ib, ub: jnp.zeros((slen, dm), x.dtype).at[ib].add(ub))(
        idx_flat, y.reshape(bsz, -1, dm))
    return out


def setup_inputs(seed: int = 0) -> dict:
    key = jax.random.key(seed)
    ks = jax.random.split(key, 32)
    f32 = jnp.float32
    nh, hd = N_HEADS_G, HEAD_DIM

    def nrm(k, shape, scale):
        return jax.random.normal(k, shape, f32) * scale

    x = nrm(ks[0], (BATCH, SEQ, D_MODEL), 1.0)
    positions = (jax.random.randint(ks[1], (BATCH, 1), 0, 4096, dtype=jnp.int32)
                 + jnp.arange(SEQ, dtype=jnp.int32)[None, :])
    w_in = nrm(ks[2], (DEPTH, D_MODEL, D_IN), D_MODEL ** -0.5)
    w_out = nrm(ks[3], (DEPTH, D_MIX, D_MODEL), DN_BETA * D_MIX ** -0.5)
    fno_w = nrm(ks[4], (DEPTH, nh, hd, hd), hd ** -0.5)
    fno_b = nrm(ks[5], (DEPTH, nh, hd), 0.02)
    lru_conv_w = nrm(ks[6], (DEPTH, CONV_W, GROUP_W), CONV_W ** -0.5)
    lru_conv_b = nrm(ks[7], (DEPTH, GROUP_W), 0.02)
    lru_wa = nrm(ks[8], (DEPTH, 2, nh, hd, hd), hd ** -0.5)
    lru_ba = nrm(ks[9], (DEPTH, 2, GROUP_W), 0.02)
    lru_wx = nrm(ks[10], (DEPTH, 2, nh, hd, hd), hd ** -0.5)
    lru_bx = nrm(ks[11], (DEPTH, 2, GROUP_W), 0.02)
    a_c = jax.random.uniform(ks[12], (DEPTH, 2, GROUP_W), f32, 0.9, 0.999) ** (1.0 / LRU_C)
    lru_lam = jnp.log(a_c) - jnp.log1p(-a_c)
    dn_conv_w = nrm(ks[13], (DEPTH, CONV_W, 3 * GROUP_W), CONV_W ** -0.5)
    dn_conv_b = nrm(ks[14], (DEPTH, 3 * GROUP_W), 0.02)
    dn_a_log = jnp.log(jax.random.uniform(ks[15], (DEPTH, 2, nh), f32, 1.0, 16.0))
    dt = jnp.exp(jax.random.uniform(ks[16], (DEPTH, 2, nh), f32, float(np.log(1e-3)), float(np.log(1e-1))))
    dn_dt_bias = dt + jnp.log(-jnp.expm1(-dt))
    dn_norm_w = 1.0 + nrm(ks[17], (DEPTH, hd), 0.02)
    ln1_g = 1.0 + nrm(ks[18], (DEPTH, D_MODEL), 0.02)
    ln1_b = nrm(ks[19], (DEPTH, D_MODEL), 0.02)
    router_w = nrm(ks[20], (DEPTH, D_MODEL, N_EXPERTS), D_MODEL ** -0.5)
    exp_w1 = nrm(ks[21], (DEPTH, N_EXPERTS, D_MODEL, D_FF_EXPERT), D_MODEL ** -0.5)
    exp_w3 = nrm(ks[22], (DEPTH, N_EXPERTS, D_MODEL, D_FF_EXPERT), D_MODEL ** -0.5)
    exp_w2 = nrm(ks[23], (DEPTH, N_EXPERTS, D_FF_EXPERT, D_MODEL), DN_BETA * D_FF_EXPERT ** -0.5)
    ln2_g = 1.0 + nrm(ks[24], (DEPTH, D_MODEL), 0.02)
    ln2_b = nrm(ks[25], (DEPTH, D_MODEL), 0.02)
    return {'x': x, 'positions': positions, 'w_in': w_in, 'w_out': w_out,
            'fno_w': fno_w, 'fno_b': fno_b, 'lru_conv_w': lru_conv_w, 'lru_conv_b': lru_conv_b,
            'lru_wa': lru_wa, 'lru_ba': lru_ba, 'lru_wx': lru_wx, 'lru_bx': lru_bx,
            'lru_lam': lru_lam, 'dn_conv_w': dn_conv_w, 'dn_conv_b': dn_conv_b,
            'dn_a_log': dn_a_log, 'dn_dt_bias': dn_dt_bias, 'dn_norm_w': dn_norm_w,
            'ln1_g': ln1_g, 'ln1_b': ln1_b, 'router_w': router_w, 'exp_w1': exp_w1,
            'exp_w3': exp_w3, 'exp_w2': exp_w2, 'ln2_g': ln2_g, 'ln2_b': ln2_b}


def reference(x, positions, w_in, w_out, fno_w, fno_b, lru_conv_w, lru_conv_b, lru_wa, lru_ba,
              lru_wx, lru_bx, lru_lam, dn_conv_w, dn_conv_b, dn_a_log, dn_dt_bias, dn_norm_w,
              ln1_g, ln1_b, router_w, exp_w1, exp_w3, exp_w2, ln2_g, ln2_b):
    for l in range(DEPTH):
        mix = hybrid_mixer(x, positions, w_in[l], w_out[l], fno_w[l], fno_b[l],
                           lru_conv_w[l], lru_conv_b[l], lru_wa[l], lru_ba[l], lru_wx[l],
                           lru_bx[l], lru_lam[l], dn_conv_w[l], dn_conv_b[l], dn_a_log[l],
                           dn_dt_bias[l], dn_norm_w[l])
        x = layer_norm(DN_ALPHA * x + mix, ln1_g[l], ln1_b[l])
        moe = expert_choice_ffn(x, router_w[l], exp_w1[l], exp_w3[l], exp_w2[l])
        x = layer_norm(DN_ALPHA * x + moe, ln2_g[l], ln2_b[l])
    return x
```

```python
import numpy as np
import concourse.bass as bass
import concourse.mybir as mybir

F32 = mybir.dt.float32
BF16 = mybir.dt.bfloat16
I32 = mybir.dt.int32
AF = mybir.ActivationFunctionType
ALU = mybir.AluOpType
AX = mybir.AxisListType


class Res:
    __slots__ = ("lastw", "rd_c", "rd_d")

    def __init__(self):
        self.lastw = None
        self.rd_c = {}
        self.rd_d = []


class EM:
    CE = ("pe", "act", "dve", "pool")

    def __init__(self, nc, ndma=96, same_eng_sync=True):
        self.nc = nc
        self.eng = dict(pe=nc.tensor, act=nc.scalar, dve=nc.vector, pool=nc.gpsimd, sp=nc.sync)
        self.sem = {e: nc.alloc_semaphore("cs_" + e) for e in self.CE}
        self.cnt = {e: 0 for e in self.CE}
        self.seen = {e: {f: 0 for f in self.CE} for e in self.eng}
        self.dsem = [nc.alloc_semaphore("ds%d" % i) for i in range(ndma)]
        self.dcnt = [0] * ndma
        self.dseen = {e: [0] * ndma for e in self.eng}
        self.drr = 0
        self.res = {}
        self.same = same_eng_sync
        self.nins = 0

    def _res(self, k):
        r = self.res.get(k)
        if r is None:
            r = self.res[k] = Res()
        return r

    def _wait(self, e, tok):
        if tok is None:
            return
        if tok[0] == "c":
            _, f, n = tok
            if f == e and (not self.same or e == "pe"):
                return
            if self.seen[e][f] >= n:
                return
            self.eng[e].wait_ge(self.sem[f], n)
            self.seen[e][f] = n
        else:
            _, si, v = tok
            if self.dseen[e][si] >= v:
                return
            self.eng[e].wait_ge(self.dsem[si], v)
            self.dseen[e][si] = v

    def _deps(self, e, reads, writes):
        for k in reads:
            self._wait(e, self._res(k).lastw)
        for k in writes:
            r = self._res(k)
            self._wait(e, r.lastw)
            for f, n in r.rd_c.items():
                self._wait(e, ("c", f, n))
            for t in r.rd_d:
                self._wait(e, t)

    def _record(self, tok, reads, writes):
        for k in reads:
            r = self._res(k)
            if tok[0] == "c":
                if r.rd_c.get(tok[1], 0) < tok[2]:
                    r.rd_c[tok[1]] = tok[2]
            else:
                r.rd_d.append(tok)
        for k in writes:
            r = self._res(k)
            r.lastw = tok
            r.rd_c = {}
            r.rd_d = []

    def op(self, e, fn, reads=(), writes=()):
        pr = [k for k in reads if isinstance(k, tuple) and isinstance(k[0], str) and k[0].startswith("ps")]
        if pr:
            writes = list(writes) + pr
        self._deps(e, reads, writes)
        ins = fn(self.eng[e])
        self.cnt[e] += 1
        ins.then_inc(self.sem[e], 1)
        tok = ("c", e, self.cnt[e])
        self._record(tok, reads, writes)
        self.nins += 1
        return tok

    def dma(self, q, out, in_, reads=(), writes=(), fn=None, **kw):
        si = self.drr
        self.drr = (self.drr + 1) % len(self.dsem)
        if self.dcnt[si] > 0:
            self._wait(q, ("d", si, 16 * self.dcnt[si]))
        self._deps(q, reads, writes)
        if fn is None:
            ins = self.eng[q].dma_start(out=out, in_=in_, **kw)
        else:
            ins = fn(self.eng[q])
        self.dcnt[si] += 1
        ins.then_inc(self.dsem[si], 16)
        tok = ("d", si, 16 * self.dcnt[si])
        self._record(tok, reads, writes)
        self.nins += 1
        return tok

    def barrier(self):
        for e in self.eng:
            self.finish(e)

    def finish(self, e="sp"):
        for si in range(len(self.dsem)):
            if self.dcnt[si] > 0:
                self._wait(e, ("d", si, 16 * self.dcnt[si]))
        for f in self.CE:
            if self.cnt[f] > 0:
                self._wait(e, ("c", f, self.cnt[f]))


class Pool:
    def __init__(self, nc):
        import contextlib
        self.nc = nc
        self.st = contextlib.ExitStack()

    _ctr = [0]

    def sb(self, name, shape, dt=F32):
        Pool._ctr[0] += 1
        return self.st.enter_context(self.nc.sbuf_tensor("%s_%d" % (name, Pool._ctr[0]), list(shape), dt))

    def ps(self, name, shape, dt=F32):
        Pool._ctr[0] += 1
        return self.st.enter_context(self.nc.psum_tensor("%s_%d" % (name, Pool._ctr[0]), list(shape), dt))

    def close(self):
        self.st.close()


def rev_ap(ap):
    (ps, pn), (fs, fn) = ap.ap
    return bass.AP(ap.tensor, ap.offset + fs * (fn - 1), [[ps, pn], [-fs, fn]])


import numpy as np

S = 16384
DM = 1024
NCOL = 772
C_FA, C_LX, C_LG, C_Q, C_QR, C_K, C_KR, C_V, C_DQ, C_DK, C_DV, C_DG, C_BD = [64 * i for i in range(13)]
PV_LCW = 0
PV_LCB = 4
PV_BA = 5
PV_BX = 7
PV_LAM = 9
PV_FB = 11
PV_DCW = 12
PV_DCB = 24
PV_DNW = 27
PV_INV = 28
PV_SGN = 29
NPV = 32


def stage1(em, nc, P, xT, wh, hT, ps):
    wb = P.sb("s1_wb", [128, 8, NCOL], BF16)
    wtmp = [P.sb("s1_wtmp%d" % i, [128, NCOL], F32) for i in range(2)]
    x32 = [P.sb("s1_x32_%d" % i, [128, 8, 512], F32) for i in range(2)]
    xb = [P.sb("s1_xb_%d" % i, [128, 8, 512], BF16) for i in range(2)]
    ho = [P.sb("s1_ho_%d" % i, [128, 512], F32) for i in range(4)]
    whv = wh.rearrange("(kc p) c -> p kc c", p=128)
    for kc in range(8):
        b = kc % 2
        em.dma("sp", wtmp[b][:], whv[:, kc, :], writes=[("wtmp", b)])
        em.op("dve", lambda e: e.tensor_copy(out=wb[:, kc, :], in_=wtmp[b][:]), reads=[("wtmp", b)], writes=["wb"])
    xTv = xT.rearrange("(kc p) t -> p kc t", p=128)
    groups = [(g * 128, min(128, NCOL - g * 128)) for g in range((NCOL + 127) // 128)]
    pi = 0
    hi = 0
    for tt in range(S // 512):
        b = tt % 2
        em.dma("sp", x32[b][:], xTv[:, :, tt * 512:(tt + 1) * 512], writes=[("x32", b)])
        for kc in range(8):
            eng = ("dve", "pool", "act")[kc % 3] if False else ("dve" if kc % 2 == 0 else "pool")
            em.op(eng, lambda e: e.tensor_copy(out=xb[b][:, kc, :], in_=x32[b][:, kc, :]),
                  reads=[("x32", b)], writes=[("xb", b, kc)])
        for (c0, m) in groups:
            p = pi % 8
            pi += 1
            for kc in range(8):
                em.op("pe", lambda e: e.matmul(ps[p][0:m, :], lhsT=wb[:, kc, c0:c0 + m], rhs=xb[b][:, kc, :],
                                               start=(kc == 0), stop=(kc == 7)),
                      reads=["wb", ("xb", b, kc)], writes=[("ps", p)])
            hb = hi % 4
            hi += 1
            if hi % 2 == 0:
                em.op("act", lambda e: e.copy(out=ho[hb][0:m, :], in_=ps[p][0:m, :]), reads=[("ps", p)], writes=[("ho", hb)])
            else:
                em.op("dve", lambda e: e.tensor_copy(out=ho[hb][0:m, :], in_=ps[p][0:m, :]), reads=[("ps", p)], writes=[("ho", hb)])
            em.dma("sp", hT[c0:c0 + m, tt * 512:(tt + 1) * 512], ho[hb][0:m, :], reads=[("ho", hb)], writes=["hT"])


def mixer_lru(em, nc, P, hT, yT, pv, lw, ps, yr=64):
    CH = 2048
    NCH = S // CH
    lws = P.sb("lru_w", [64, 4, 64], F32)
    em.dma("sp", lws[:], lw, writes=["lru_w"])
    cneg = P.sb("lru_cneg", [64, 4], F32)
    em.op("act", lambda e: e.activation(out=cneg[:, 0:2], in_=pv[:, PV_LAM:PV_LAM + 2], func=AF.Exp, scale=-1.0),
          reads=["pv"], writes=["cneg"])
    em.op("act", lambda e: e.activation(out=cneg[:, 0:2], in_=cneg[:, 0:2], func=AF.Ln, bias=1.0, scale=1.0),
          reads=["cneg"], writes=["cneg"])
    em.op("dve", lambda e: e.tensor_scalar_mul(out=cneg[:, 2:4], in0=cneg[:, 0:2], scalar1=-16.0), reads=["cneg"], writes=["cneg"])
    em.op("dve", lambda e: e.tensor_scalar_mul(out=cneg[:, 0:2], in0=cneg[:, 0:2], scalar1=-8.0), reads=["cneg"], writes=["cneg"])
    hf = P.sb("lru_hf", [64, S], F32)
    xin = [P.sb("lru_xin%d" % i, [64, CH + 3], F32) for i in range(2)]
    gin = [P.sb("lru_gin%d" % i, [64, CH], F32) for i in range(2)]
    xc = P.sb("lru_xc", [64, CH], F32)
    rr = P.sb("lru_r", [64, CH], F32)
    ii = P.sb("lru_i", [64, CH], F32)
    aa = P.sb("lru_a", [64, CH], F32)
    uu = P.sb("lru_u", [64, CH], F32)
    hb = P.sb("lru_hb", [64, CH], F32)
    carry = P.sb("lru_carry", [64, 1], F32)
    li = 0
    for d in range(2):
        order = range(NCH) if d == 0 else range(NCH - 1, -1, -1)
        for ci, c in enumerate(order):
            t0 = c * CH
            b = li % 2
            li += 1
            lo = max(t0 - 2, 0)
            hi_ = min(t0 + CH + 1, S)
            if lo > t0 - 2:
                em.op("pool", lambda e: e.memset(xin[b][:, 0:2], 0.0), writes=[("xin", b)])
            if hi_ < t0 + CH + 1:
                em.op("pool", lambda e: e.memset(xin[b][:, CH + 2:CH + 3], 0.0), writes=[("xin", b)])
            em.dma("sp", xin[b][:, lo - (t0 - 2):hi_ - (t0 - 2)], hT[C_LX:C_LX + 64, lo:hi_], reads=["hT"], writes=[("xin", b)])
            if d == 1:
                em.dma("sp", gin[b][:], hT[C_LG:C_LG + 64, t0:t0 + CH], reads=["hT"], writes=[("gin", b)])
            em.op("dve", lambda e: e.tensor_scalar(out=xc[:], in0=xin[b][:, 0:CH], scalar1=pv[:, PV_LCW:PV_LCW + 1],
                                                   scalar2=pv[:, PV_LCB:PV_LCB + 1], op0=ALU.mult, op1=ALU.add),
                  reads=[("xin", b), "pv"], writes=["xc"])
            for j in range(1, 4):
                em.op("dve", lambda e: e.scalar_tensor_tensor(out=xc[:], in0=xin[b][:, j:j + CH], scalar=pv[:, PV_LCW + j:PV_LCW + j + 1],
                                                              in1=xc[:], op0=ALU.mult, op1=ALU.add),
                      reads=[("xin", b), "pv", "xc"], writes=["xc"])
            for q in range(CH // 512):
                sl = slice(q * 512, (q + 1) * 512)
                pa = (2 * q) % 8
                px = (2 * q + 1) % 8
                em.op("pe", lambda e: e.matmul(ps[pa][0:64, :], lhsT=lws[:, d, :], rhs=xc[:, sl], start=True, stop=True),
                      reads=["lru_w", "xc"], writes=[("ps", pa)])
                em.op("pe", lambda e: e.matmul(ps[px][0:64, :], lhsT=lws[:, 2 + d, :], rhs=xc[:, sl], start=True, stop=True),
                      reads=["lru_w", "xc"], writes=[("ps", px)])
                em.op("act", lambda e: e.activation(out=rr[:, sl], in_=ps[pa][0:64, :], func=AF.Sigmoid,
                                                    bias=pv[:, PV_BA + d:PV_BA + d + 1], scale=1.0),
                      reads=[("ps", pa), "pv"], writes=["rr"])
                em.op("act", lambda e: e.activation(out=ii[:, sl], in_=ps[px][0:64, :], func=AF.Sigmoid,
                                                    bias=pv[:, PV_BX + d:PV_BX + d + 1], scale=1.0),
                      reads=[("ps", px), "pv"], writes=["ii"])
            em.op("act", lambda e: e.activation(out=aa[:], in_=rr[:], func=AF.Exp, scale=cneg[:, d:d + 1]),
                  reads=["rr", "cneg"], writes=["aa"])
            em.op("act", lambda e: e.activation(out=rr[:], in_=rr[:], func=AF.Exp, scale=cneg[:, 2 + d:3 + d]),
                  reads=["rr", "cneg"], writes=["rr"])
            em.op("act", lambda e: e.activation(out=rr[:], in_=rr[:], func=AF.Sqrt, bias=1.0, scale=-1.0),
                  reads=["rr"], writes=["rr"])
            em.op("dve", lambda e: e.tensor_tensor(out=uu[:], in0=rr[:], in1=ii[:], op=ALU.mult), reads=["rr", "ii"], writes=["uu"])
            em.op("dve", lambda e: e.tensor_tensor(out=uu[:], in0=uu[:], in1=xc[:], op=ALU.mult), reads=["uu", "xc"], writes=["uu"])
            if d == 0:
                init = 0.0 if ci == 0 else hf[:, t0 - 1:t0]
                em.op("dve", lambda e: e.tensor_tensor_scan(out=hf[:, t0:t0 + CH], data0=aa[:], data1=uu[:], initial=init,
                                                            op0=ALU.mult, op1=ALU.add),
                      reads=["aa", "uu", "hf"], writes=["hf"])
            else:
                init = 0.0 if ci == 0 else carry[:, 0:1]
                em.op("dve", lambda e: e.tensor_tensor_scan(out=rev_ap(hb[:]), data0=rev_ap(aa[:]), data1=rev_ap(uu[:]), initial=init,
                                                            op0=ALU.mult, op1=ALU.add),
                      reads=["aa", "uu", "carry"], writes=["hb"])
                em.op("dve", lambda e: e.tensor_copy(out=carry[:], in_=hb[:, 0:1]), reads=["hb"], writes=["carry"])
                em.op("act", lambda e: e.activation(out=gin[b][:], in_=gin[b][:], func=AF.Gelu), reads=[("gin", b)], writes=[("gin", b)])
                em.op("pool", lambda e: e.tensor_tensor(out=hb[:], in0=hb[:], in1=hf[:, t0:t0 + CH], op=ALU.add), reads=["hb", "hf"], writes=["hb"])
                em.op("pool", lambda e: e.tensor_tensor(out=gin[b][:], in0=hb[:], in1=gin[b][:], op=ALU.mult), reads=["hb", ("gin", b)], writes=[("gin", b)])
                em.dma("sp", yT[yr:yr + 64, t0:t0 + CH], gin[b][:], reads=[("gin", b)], writes=["yT"])


def head_cols(h):
    cols = []
    seg = lambda k: list(range(k * 256 + h * 64, k * 256 + h * 64 + 64))
    q = seg(3)
    k = seg(4)
    cols += seg(0) + seg(1) + seg(2) + q + q[32:] + q[:32] + k + k[32:] + k[:32] + seg(5) + seg(6) + seg(7) + seg(8) + seg(9)
    cols += [2560 + h, 2560 + 4 + h, 2568 + h, 2568 + 4 + h]
    return np.array(cols)


def prep_A(inp, l, b, h, xT_b):
    f = lambda a: np.ascontiguousarray(a, dtype=np.float32)
    hs = slice(h * 64, h * 64 + 64)
    pv = np.zeros((64, NPV), np.float32)
    pv[:, PV_LCW:PV_LCW + 4] = inp['lru_conv_w'][l][:, 64 + 0 * 0 + h * 64 - 64 + 0:][:, 0:0].T if False else inp['lru_conv_w'][l][:, hs].T
    pv[:, PV_LCB] = inp['lru_conv_b'][l][hs]
    pv[:, PV_BA:PV_BA + 2] = inp['lru_ba'][l][:, hs].T
    pv[:, PV_BX:PV_BX + 2] = inp['lru_bx'][l][:, hs].T
    pv[:, PV_LAM:PV_LAM + 2] = inp['lru_lam'][l][:, hs].T
    pv[:, PV_FB] = inp['fno_b'][l][h]
    for j in range(3):
        pv[:, PV_DCW + 4 * j:PV_DCW + 4 * j + 4] = inp['dn_conv_w'][l][:, j * 256 + h * 64:j * 256 + h * 64 + 64].T
        pv[:, PV_DCB + j] = inp['dn_conv_b'][l][j * 256 + h * 64:j * 256 + h * 64 + 64]
    pv[:, PV_DNW] = inp['dn_norm_w'][l]
    lw = np.stack([inp['lru_wa'][l][0, h], inp['lru_wa'][l][1, h], inp['lru_wx'][l][0, h], inp['lru_wx'][l][1, h]], axis=1)
    fc, f64 = fourier_consts()
    ident, mask, sel = attn_consts()
    inv = np.power(np.float32(10000.0), -(np.arange(32, dtype=np.float32) / np.float32(32))).astype(np.float32)
    pv[:, PV_INV] = np.concatenate([inv, inv])
    pv[:, PV_INV + 2] = np.pi / 2
    pv[0:32, PV_SGN] = -1.0
    pv[32:64, PV_SGN] = 1.0
    return {"xT": xT_b, "wh": f(inp['w_in'][l][:, head_cols(h)]), "pv": pv, "lw": f(lw),
            "fc": fc, "f64": f64, "wf": f(inp['fno_w'][l][h]),
            "pos": np.ascontiguousarray(inp['positions'][b:b + 1]).astype(np.int32), "ident": ident, "amask": mask, "asel": sel, "dnc": dn_consts(),
            "dsc": np.tile(np.concatenate([inp['dn_dt_bias'][l][:, h], inp['dn_a_log'][l][:, h]])[None, :], (128, 1)).astype(np.float32)}


def build_A(debug=False, mixers="b"):
    import concourse.bass as bass
    nc = bass.Bass("TRN2", target_bir_lowering=False)
    xT = nc.dram_tensor("xT", [DM, S], F32, kind="ExternalInput").ap()
    wh = nc.dram_tensor("wh", [DM, NCOL], F32, kind="ExternalInput").ap()
    pvd = nc.dram_tensor("pv", [64, NPV], F32, kind="ExternalInput").ap()
    lw = nc.dram_tensor("lw", [64, 4, 64], F32, kind="ExternalInput").ap()
    fcd = nc.dram_tensor("fc", [128, 5, 128], F32, kind="ExternalInput").ap()
    f64d = nc.dram_tensor("f64", [64, 2, 64], F32, kind="ExternalInput").ap()
    wfd = nc.dram_tensor("wf", [64, 64], F32, kind="ExternalInput").ap()
    posd = nc.dram_tensor("pos", [1, S], I32, kind="ExternalInput").ap()
    identd = nc.dram_tensor("ident", [128, 128], F32, kind="ExternalInput").ap()
    maskd = nc.dram_tensor("amask", [128, 2, 128], F32, kind="ExternalInput").ap()
    seld = nc.dram_tensor("asel", [65, 64], F32, kind="ExternalInput").ap()
    dncd = nc.dram_tensor("dnc", [128, 5, 128], F32, kind="ExternalInput").ap()
    dscd = nc.dram_tensor("dsc", [128, 4], F32, kind="ExternalInput").ap()
    vaug = nc.dram_tensor("vaug", [S + 2 * VPAD, 65], BF16).ap()
    yT = nc.dram_tensor("yT", [256, S], F32, kind="ExternalOutput").ap()
    hT = nc.dram_tensor("hT", [NCOL, S], F32, kind="ExternalOutput" if debug else "Internal").ap()
    em = EM(nc)
    P0 = Pool(nc)
    ps = [P0.ps("ps%d" % i, [128, 512], F32) for i in range(8)]
    pv = P0.sb("pv_sb", [64, NPV], F32)
    em.dma("sp", pv[:], pvd, writes=["pv"])
    P = Pool(nc)
    stage1(em, nc, P, xT, wh, hT, ps)
    em.barrier()
    P.close()
    if "a" in mixers:
        mixer_fourier(em, nc, hT, yT, pv, fcd, f64d, wfd, ps)
    if "c" in mixers:
        mixer_attn(em, nc, hT, yT, pv, posd, identd, maskd, seld, vaug, ps)
    if "d" in mixers:
        mixer_dn(em, nc, hT, yT, pv, identd, dncd, dscd, ps)
    if "b" in mixers:
        P = Pool(nc)
        mixer_lru(em, nc, P, hT, yT, pv, lw, ps)
        em.barrier()
        P.close()
    em.finish("sp")
    print("A instructions:", em.nins)
    return nc


def fourier_consts():
    j = np.arange(128)
    th = 2 * np.pi * np.outer(j, j) / 128.0
    tw = 2 * np.pi * np.outer(j, j) / float(S)
    c = np.arange(64)
    t64 = 2 * np.pi * np.outer(c, c) / 64.0
    fc = np.zeros((128, 5, 128), np.float64)
    fc[:, 0] = np.cos(th)
    fc[:, 1] = np.sin(th)
    fc[:, 2] = -np.sin(th)
    fc[:, 3] = np.cos(tw)
    fc[:, 4] = np.sin(tw)
    f64 = np.zeros((64, 2, 64), np.float64)
    f64[:, 0] = np.cos(t64) / 8.0
    f64[:, 1] = -np.sin(t64) / 8.0
    return fc.astype(np.float32), f64.astype(np.float32)


def mixer_fourier(em, nc, hT, yT, pv, fcd, f64d, wfd, ps, yr=0):
    Pc = Pool(nc)
    fc = Pc.sb("f_fc", [128, 5, 128], F32)
    f64 = Pc.sb("f_f64", [64, 2, 64], F32)
    wf = Pc.sb("f_wf", [64, 64], F32)
    G = Pc.sb("f_G", [64, 128], F32)
    em.dma("sp", fc[:], fcd, writes=["fc"])
    em.dma("sp", f64[:], f64d, writes=["f64"])
    em.dma("sp", wf[:], wfd, writes=["wf"])
    for r in range(2):
        em.op("pe", lambda e: e.matmul(ps[0][0:64, r * 64:(r + 1) * 64], lhsT=f64[:, r, :], rhs=wf[:], start=True, stop=True),
              reads=["f64", "wf"], writes=[("ps", 0)])
    em.op("dve", lambda e: e.tensor_copy(out=G[:], in_=ps[0][0:64, 0:128]), reads=[("ps", 0)], writes=["G"])
    PZ = Pool(nc)
    Z = PZ.sb("f_Z", [128, 128, 128], F32)
    P1 = Pool(nc)
    ha = P1.sb("f_ha", [64, S], F32)
    for q in range(4):
        em.dma("sp", ha[:, q * 4096:(q + 1) * 4096], hT[C_FA:C_FA + 64, q * 4096:(q + 1) * 4096], reads=["hT"], writes=["ha"])
    hav = ha[:].rearrange("c (s1 s2) -> c s2 s1", s2=128)
    pi = 0
    for sb in range(32):
        p = pi % 8
        pi += 1
        pv4 = ps[p][:].rearrange("p (a n) -> p a n", n=128)
        for a in range(4):
            s2 = sb * 4 + a
            em.op("pe", lambda e: e.matmul(pv4[:, a, :], lhsT=hav[:, s2, :], rhs=G[:], start=True, stop=True),
                  reads=["ha", "G"], writes=[("ps", p)])
        eng = "dve" if sb % 2 == 0 else "act"
        if eng == "dve":
            em.op("dve", lambda e: e.tensor_copy(out=Z[:, sb * 4:sb * 4 + 4, :], in_=pv4), reads=[("ps", p)], writes=["Z"])
        else:
            em.op("act", lambda e: e.copy(out=Z[:, sb * 4:sb * 4 + 4, :], in_=pv4), reads=[("ps", p)], writes=["Z"])
    em.barrier()
    P1.close()
    PA = Pool(nc)
    A = PA.sb("f_A", [128, 64, 2, 128], F32)
    for db in range(32):
        p = pi % 8
        pi += 1
        pv4 = ps[p][:].rearrange("p (a n) -> p a n", n=128)
        for dd in range(2):
            d = db * 2 + dd
            zre = Z[:, :, d]
            zim = Z[:, :, 64 + d]
            em.op("pe", lambda e: e.matmul(pv4[:, dd * 2, :], lhsT=zre, rhs=fc[:, 0, :], start=True, stop=False), reads=["Z", "fc"], writes=[("ps", p)])
            em.op("pe", lambda e: e.matmul(pv4[:, dd * 2, :], lhsT=zim, rhs=fc[:, 1, :], start=False, stop=True), reads=["Z", "fc"], writes=[("ps", p)])
            em.op("pe", lambda e: e.matmul(pv4[:, dd * 2 + 1, :], lhsT=zim, rhs=fc[:, 0, :], start=True, stop=False), reads=["Z", "fc"], writes=[("ps", p)])
            em.op("pe", lambda e: e.matmul(pv4[:, dd * 2 + 1, :], lhsT=zre, rhs=fc[:, 2, :], start=False, stop=True), reads=["Z", "fc"], writes=[("ps", p)])
        outv = A[:, db * 2:db * 2 + 2, :, :].rearrange("p d r k -> p (d r) k")
        if db % 2 == 0:
            em.op("dve", lambda e: e.tensor_copy(out=outv, in_=pv4), reads=[("ps", p)], writes=["A"])
        else:
            em.op("act", lambda e: e.copy(out=outv, in_=pv4), reads=[("ps", p)], writes=["A"])
    em.barrier()
    PT = Pool(nc)
    t1 = PT.sb("f_t1", [128, 32, 128], F32)
    t2 = PT.sb("f_t2", [128, 32, 128], F32)
    tc_b = fc[:, 3:4, :].to_broadcast([128, 32, 128])
    ts_b = fc[:, 4:5, :].to_broadcast([128, 32, 128])
    for hh in range(2):
        are = A[:, hh * 32:(hh + 1) * 32, 0, :]
        aim = A[:, hh * 32:(hh + 1) * 32, 1, :]
        em.op("dve", lambda e: e.tensor_tensor(out=t1[:], in0=are, in1=ts_b, op=ALU.mult), reads=["A", "fc"], writes=["t1"])
        em.op("pool", lambda e: e.tensor_tensor(out=t2[:], in0=aim, in1=ts_b, op=ALU.mult), reads=["A", "fc"], writes=["t2"])
        em.op("dve", lambda e: e.tensor_tensor(out=are, in0=are, in1=tc_b, op=ALU.mult), reads=["A", "fc", "t1"], writes=[("A", "re")])
        em.op("pool", lambda e: e.tensor_tensor(out=aim, in0=aim, in1=tc_b, op=ALU.mult), reads=["A", "fc", "t2"], writes=[("A", "im")])
        em.op("dve", lambda e: e.tensor_tensor(out=are, in0=are, in1=t2[:], op=ALU.add), reads=[("A", "re"), "t2"], writes=[("A", "re"), "A"])
        em.op("pool", lambda e: e.tensor_tensor(out=aim, in0=aim, in1=t1[:], op=ALU.subtract), reads=[("A", "im"), "t1"], writes=[("A", "im"), "A"])
    em.barrier()
    PT.close()
    ya = Z[0:64, :, :].rearrange("p a b -> p (a b)")
    yav = ya.rearrange("d (k2 k1) -> d k1 k2", k1=128)
    for kb in range(32):
        p = pi % 8
        pi += 1
        pv4 = ps[p][0:64, :].rearrange("p (a n) -> p a n", n=128)
        for a in range(4):
            k1 = kb * 4 + a
            em.op("pe", lambda e: e.matmul(pv4[:, a, :], lhsT=A[:, :, 0, k1], rhs=fc[:, 0, :], start=True, stop=False), reads=["A", "fc"], writes=[("ps", p)])
            em.op("pe", lambda e: e.matmul(pv4[:, a, :], lhsT=A[:, :, 1, k1], rhs=fc[:, 1, :], start=False, stop=True), reads=["A", "fc"], writes=[("ps", p)])
        em.op("act", lambda e: e.activation(out=yav[:, kb * 4:kb * 4 + 4, :], in_=pv4, func=AF.Identity,
                                            bias=pv[:, PV_FB:PV_FB + 1], scale=1.0 / 128.0),
              reads=[("ps", p), "pv"], writes=["Z"])
    for q in range(4):
        em.dma("sp", yT[yr:yr + 64, q * 4096:(q + 1) * 4096], ya[:, q * 4096:(q + 1) * 4096], reads=["Z"], writes=["yT"])
    em.barrier()
    PA.close()
    PZ.close()
    Pc.close()


VPAD = 1024
TWO_PI = float(np.float32(2 * np.pi))


def attn_consts():
    ident = np.eye(128, dtype=np.float32)
    j = np.arange(128)[:, None]
    i = np.arange(128)[None, :]
    mask = np.zeros((128, 2, 128), np.float32)
    mask[:, 0, :] = (j >= i)
    mask[:, 1, :] = (j <= i)
    sel = np.zeros((65, 64), np.float32)
    sel[64, :] = 1.0
    return ident, mask, sel


def bc_ap(ap, nparts):
    import concourse.bass as bass
    a = ap.ap
    return bass.AP(ap.tensor, ap.offset, [[0, nparts], [a[-1][0], a[-1][1]]])


def mixer_attn(em, nc, hT, yT, pv, posd, identd, maskd, seld, vaug, ps, yr=128):
    Pc = Pool(nc)
    ident = Pc.sb("a_ident", [128, 128], F32)
    mask32 = Pc.sb("a_mask32", [128, 2, 128], F32)
    mask = Pc.sb("a_mask", [128, 2, 128], BF16)
    sel = Pc.sb("a_sel", [65, 64], F32)
    em.dma("sp", ident[:], identd, writes=["ident"])
    em.dma("sp", mask32[:], maskd, writes=["mask32"])
    em.dma("sp", sel[:], seld, writes=["sel"])
    em.op("dve", lambda e: e.tensor_copy(out=mask[:], in_=mask32[:]), reads=["mask32"], writes=["mask"])
    qb = Pc.sb("a_qb", [64, S], BF16)
    kb = Pc.sb("a_kb", [64, S + 2 * VPAD], BF16)
    acc = Pc.sb("a_acc", [65, S], F32)
    em.op("pool", lambda e: e.memset(kb[:, 0:VPAD], 0.0), writes=["kb"])
    em.op("pool", lambda e: e.memset(kb[:, VPAD + S:], 0.0), writes=["kb"])
    P0 = Pool(nc)
    zt = P0.sb("a_zt", [128, 8, 65], BF16)
    em.op("pool", lambda e: e.memset(zt[:], 0.0), writes=["zt"])
    em.dma("sp", vaug[0:VPAD, :].rearrange("(p a) c -> p a c", a=8), zt[:], reads=["zt"], writes=["vaug"])
    em.dma("sp", vaug[VPAD + S:, :].rearrange("(p a) c -> p a c", a=8), zt[:], reads=["zt"], writes=["vaug"])
    v32 = [P0.sb("a_v32_%d" % i, [64, 512], F32) for i in range(2)]
    vt = [P0.sb("a_vt_%d" % i, [128, 4, 65], BF16) for i in range(2)]
    for i in range(2):
        em.op("pool", lambda e: e.memset(vt[i][:], 1.0), writes=[("vt", i)])
    for tt in range(S // 512):
        b = tt % 2
        p = tt % 8
        em.dma("sp", v32[b][:], hT[C_V:C_V + 64, tt * 512:(tt + 1) * 512], reads=["hT"], writes=[("v32", b)])
        pv4 = ps[p][:, 0:256].rearrange("p (a c) -> p a c", c=64)
        for a in range(4):
            em.op("pe", lambda e: e.transpose(out=pv4[:, a, :], in_=v32[b][:, a * 128:(a + 1) * 128], identity=ident[0:64, 0:64]),
                  reads=[("v32", b), "ident"], writes=[("ps", p)])
        em.op("dve", lambda e: e.tensor_copy(out=vt[b][:, :, 0:64], in_=pv4), reads=[("ps", p)], writes=[("vt", b)])
        em.dma("sp", vaug[VPAD + tt * 512:VPAD + (tt + 1) * 512, :].rearrange("(a p) c -> p a c", p=128), vt[b][:],
               reads=[("vt", b)], writes=["vaug"])
    em.barrier()
    P0.close()
    P1 = Pool(nc)
    CH = 1024
    posi = P1.sb("a_posi", [64, CH], I32)
    ang = P1.sb("a_ang", [64, CH], F32)
    cs = P1.sb("a_cs", [64, CH], F32)
    sn = P1.sb("a_sn", [64, CH], F32)
    xin = [P1.sb("a_xin%d" % i, [64, CH], F32) for i in range(4)]
    t1 = P1.sb("a_t1", [64, CH], F32)
    t2 = P1.sb("a_t2", [64, CH], F32)
    for c in range(S // CH):
        t0 = c * CH
        em.dma("sp", posi[:], bc_ap(posd[0:1, t0:t0 + CH], 64), writes=["posi"])
        for i, row in enumerate((C_Q, C_QR, C_K, C_KR)):
            em.dma("sp", xin[i][:], hT[row:row + 64, t0:t0 + CH], reads=["hT"], writes=[("xin", i)])
        em.op("dve", lambda e: e.tensor_copy(out=ang[:], in_=posi[:]), reads=["posi"], writes=["ang"])
        em.op("dve", lambda e: e.tensor_scalar_mul(out=ang[:], in0=ang[:], scalar1=pv[:, PV_INV:PV_INV + 1]), reads=["ang", "pv"], writes=["ang"])
        em.op("dve", lambda e: e.tensor_scalar(out=cs[:], in0=ang[:], scalar1=float(1.0 / (2 * np.pi)), scalar2=12582912.0,
                                               op0=ALU.mult, op1=ALU.add), reads=["ang"], writes=["cs"])
        em.op("dve", lambda e: e.tensor_scalar_add(out=cs[:], in0=cs[:], scalar1=-12582912.0), reads=["cs"], writes=["cs"])
        em.op("dve", lambda e: e.scalar_tensor_tensor(out=ang[:], in0=cs[:], scalar=-6.28125, in1=ang[:], op0=ALU.mult, op1=ALU.add),
              reads=["cs", "ang"], writes=["ang"])
        em.op("dve", lambda e: e.scalar_tensor_tensor(out=ang[:], in0=cs[:], scalar=-0.0019353071795864769, in1=ang[:], op0=ALU.mult, op1=ALU.add),
              reads=["cs", "ang"], writes=["ang"])
        em.op("dve", lambda e: e.tensor_scalar(out=ang[:], in0=ang[:], scalar1=3.1415925, scalar2=-3.1415925, op0=ALU.min, op1=ALU.max),
              reads=["ang"], writes=["ang"])
        em.op("act", lambda e: e.activation(out=sn[:], in_=ang[:], func=AF.Sin), reads=["ang"], writes=["sn"])
        em.op("dve", lambda e: e.scalar_tensor_tensor(out=ang[:], in0=ang[:], scalar=-1.0, in1=ang[:], op0=ALU.mult, op1=ALU.max), reads=["ang", "sn"], writes=["ang"])
        em.op("act", lambda e: e.activation(out=cs[:], in_=ang[:], func=AF.Sin, bias=pv[:, PV_INV + 2:PV_INV + 3], scale=-1.0),
              reads=["ang", "pv"], writes=["cs"])
        em.op("dve", lambda e: e.tensor_scalar_mul(out=sn[:], in0=sn[:], scalar1=pv[:, PV_SGN:PV_SGN + 1]), reads=["sn", "pv"], writes=["sn"])
        for which, dst in ((0, qb[:, t0:t0 + CH]), (1, kb[:, VPAD + t0:VPAD + t0 + CH])):
            em.op("dve", lambda e: e.tensor_tensor(out=t1[:], in0=xin[2 * which][:], in1=cs[:], op=ALU.mult),
                  reads=[("xin", 2 * which), "cs"], writes=["t1"])
            em.op("pool", lambda e: e.tensor_tensor(out=t2[:], in0=xin[2 * which + 1][:], in1=sn[:], op=ALU.mult),
                  reads=[("xin", 2 * which + 1), "sn"], writes=["t2"])
            em.op("dve", lambda e: e.tensor_tensor(out=dst, in0=t1[:], in1=t2[:], op=ALU.add), reads=["t1", "t2"],
                  writes=["qb" if which == 0 else "kb"])
    em.barrier()
    P1.close()
    P2 = Pool(nc)
    vg = [P2.sb("a_vg%d" % i, [128, 5, 65], BF16) for i in range(3)]
    pT = [P2.sb("a_pT%d" % i, [128, 2, 128], BF16) for i in range(4)]
    import concourse.bass as bass
    gi = 0
    bi = 0
    for (win, d) in ((128, 1), (512, 4), (2048, 16)):
        L = S // d
        for r in range(d):
            for g in range(L // 512):
                vb = gi % 3
                po = 4 + gi % 4
                gi += 1
                row0 = VPAD + r + d * (128 * 4 * g - 64)
                src = bass.AP(vaug.tensor, vaug.offset + row0 * 65, [[d * 65, 128], [128 * d * 65, 5], [1, 65]])
                em.dma("sp", vg[vb][:], src, reads=["vaug"], writes=[("vg", vb)])
                for a in range(4):
                    n = 4 * g + a
                    psi = bi % 4
                    pb = bi % 4
                    bi += 1
                    sv = ps[psi][:, 0:256].rearrange("p (a c) -> p a c", c=128)
                    q0 = r + d * 128 * n
                    qv = qb[:, q0:q0 + d * 127 + 1:d]
                    for tl in range(2):
                        k0 = VPAD + r + d * (128 * (n + tl) - 64)
                        kv = kb[:, k0:k0 + d * 127 + 1:d]
                        em.op("pe", lambda e: e.matmul(sv[:, tl, :], lhsT=kv, rhs=qv, start=True, stop=True),
                              reads=["qb", "kb"], writes=[("ps", psi)])
                    em.op("act", lambda e: e.activation(out=pT[pb][:], in_=sv, func=AF.Exp, scale=0.125),
                          reads=[("ps", psi)], writes=[("pT", pb)])
                    em.op("pool", lambda e: e.tensor_tensor(out=pT[pb][:], in0=pT[pb][:], in1=mask[:], op=ALU.mult),
                          reads=[("pT", pb), "mask"], writes=[("pT", pb)])
                    for tl in range(2):
                        em.op("pe", lambda e: e.matmul(ps[po][0:65, a * 128:(a + 1) * 128], lhsT=vg[vb][:, a + tl, :], rhs=pT[pb][:, tl, :],
                                                       start=(tl == 0), stop=(tl == 1)),
                              reads=[("vg", vb), ("pT", pb)], writes=[("ps", po)])
                a0 = r + d * 512 * g
                av = acc[:, a0:a0 + d * 511 + 1:d]
                if d == 1:
                    em.op("dve", lambda e: e.tensor_copy(out=av, in_=ps[po][0:65, :]), reads=[("ps", po)], writes=["acc"])
                else:
                    em.op("dve", lambda e: e.tensor_tensor(out=av, in0=av, in1=ps[po][0:65, :], op=ALU.add),
                          reads=[("ps", po), "acc"], writes=["acc"])
    em.barrier()
    P2.close()
    P3 = Pool(nc)
    rec = [P3.sb("a_rec%d" % i, [64, 512], F32) for i in range(2)]
    yo = [P3.sb("a_yo%d" % i, [64, 512], F32) for i in range(2)]
    for tt in range(S // 512):
        b = tt % 2
        p = tt % 8
        sl = slice(tt * 512, (tt + 1) * 512)
        em.op("pe", lambda e: e.matmul(ps[p][0:64, :], lhsT=sel[:], rhs=acc[:, sl], start=True, stop=True),
              reads=["sel", "acc"], writes=[("ps", p)])
        em.op("dve", lambda e: e.reciprocal(out=rec[b][:], in_=ps[p][0:64, :]), reads=[("ps", p)], writes=[("rec", b)])
        em.op("pool", lambda e: e.tensor_tensor(out=yo[b][:], in0=acc[0:64, sl], in1=rec[b][:], op=ALU.mult),
              reads=["acc", ("rec", b)], writes=[("yo", b)])
        em.dma("sp", yT[yr:yr + 64, sl], yo[b][:], reads=[("yo", b)], writes=["yT"])
    em.barrier()
    P3.close()
    Pc.close()


def dn_consts():
    p = np.arange(128)[:, None]
    f = np.arange(128)[None, :]
    c = np.zeros((128, 5, 128), np.float32)
    c[:, 0] = 1.0
    c[:, 1] = (f >= p)
    c[:, 2] = (f > p)
    c[:, 3] = (f <= p)
    c[:, 4] = (f < p)
    return c


def mixer_dn(em, nc, hT, yT, pv, identd, dncd, dscd, ps, yr=192):
    C = 128
    NCH = S // C
    Pc = Pool(nc)
    ident = Pc.sb("d_ident", [128, 128], F32)
    dnc = Pc.sb("d_dnc", [128, 5, 128], F32)
    dsc = Pc.sb("d_dsc", [128, 4], F32)
    em.dma("sp", ident[:], identd, writes=["d_ident"])
    em.dma("sp", dnc[:], dncd, writes=["dnc"])
    em.dma("sp", dsc[:], dscd, writes=["dsc"])
    ones = dnc[:, 0, :]
    Qn = Pc.sb("d_Qn", [128, NCH, 64], F32)
    Kn = Pc.sb("d_Kn", [128, NCH, 64], F32)
    Vt = Pc.sb("d_Vt", [128, NCH, 64], F32)
    Os = Pc.sb("d_Os", [128, NCH, 64], F32)
    bd = Pc.sb("d_bd", [128, NCH, 4], F32)
    beta = Pc.sb("d_beta", [128, 2, NCH], F32)
    gg = Pc.sb("d_g", [128, 2, NCH], F32)
    gam = Pc.sb("d_gam", [128, 2, NCH], F32)
    egam = Pc.sb("d_egam", [128, 2, NCH], F32)
    etg = Pc.sb("d_etg", [128, 2, NCH], F32)
    etot = Pc.sb("d_etot", [128, 2, NCH], F32)
    bsc = Pc.sb("d_bsc", [128, 2, NCH], F32)
    nA = Pc.sb("d_nA", [128, 2], F32)
    em.op("act", lambda e: e.activation(out=nA[:], in_=dsc[:, 2:4], func=AF.Exp), reads=["dsc"], writes=["nA"])
    em.op("dve", lambda e: e.tensor_scalar_mul(out=nA[:], in0=nA[:], scalar1=-1.0), reads=["nA"], writes=["nA"])
    P1 = Pool(nc)
    CH = 2048
    xin = [P1.sb("d_xin%d" % i, [64, CH + 3], F32) for i in range(3)]
    xc = [P1.sb("d_xc%d" % i, [64, CH], F32) for i in range(3)]
    bdin = P1.sb("d_bdin", [4, CH], F32)
    sq = P1.sb("d_sq", [128, 16, 64], F32)
    ss = P1.sb("d_ss", [128, 2, 16], F32)
    dst = (Qn, Kn, Vt)
    for c in range(S // CH):
        t0 = c * CH
        lo = max(t0 - 2, 0)
        hi_ = min(t0 + CH + 1, S)
        for j, row in enumerate((C_DQ, C_DK, C_DV)):
            if lo > t0 - 2:
                em.op("pool", lambda e: e.memset(xin[j][:, 0:2], 0.0), writes=[("dxin", j)])
            if hi_ < t0 + CH + 1:
                em.op("pool", lambda e: e.memset(xin[j][:, CH + 2:CH + 3], 0.0), writes=[("dxin", j)])
            em.dma("sp", xin[j][:, lo - (t0 - 2):hi_ - (t0 - 2)], hT[row:row + 64, lo:hi_], reads=["hT"], writes=[("dxin", j)])
            eng = "dve"
            em.op(eng, lambda e: e.tensor_scalar(out=xc[j][:], in0=xin[j][:, 0:CH], scalar1=pv[:, PV_DCW + 4 * j:PV_DCW + 4 * j + 1],
                                                 scalar2=pv[:, PV_DCB + j:PV_DCB + j + 1], op0=ALU.mult, op1=ALU.add),
                  reads=[("dxin", j), "pv"], writes=[("dxc", j)])
            for t in range(1, 4):
                em.op(eng, lambda e: e.scalar_tensor_tensor(out=xc[j][:], in0=xin[j][:, t:t + CH],
                                                            scalar=pv[:, PV_DCW + 4 * j + t:PV_DCW + 4 * j + t + 1],
                                                            in1=xc[j][:], op0=ALU.mult, op1=ALU.add),
                      reads=[("dxin", j), "pv", ("dxc", j)], writes=[("dxc", j)])
            em.op("act", lambda e: e.activation(out=xc[j][:], in_=xc[j][:], func=AF.Silu), reads=[("dxc", j)], writes=[("dxc", j)])
        em.dma("sp", bdin[:], hT[C_BD:C_BD + 4, t0:t0 + CH], reads=["hT"], writes=["bdin"])
        for j in range(3):
            for half in range(2):
                p = (2 * j + half) % 6
                pv8 = ps[p][:].rearrange("p (a c) -> p a c", c=64)
                for a in range(8):
                    blk = half * 8 + a
                    em.op("pe", lambda e: e.transpose(out=pv8[:, a, :], in_=xc[j][:, blk * 128:(blk + 1) * 128], identity=ident[0:64, 0:64]),
                          reads=[("dxc", j), "d_ident"], writes=[("ps", p)])
                o_ = dst[j][:, c * 16 + half * 8:c * 16 + half * 8 + 8, :]
                if half == 0:
                    em.op("dve", lambda e: e.tensor_copy(out=o_, in_=pv8), reads=[("ps", p)], writes=[("dst", j)])
                else:
                    em.op("act", lambda e: e.copy(out=o_, in_=pv8), reads=[("ps", p)], writes=[("dst", j)])
        pb = ps[6][:, 0:64].rearrange("p (a c) -> p a c", c=4)
        for a in range(16):
            em.op("pe", lambda e: e.transpose(out=pb[:, a, :], in_=bdin[:, a * 128:(a + 1) * 128], identity=ident[0:4, 0:4]),
                  reads=["bdin", "d_ident"], writes=[("ps", 6)])
        em.op("dve", lambda e: e.tensor_copy(out=bd[:, c * 16:(c + 1) * 16, :], in_=pb), reads=[("ps", 6)], writes=["bd"])
        for j in range(2):
            blkv = dst[j][:, c * 16:(c + 1) * 16, :]
            em.op("pool", lambda e: e.tensor_tensor(out=sq[:], in0=blkv, in1=blkv, op=ALU.mult), reads=[("dst", j)], writes=["dsq"])
            em.op("dve", lambda e: e.reduce_sum(out=ss[:, j, :], in_=sq[:], axis=AX.X), reads=["dsq"], writes=["dss"])
            em.op("act", lambda e: e.activation(out=ss[:, j, :], in_=ss[:, j, :], func=AF.Sqrt, bias=1e-6, scale=1.0), reads=["dss"], writes=["dss"])
            em.op("dve", lambda e: e.reciprocal(out=ss[:, j, :], in_=ss[:, j, :]), reads=["dss"], writes=["dss"])
            if j == 0:
                em.op("dve", lambda e: e.tensor_scalar_mul(out=ss[:, j, :], in0=ss[:, j, :], scalar1=0.125), reads=["dss"], writes=["dss"])
            em.op("dve", lambda e: e.tensor_tensor(out=blkv, in0=blkv, in1=ss[:, j, :].unsqueeze(2).to_broadcast([128, 16, 64]), op=ALU.mult),
                  reads=[("dst", j), "dss"], writes=[("dst", j)])
    em.barrier()
    P1.close()
    STOP = 99
    if STOP <= 1:
        Pc.close()
        return
    _sub = [0]
    SUBLIM = 999
    def _ok():
        _sub[0] += 1
        return _sub[0] <= SUBLIM
    for dr in range(2):
        _ok() and em.op("act", lambda e: e.activation(out=beta[:, dr, :], in_=bd[:, :, dr], func=AF.Sigmoid), reads=["bd"], writes=["beta"])
        _ok() and em.op("act", lambda e: e.activation(out=gg[:, dr, :], in_=bd[:, :, 2 + dr], func=AF.Exp, bias=dsc[:, dr:dr + 1], scale=1.0),
              reads=["bd", "dsc"], writes=["gg"])
        _ok() and em.op("act", lambda e: e.activation(out=gg[:, dr, :], in_=gg[:, dr, :], func=AF.Ln, bias=1.0, scale=1.0), reads=["gg"], writes=["gg"])
        _ok() and em.op("dve", lambda e: e.tensor_scalar_mul(out=gg[:, dr, :], in0=gg[:, dr, :], scalar1=nA[:, dr:dr + 1]), reads=["gg", "nA"], writes=["gg"])
        tri = dnc[:, 1, :] if dr == 0 else dnc[:, 3, :]
        _ok() and em.op("pe", lambda e: e.matmul(ps[0][:, 0:NCH], lhsT=tri, rhs=gg[:, dr, :], start=True, stop=True), reads=["dnc", "gg"], writes=[("ps", 0)])
        _ok() and em.op("pe", lambda e: e.matmul(ps[0][:, NCH:2 * NCH], lhsT=ones, rhs=gg[:, dr, :], start=True, stop=True), reads=["dnc", "gg"], writes=[("ps", 0)])
        _ok() and em.op("dve", lambda e: e.tensor_copy(out=gam[:, dr, :], in_=ps[0][:, 0:NCH]), reads=[("ps", 0)], writes=["gam"])
        _ok() and em.op("act", lambda e: e.activation(out=egam[:, dr, :], in_=ps[0][:, 0:NCH], func=AF.Exp), reads=[("ps", 0)], writes=["egam"])
        _ok() and em.op("act", lambda e: e.activation(out=etot[:, dr, :], in_=ps[0][:, NCH:2 * NCH], func=AF.Exp), reads=[("ps", 0)], writes=["etot"])
        _ok() and em.op("dve", lambda e: e.tensor_tensor(out=etg[:, dr, :], in0=ps[0][:, NCH:2 * NCH], in1=gam[:, dr, :], op=ALU.subtract),
              reads=[("ps", 0), "gam"], writes=["etg"])
        _ok() and em.op("act", lambda e: e.activation(out=etg[:, dr, :], in_=etg[:, dr, :], func=AF.Exp), reads=["etg"], writes=["etg"])
        _ok() and em.op("dve", lambda e: e.tensor_tensor(out=bsc[:, dr, :], in0=beta[:, dr, :], in1=egam[:, dr, :], op=ALU.mult),
              reads=["beta", "egam"], writes=["bsc"])
    if STOP <= 2:
        em.barrier()
        Pc.close()
        return
    P2 = Pool(nc)
    KI = 3
    T = lambda n, shape=(128, 128): [P2.sb("d_%s%d" % (n, i), list(shape), F32) for i in range(KI)]
    dgb = T("dgb", (128, 256))
    Dm = T("Dm")
    LTi = T("LTi")
    LTs = T("LTs")
    KQT = T("KQT", (64, 256))
    AT = T("AT")
    nk = [T("nk%d" % k) for k in range(7)]
    nkT = [T("nkT%d" % k) for k in range(7)]
    Y = [T("Y%d" % k) for k in range(2)]
    X = T("X")
    UT = T("UT", (64, 128))
    Qd = T("Qd", (128, 64))
    QdT = T("QdT", (64, 128))
    Kd = T("Kd", (128, 64))
    Vn = T("Vn", (128, 64))
    St = P2.sb("d_St", [64, 64], F32)

    def prep(dr, ch, b):
        K = lambda n: (n, b)
        gam_c = gam[:, dr, ch:ch + 1]
        beta_c = beta[:, dr, ch:ch + 1]
        mi = dnc[:, 1, :] if dr == 0 else dnc[:, 3, :]
        ms = dnc[:, 2, :] if dr == 0 else dnc[:, 4, :]
        pR = pK = 2 * b
        pA = pB = pT_ = 2 * b + 1
        em.op("dve", lambda e: e.tensor_scalar_mul(out=dgb[b][:, 0:128], in0=ident[:], scalar1=gam_c), reads=["d_ident", "gam"], writes=[K("dgb")])
        yield
        em.op("pool", lambda e: e.tensor_scalar_mul(out=dgb[b][:, 128:256], in0=ident[:], scalar1=beta_c), reads=["d_ident", "beta"], writes=[K("dgb")])
        yield
        em.op("pe", lambda e: e.matmul(ps[pR][:, 0:256], lhsT=ones, rhs=dgb[b][:], start=True, stop=True), reads=["dnc", K("dgb")], writes=[("ps", pR)])
        yield
        em.op("dve", lambda e: e.tensor_scalar(out=Dm[b][:], in0=ps[pR][:, 0:128], scalar1=gam_c, scalar2=0.0, op0=ALU.subtract, op1=ALU.min),
              reads=[("ps", pR), "gam"], writes=[K("Dm")])
        yield
        em.op("act", lambda e: e.activation(out=Dm[b][:], in_=Dm[b][:], func=AF.Exp), reads=[K("Dm")], writes=[K("Dm")])
        yield
        em.op("pool", lambda e: e.tensor_tensor(out=LTi[b][:], in0=Dm[b][:], in1=mi, op=ALU.mult), reads=[K("Dm"), "dnc"], writes=[K("LTi")])
        yield
        em.op("pool", lambda e: e.tensor_tensor(out=LTs[b][:], in0=Dm[b][:], in1=ms, op=ALU.mult), reads=[K("Dm"), "dnc"], writes=[K("LTs")])
        yield
        em.op("pe", lambda e: e.transpose(out=ps[pT_][0:64, 0:128], in_=Kn[:, ch, :], identity=ident[:]), reads=[("dst", 1), "d_ident"], writes=[("ps", pT_)])
        yield
        em.op("pe", lambda e: e.transpose(out=ps[pT_][0:64, 128:256], in_=Qn[:, ch, :], identity=ident[:]), reads=[("dst", 0), "d_ident"], writes=[("ps", pT_)])
        yield
        em.op("act", lambda e: e.copy(out=KQT[b][:], in_=ps[pT_][0:64, 0:256]), reads=[("ps", pT_)], writes=[K("KQT")])
        yield
        em.op("pe", lambda e: e.matmul(ps[pK][:, 256:384], lhsT=KQT[b][:, 0:128], rhs=KQT[b][:, 0:128], start=True, stop=True), reads=[K("KQT")], writes=[("ps", pK)])
        yield
        em.op("pe", lambda e: e.matmul(ps[pK][:, 384:512], lhsT=KQT[b][:, 0:128], rhs=KQT[b][:, 128:256], start=True, stop=True), reads=[K("KQT")], writes=[("ps", pK)])
        yield
        em.op("dve", lambda e: e.tensor_tensor(out=AT[b][:], in0=ps[pK][:, 384:512], in1=LTi[b][:], op=ALU.mult), reads=[("ps", pK), K("LTi")], writes=[K("AT")])
        yield
        em.op("dve", lambda e: e.tensor_tensor(out=nkT[0][b][:], in0=ps[pK][:, 256:384], in1=LTs[b][:], op=ALU.mult), reads=[("ps", pK), K("LTs")], writes=[K("nkT0")])
        yield
        em.op("dve", lambda e: e.tensor_tensor(out=nkT[0][b][:], in0=nkT[0][b][:], in1=ps[pR][:, 128:256], op=ALU.mult), reads=[("ps", pR), K("nkT0")], writes=[K("nkT0")])
        yield
        em.op("pe", lambda e: e.transpose(out=ps[pT_][:, 384:512], in_=nkT[0][b][:], identity=ident[:]), reads=[K("nkT0"), "d_ident"], writes=[("ps", pT_)])
        yield
        em.op("act", lambda e: e.copy(out=nk[0][b][:], in_=ps[pT_][:, 384:512]), reads=[("ps", pT_)], writes=[K("nk0")])
        yield
        for k in range(1, 7):
            em.op("pe", lambda e: e.matmul(ps[pA][:, 0:128], lhsT=nkT[k - 1][b][:], rhs=nk[k - 1][b][:], start=True, stop=True),
                  reads=[K("nkT%d" % (k - 1)), K("nk%d" % (k - 1))], writes=[("ps", pA)])
            yield
            em.op("pe", lambda e: e.matmul(ps[pB][:, 128:256], lhsT=nk[k - 1][b][:], rhs=nkT[k - 1][b][:], start=True, stop=True),
                  reads=[K("nkT%d" % (k - 1)), K("nk%d" % (k - 1))], writes=[("ps", pB)])
            yield
            if k < 6:
                em.op("dve", lambda e: e.tensor_copy(out=nk[k][b][:], in_=ps[pA][:, 0:128]), reads=[("ps", pA)], writes=[K("nk%d" % k)])
                yield
            em.op("act", lambda e: e.copy(out=nkT[k][b][:], in_=ps[pB][:, 128:256]), reads=[("ps", pB)], writes=[K("nkT%d" % k)])
            yield
        yb = 0
        em.op("pool", lambda e: e.tensor_scalar_mul(out=Y[yb][b][:, 0:64], in0=Vt[:, ch, :], scalar1=beta_c), reads=[("dst", 2), "beta"], writes=[K("Y%d" % yb)])
        yield
        em.op("pool", lambda e: e.tensor_scalar_mul(out=Y[yb][b][:, 64:128], in0=Kn[:, ch, :], scalar1=bsc[:, dr, ch:ch + 1]), reads=[("dst", 1), "bsc"], writes=[K("Y%d" % yb)])
        yield
        for k in range(6, -1, -1):
            pp = pA if k % 2 == 0 else pB
            em.op("pe", lambda e: e.matmul(ps[pp][:, 256:384], lhsT=nkT[k][b][:], rhs=Y[yb][b][:], start=True, stop=True),
                  reads=[K("nkT%d" % k), K("Y%d" % yb)], writes=[("ps", pp)])
            yield
            if k > 0:
                em.op("dve", lambda e: e.tensor_tensor(out=Y[1 - yb][b][:], in0=Y[yb][b][:], in1=ps[pp][:, 256:384], op=ALU.add),
                      reads=[("ps", pp), K("Y%d" % yb)], writes=[K("Y%d" % (1 - yb))])
                yield
                yb = 1 - yb
            else:
                em.op("dve", lambda e: e.tensor_tensor(out=X[b][:], in0=Y[yb][b][:], in1=ps[pp][:, 256:384], op=ALU.subtract),
                      reads=[("ps", pp), K("Y%d" % yb)], writes=[K("X")])
                yield
        em.op("pe", lambda e: e.transpose(out=ps[pT_][0:64, 384:512], in_=X[b][:, 64:128], identity=ident[:]), reads=[K("X"), "d_ident"], writes=[("ps", pT_)])
        yield
        em.op("act", lambda e: e.copy(out=UT[b][:], in_=ps[pT_][0:64, 384:512]), reads=[("ps", pT_)], writes=[K("UT")])
        yield
        em.op("pool", lambda e: e.tensor_scalar_mul(out=Qd[b][:], in0=Qn[:, ch, :], scalar1=egam[:, dr, ch:ch + 1]), reads=[("dst", 0), "egam"], writes=[K("Qd")])
        yield
        em.op("pe", lambda e: e.transpose(out=ps[pT_][0:64, 0:128], in_=Qd[b][:], identity=ident[:]), reads=[K("Qd"), "d_ident"], writes=[("ps", pT_)])
        yield
        em.op("act", lambda e: e.copy(out=QdT[b][:], in_=ps[pT_][0:64, 0:128]), reads=[("ps", pT_)], writes=[K("QdT")])
        yield
        em.op("pool", lambda e: e.tensor_scalar_mul(out=Kd[b][:], in0=Kn[:, ch, :], scalar1=etg[:, dr, ch:ch + 1]), reads=[("dst", 1), "etg"], writes=[K("Kd")])
        yield

    def seq(dr, ch, b, first):
        K = lambda n: (n, b)
        p7 = 7
        if first:
            em.op("pool", lambda e: e.memset(St[:], 0.0), writes=["St"])
        em.op("pe", lambda e: e.matmul(ps[p7][:, 0:64], lhsT=UT[b][:], rhs=St[:], start=True, stop=True), reads=[K("UT"), "St"], writes=[("ps", 7)])
        em.op("dve", lambda e: e.tensor_tensor(out=Vn[b][:], in0=X[b][:, 0:64], in1=ps[p7][:, 0:64], op=ALU.subtract),
              reads=[("ps", 7), K("X")], writes=[K("Vn")])
        em.op("pe", lambda e: e.matmul(ps[p7][:, 64:128], lhsT=QdT[b][:], rhs=St[:], start=True, stop=False), reads=[K("QdT"), "St"], writes=[("ps", 7)])
        em.op("pe", lambda e: e.matmul(ps[p7][:, 64:128], lhsT=AT[b][:], rhs=Vn[b][:], start=False, stop=True), reads=[K("AT"), K("Vn")], writes=[("ps", 7)])
        em.op("pe", lambda e: e.matmul(ps[p7][0:64, 128:192], lhsT=Kd[b][:], rhs=Vn[b][:], start=True, stop=True), reads=[K("Kd"), K("Vn")], writes=[("ps", 7)])
        em.op("dve", lambda e: e.scalar_tensor_tensor(out=St[:], in0=St[:], scalar=etot[0:64, dr, ch:ch + 1], in1=ps[p7][0:64, 128:192],
                                                      op0=ALU.mult, op1=ALU.add), reads=[("ps", 7), "St", "etot"], writes=["St"])
        if dr == 0:
            em.op("act", lambda e: e.copy(out=Os[:, ch, :], in_=ps[p7][:, 64:128]), reads=[("ps", 7)], writes=["Os"])
        else:
            em.op("dve", lambda e: e.tensor_tensor(out=Os[:, ch, :], in0=Os[:, ch, :], in1=ps[p7][:, 64:128], op=ALU.add), reads=[("ps", 7), "Os"], writes=["Os"])

    def run_il(gens):
        gens = list(gens)
        while gens:
            for g_ in list(gens):
                try:
                    next(g_)
                except StopIteration:
                    gens.remove(g_)

    for dr in range(2):
        order = list(range(NCH)) if dr == 0 else list(range(NCH - 1, -1, -1))
        for n0 in range(0, NCH, KI):
            grp = order[n0:n0 + KI]
            run_il([prep(dr, ch, s_) for s_, ch in enumerate(grp)])
            for s_, ch in enumerate(grp):
                seq(dr, ch, s_, n0 + s_ == 0)
    em.barrier()
    P2.close()
    P3 = Pool(nc)
    ssn = P3.sb("d_ssn", [128, NCH], F32)
    sg = [P3.sb("d_sg%d" % i, [64, 512], F32) for i in range(2)]
    yo = [P3.sb("d_yo%d" % i, [64, 512], F32) for i in range(2)]
    em.op("pool", lambda e: e.tensor_tensor(out=Vt[:], in0=Os[:], in1=Os[:], op=ALU.mult), reads=["Os"], writes=[("dst", 2)])
    em.op("dve", lambda e: e.reduce_sum(out=ssn[:], in_=Vt[:], axis=AX.X), reads=[("dst", 2)], writes=["ssn"])
    em.op("act", lambda e: e.activation(out=ssn[:], in_=ssn[:], func=AF.Sqrt, bias=1e-6, scale=1.0 / 64.0), reads=["ssn"], writes=["ssn"])
    em.op("dve", lambda e: e.reciprocal(out=ssn[:], in_=ssn[:]), reads=["ssn"], writes=["ssn"])
    em.op("dve", lambda e: e.tensor_tensor(out=Os[:], in0=Os[:], in1=ssn[:].unsqueeze(2).to_broadcast([128, NCH, 64]), op=ALU.mult),
          reads=["Os", "ssn"], writes=["Os"])
    for tt in range(S // 512):
        b = tt % 2
        p = tt % 6
        sl = slice(tt * 512, (tt + 1) * 512)
        em.dma("sp", sg[b][:], hT[C_DG:C_DG + 64, sl], reads=["hT"], writes=[("sg", b)])
        em.op("act", lambda e: e.activation(out=sg[b][:], in_=sg[b][:], func=AF.Silu), reads=[("sg", b)], writes=[("sg", b)])
        for a in range(4):
            em.op("pe", lambda e: e.transpose(out=ps[p][0:64, a * 128:(a + 1) * 128], in_=Os[:, tt * 4 + a, :], identity=ident[:]),
                  reads=["Os", "d_ident"], writes=[("ps", p)])
        em.op("dve", lambda e: e.scalar_tensor_tensor(out=yo[b][:], in0=ps[p][0:64, :], scalar=pv[:, PV_DNW:PV_DNW + 1], in1=sg[b][:],
                                                      op0=ALU.mult, op1=ALU.mult), reads=[("ps", p), "pv", ("sg", b)], writes=[("yo", b)])
        em.dma("sp", yT[yr:yr + 64, sl], yo[b][:], reads=[("yo", b)], writes=["yT"])
    em.barrier()
    P3.close()
    Pc.close()


import numpy as np

DM = 1024
TPC = 4096
ALPHA = float((2.0 * 4) ** 0.25)
LN_EPS = 1e-5


def layer_norm_tile(em, nc, z, gB, bB, tmp, st, out, tagz):
    em.op("dve", lambda e: e.reduce_sum(out=st[:, 0:1], in_=z[:], axis=AX.X), reads=[tagz], writes=["ln_st"])
    em.op("dve", lambda e: e.tensor_scalar_mul(out=st[:, 0:1], in0=st[:, 0:1], scalar1=-1.0 / DM), reads=["ln_st"], writes=["ln_st"])
    em.op("act", lambda e: e.activation(out=z[:], in_=z[:], func=AF.Identity, bias=st[:, 0:1], scale=1.0), reads=[tagz, "ln_st"], writes=[tagz])
    em.op("pool", lambda e: e.tensor_tensor(out=tmp[:], in0=z[:], in1=z[:], op=ALU.mult), reads=[tagz], writes=["ln_tmp"])
    em.op("dve", lambda e: e.reduce_sum(out=st[:, 1:2], in_=tmp[:], axis=AX.X), reads=["ln_tmp"], writes=["ln_st"])
    em.op("act", lambda e: e.activation(out=st[:, 1:2], in_=st[:, 1:2], func=AF.Sqrt, bias=LN_EPS, scale=1.0 / DM), reads=["ln_st"], writes=["ln_st"])
    em.op("dve", lambda e: e.reciprocal(out=st[:, 1:2], in_=st[:, 1:2]), reads=["ln_st"], writes=["ln_st"])
    em.op("dve", lambda e: e.scalar_tensor_tensor(out=out[:], in0=z[:], scalar=st[:, 1:2], in1=gB[:], op0=ALU.mult, op1=ALU.mult),
          reads=[tagz, "ln_st", "gB"], writes=[tagz])
    em.op("pool", lambda e: e.tensor_tensor(out=out[:], in0=out[:], in1=bB[:], op=ALU.add), reads=[tagz, "bB"], writes=[tagz])


def bc_rows(ap, nparts):
    import concourse.bass as bass
    a = ap.ap
    return bass.AP(ap.tensor, ap.offset, [[0, nparts], [a[-1][0], a[-1][1]]])


def body_B(em, nc, yTs, xs, wo, lng, lnb, rw, identd, x1o, affo, ps, ntok=TPC, affT=None, moe0=None):
    P = Pool(nc)
    wob = P.sb("b_wob", [128, 8, DM], BF16)
    wtmp = [P.sb("b_wtmp%d" % i, [128, DM], F32) for i in range(2)]
    rws = P.sb("b_rw", [128, 8, 16], F32)
    gB = P.sb("b_gB", [128, DM], F32)
    bB = P.sb("b_bB", [128, DM], F32)
    ident = P.sb("b_ident", [128, 128], F32)
    em.dma("sp", ident[:], identd, writes=["b_ident"])
    em.dma("sp", gB[:], bc_rows(lng, 128), writes=["gB"])
    em.dma("sp", bB[:], bc_rows(lnb, 128), writes=["bB"])
    em.dma("sp", rws[:], rw.rearrange("(kc p) e -> p kc e", p=128), writes=["rws"])
    wov = wo.rearrange("(kc p) c -> p kc c", p=128)
    for kc in range(8):
        b = kc % 2
        em.dma("sp", wtmp[b][:], wov[:, kc, :], writes=[("bwtmp", b)])
        em.op("dve", lambda e: e.tensor_copy(out=wob[:, kc, :], in_=wtmp[b][:]), reads=[("bwtmp", b)], writes=["wob"])
    y32 = [P.sb("b_y32_%d" % i, [128, 8, 512], F32) for i in range(2)]
    yb = [P.sb("b_yb_%d" % i, [128, 8, 512], BF16) for i in range(2)]
    xt = [P.sb("b_xt%d" % i, [128, DM], F32) for i in range(2)]
    z = [P.sb("b_z%d" % i, [128, DM], F32) for i in range(2)]
    tmp = P.sb("b_tmp", [128, DM], F32)
    st = P.sb("b_st", [128, 4], F32)
    x1T = P.sb("b_x1T", [128, 8, 128], F32)
    lg = P.sb("b_lg", [128, 16], F32)
    sm = P.sb("b_sm", [128, 4], F32)
    yTv = yTs.rearrange("(kc p) t -> p kc t", p=128)
    pi = 0
    afT = P.sb("b_afT", [16, 128], F32)
    for tt in range(ntok // 512):
        b = tt % 2
        em.dma("sp", y32[b][:], yTv[:, :, tt * 512:(tt + 1) * 512], writes=[("y32", b)])
        for kc in range(8):
            eng = "dve" if kc % 2 == 0 else "pool"
            em.op(eng, lambda e: e.tensor_copy(out=yb[b][:, kc, :], in_=y32[b][:, kc, :]), reads=[("y32", b)], writes=[("yb", b, kc)])
        for a in range(4):
            ti = tt * 4 + a
            zb = ti % 2
            r0 = ti * 128
            em.dma("sp", xt[zb][:], xs[r0:r0 + 128, :], writes=[("xt", zb)])
            for half in range(2):
                p = pi % 4
                pi += 1
                for kc in range(8):
                    em.op("pe", lambda e: e.matmul(ps[p][:, :], lhsT=yb[b][:, kc, a * 128:(a + 1) * 128], rhs=wob[:, kc, half * 512:(half + 1) * 512],
                                                   start=(kc == 0), stop=(kc == 7)), reads=[("yb", b, kc), "wob"], writes=[("ps", p)])
                em.op("dve", lambda e: e.scalar_tensor_tensor(out=z[zb][:, half * 512:(half + 1) * 512], in0=xt[zb][:, half * 512:(half + 1) * 512],
                                                              scalar=ALPHA, in1=ps[p][:, :], op0=ALU.mult, op1=ALU.add),
                      reads=[("xt", zb), ("ps", p)], writes=[("z", zb)])
            layer_norm_tile(em, nc, z[zb], gB, bB, tmp, st, z[zb], ("z", zb))
            em.dma("sp", x1o[r0:r0 + 128, :], z[zb][:], reads=[("z", zb)], writes=["x1o"])
            if moe0 is not None:
                em.op("act", lambda e: e.mul(out=tmp[:], in_=z[zb][:], mul=ALPHA), reads=[("z", zb)], writes=["ln_tmp"])
                em.dma("sp", moe0[r0:r0 + 128, :], tmp[:], reads=["ln_tmp"], writes=["moe0"])
            for kc in range(8):
                pt = 4 + (kc // 4)
                em.op("pe", lambda e: e.transpose(out=ps[pt][:, (kc % 4) * 128:(kc % 4 + 1) * 128], in_=z[zb][:, kc * 128:(kc + 1) * 128], identity=ident[:]),
                      reads=[("z", zb), "b_ident"], writes=[("ps", pt)])
            em.op("act", lambda e: e.copy(out=x1T[:, 0:4, :], in_=ps[4][:, :].rearrange("p (a c) -> p a c", c=128)), reads=[("ps", 4)], writes=["x1T"])
            em.op("dve", lambda e: e.tensor_copy(out=x1T[:, 4:8, :], in_=ps[5][:, :].rearrange("p (a c) -> p a c", c=128)), reads=[("ps", 5)], writes=["x1T"])
            for kc in range(8):
                em.op("pe", lambda e: e.matmul(ps[6][:, 0:16], lhsT=x1T[:, kc, :], rhs=rws[:, kc, :], start=(kc == 0), stop=(kc == 7)),
                      reads=["x1T", "rws"], writes=[("ps", 6)])
            em.op("dve", lambda e: e.reduce_max(out=sm[:, 0:1], in_=ps[6][:, 0:16], axis=AX.X), reads=[("ps", 6)], writes=["sm"])
            em.op("dve", lambda e: e.tensor_scalar_mul(out=sm[:, 0:1], in0=sm[:, 0:1], scalar1=-1.0), reads=["sm"], writes=["sm"])
            em.op("act", lambda e: e.activation(out=lg[:], in_=ps[6][:, 0:16], func=AF.Exp, bias=sm[:, 0:1], scale=1.0), reads=[("ps", 6), "sm"], writes=["lg"])
            em.op("dve", lambda e: e.reduce_sum(out=sm[:, 1:2], in_=lg[:], axis=AX.X), reads=["lg"], writes=["sm"])
            em.op("dve", lambda e: e.reciprocal(out=sm[:, 1:2], in_=sm[:, 1:2]), reads=["sm"], writes=["sm"])
            em.op("dve", lambda e: e.tensor_scalar_mul(out=lg[:], in0=lg[:], scalar1=sm[:, 1:2]), reads=["lg", "sm"], writes=["lg"])
            if affo is not None:
                em.dma("sp", affo[r0:r0 + 128, :], lg[:], reads=["lg"], writes=["affo"])
            if affT is not None:
                em.op("pe", lambda e: e.transpose(out=ps[7][0:16, 0:128], in_=lg[:], identity=ident[:]), reads=["lg", "b_ident"], writes=[("ps", 7)])
                em.op("act", lambda e: e.copy(out=afT[:], in_=ps[7][0:16, 0:128]), reads=[("ps", 7)], writes=["afT"])
                em.dma("sp", affT[:, r0:r0 + 128], afT[:], reads=["afT"], writes=["affT"])
    em.barrier()
    P.close()


def build_B():
    import concourse.bass as bass
    nc = bass.Bass("TRN2", target_bir_lowering=False)
    yTs = nc.dram_tensor("yTs", [DM, TPC], F32, kind="ExternalInput").ap()
    xs = nc.dram_tensor("xs", [TPC, DM], F32, kind="ExternalInput").ap()
    wo = nc.dram_tensor("wo", [DM, DM], F32, kind="ExternalInput").ap()
    lng = nc.dram_tensor("lng", [1, DM], F32, kind="ExternalInput").ap()
    lnb = nc.dram_tensor("lnb", [1, DM], F32, kind="ExternalInput").ap()
    rw = nc.dram_tensor("rw", [DM, 16], F32, kind="ExternalInput").ap()
    identd = nc.dram_tensor("ident", [128, 128], F32, kind="ExternalInput").ap()
    x1o = nc.dram_tensor("x1o", [TPC, DM], F32, kind="ExternalOutput").ap()
    affo = nc.dram_tensor("affo", [TPC, 16], F32, kind="ExternalOutput").ap()
    em = EM(nc)
    P0 = Pool(nc)
    ps = [P0.ps("ps%d" % i, [128, 512], F32) for i in range(8)]
    body_B(em, nc, yTs, xs, wo, lng, lnb, rw, identd, x1o, affo, ps)
    em.finish("sp")
    print("B instructions:", em.nins)
    return nc


import numpy as np

DM = 1024
DFF = 2688
NFC = DFF // 128
SEQ = 16384
CAP = 2048
DMA_W = DM + 24
NITER = 30


def moe_consts():
    p = np.arange(128)[:, None]
    f = np.arange(128)[None, :]
    c = np.zeros((128, 4, 128), np.float32)
    c[:, 3] = p * 128 + f
    c[:, 0] = 1.0
    c[:, 1] = (p < f)
    c[:, 2] = np.eye(128)
    return c


def body_C(em, nc, x1, affs, w1, w3, w2, mcd, yg, idxo, xg, gated, ps, NB=2, NE=2, moe=None):
    import concourse.bass as bass
    NG = NE * NB
    Pc = Pool(nc)
    mc = Pc.sb("c_mc", [128, 4, 128], F32)
    em.dma("sp", mc[:], mcd, writes=["mc"])
    ones = mc[:, 0, :]
    lstr = mc[:, 1, :]
    ident = mc[:, 2, :]
    aff = Pc.sb("c_aff", [128, NG, 128], F32)
    em.dma("sp", aff[:], affs.rearrange("g (p j) -> p g j", j=128), writes=["aff"])
    idx_i = Pc.sb("c_idx", [128, NG, 128], I32)
    P1 = Pool(nc)
    cmp_ = P1.sb("c_cmp", [128, NG, 128], F32)
    lo = P1.sb("c_lo", [128, NG], F32)
    mid = P1.sb("c_mid", [128, NG], F32)
    cnt = P1.sb("c_cnt", [128, NG], F32)
    fl = P1.sb("c_fl", [128, NG], F32)
    em.op("dve", lambda e: e.memset(lo[:], 0.0), writes=["lo"])
    w = 0.5
    for it in range(NITER):
        em.op("dve", lambda e: e.tensor_scalar_add(out=mid[:], in0=lo[:], scalar1=float(w)), reads=["lo"], writes=["mid"])
        em.op("dve", lambda e: e.tensor_tensor(out=cmp_[:], in0=aff[:], in1=mid[:].unsqueeze(2).to_broadcast([128, NG, 128]), op=ALU.is_ge),
              reads=["aff", "mid"], writes=["cmp"])
        em.op("dve", lambda e: e.reduce_sum(out=cnt[:], in_=cmp_[:], axis=AX.X), reads=["cmp"], writes=["cnt"])
        em.op("pe", lambda e: e.matmul(ps[0][:, 0:NG], lhsT=ones, rhs=cnt[:], start=True, stop=True), reads=["mc", "cnt"], writes=[("ps", 0)])
        em.op("dve", lambda e: e.tensor_single_scalar(out=fl[:], in_=ps[0][:, 0:NG], scalar=float(CAP) - 0.5, op=ALU.is_ge), reads=[("ps", 0)], writes=["fl"])
        em.op("dve", lambda e: e.scalar_tensor_tensor(out=lo[:], in0=fl[:], scalar=float(w), in1=lo[:], op0=ALU.mult, op1=ALU.add),
              reads=["fl", "lo"], writes=["lo"])
        w *= 0.5
    cs = P1.sb("c_cs", [128, NG, 128], F32)
    onesj = P1.sb("c_onesj", [128, 128], F32)
    off = P1.sb("c_off", [128, NG], F32)
    em.op("pool", lambda e: e.memset(onesj[:], 1.0), writes=["onesj"])
    em.op("dve", lambda e: e.tensor_tensor(out=cmp_[:], in0=aff[:], in1=lo[:].unsqueeze(2).to_broadcast([128, NG, 128]), op=ALU.is_ge),
          reads=["aff", "lo"], writes=["cmp"])
    for g in range(NG):
        em.op("dve", lambda e: e.tensor_tensor_scan(out=cs[:, g, :], data0=onesj[:], data1=cmp_[:, g, :], initial=0.0, op0=ALU.mult, op1=ALU.add),
              reads=["cmp", "onesj"], writes=["cs"])
    em.op("dve", lambda e: e.tensor_copy(out=cnt[:], in_=cs[:, :, 127]), reads=["cs"], writes=["cnt"])
    em.op("pe", lambda e: e.matmul(ps[0][:, 0:NG], lhsT=lstr, rhs=cnt[:], start=True, stop=True), reads=["mc", "cnt"], writes=[("ps", 0)])
    em.op("dve", lambda e: e.tensor_scalar_add(out=off[:], in0=ps[0][:, 0:NG], scalar1=-1.0), reads=[("ps", 0)], writes=["off"])
    em.op("dve", lambda e: e.tensor_tensor(out=cs[:], in0=cs[:], in1=off[:].unsqueeze(2).to_broadcast([128, NG, 128]), op=ALU.add),
          reads=["cs", "off"], writes=["cs"])
    em.op("dve", lambda e: e.tensor_scalar(out=cmp_[:], in0=cmp_[:], scalar1=-100000.0, scalar2=100000.0, op0=ALU.mult, op1=ALU.add),
          reads=["cmp"], writes=["cmp"])
    em.op("dve", lambda e: e.tensor_tensor(out=cs[:], in0=cs[:], in1=cmp_[:], op=ALU.add), reads=["cs", "cmp"], writes=["cs"])
    em.op("dve", lambda e: e.tensor_copy(out=idx_i[:], in_=cs[:]), reads=["cs"], writes=["idx"])
    if idxo is not None:
        em.dma("sp", idxo.rearrange("g (p j) -> p g j", j=128), idx_i[:], reads=["idx"], writes=["idxo"])
    em.barrier()
    P1.close()
    P2 = Pool(nc)
    xt = [P2.sb("c_xt%d" % i, [128, DMA_W], F32) for i in range(3)]
    bcreg = nc.gpsimd.to_reg(CAP - 1)
    for b in range(NB):
        for j in range(128):
            xb_ = j % 3
            src = bass.AP(x1.tensor, x1.offset + (b * SEQ + j) * DM, [[128 * DM, 128], [1, DM]])
            em.dma("sp", xt[xb_][:, 0:DM], src, writes=[("cxt", xb_)])
            em.op("pool", lambda e: e.tensor_copy(out=xt[xb_][:, DM:DM + NG], in_=aff[:, :, j]), reads=["aff"], writes=[("cxt", xb_)])
            em.op("pool", lambda e: e.tensor_copy(out=xt[xb_][:, DM + 16:DM + 17], in_=mc[:, 3, j:j + 1]), reads=["mc"], writes=[("cxt", xb_)])
            for i in range(NE):
                g = i * NB + b
                em.dma("pool", None, None, reads=[("cxt", xb_), "idx"], writes=["xg"],
                       fn=lambda e: e.indirect_dma_start(out=xg[g], out_offset=bass.IndirectOffsetOnAxis(ap=idx_i[:, g, j:j + 1], axis=0),
                                                         in_=xt[xb_][:], in_offset=None, bounds_check=bcreg, oob_is_err=False))
    em.barrier()
    P2.close()
    P3 = Pool(nc)
    w1b = P3.sb("c_w1b", [128, 8, DFF], BF16)
    w3b = P3.sb("c_w3b", [128, 8, DFF], BF16)
    w2b = P3.sb("c_w2b", [128, NFC, DM], BF16)
    stg = [P3.sb("c_stg%d" % i, [128, DFF], F32) for i in range(2)]
    xr = [P3.sb("c_xr%d" % i, [128, DMA_W], F32) for i in range(2)]
    xgT = [P3.sb("c_xgT%d" % i, [128, 8, 256], BF16) for i in range(2)]
    gt = [P3.sb("c_gt%d" % i, [128, 2], F32) for i in range(2)]
    tk = [P3.sb("c_tk%d" % i, [128, 2], I32) for i in range(2)]
    bcm = nc.gpsimd.to_reg(SEQ - 1)
    prev_toks = []
    cur_toks = []
    sT = [P3.sb("c_sT%d" % i, [128, 256], F32) for i in range(2)]
    gT = [P3.sb("c_gT%d" % i, [128, 256], BF16) for i in range(3)]
    yo = [P3.sb("c_yo%d" % i, [128, DM], F32) for i in range(2)]
    si = 0
    ceng = ("dve", "pool", "act")

    def cast(eng, out, in_, reads, writes):
        if eng == "act":
            em.op("act", lambda e: e.copy(out=out, in_=in_), reads=reads, writes=writes)
        else:
            em.op(eng, lambda e: e.tensor_copy(out=out, in_=in_), reads=reads, writes=writes)

    xi = 0
    hi = 0
    gi = 0
    for i in range(NE):
        prev_toks = cur_toks
        cur_toks = []
        for (wsrc, wdst, tag) in ((w1, w1b, "w1b"), (w3, w3b, "w3b")):
            wv = wsrc[i].rearrange("(kc p) f -> p kc f", p=128)
            for kc in range(8):
                s = si % 2
                si += 1
                em.dma("sp", stg[s][:], wv[:, kc, :], writes=[("stg", s)])
                cast(ceng[si % 3], wdst[:, kc, :], stg[s][:], [("stg", s)], [tag])
        w2v = w2[i].rearrange("(fc p) d -> p fc d", p=128)
        for f2 in range(0, NFC, 2):
            n = min(2, NFC - f2)
            s = si % 2
            si += 1
            sv = stg[s][:, 0:n * DM].rearrange("p (a d) -> p a d", d=DM)
            em.dma("sp", sv, w2v[:, f2:f2 + n, :], writes=[("stg", s)])
            cast(ceng[si % 3], w2b[:, f2:f2 + n, :], sv, [("stg", s)], ["w2b"])
        for b in range(NB):
            g = i * NB + b
            for sg in range(CAP // 256):
                xb_ = sg % 2
                for st in range(2):
                    r = xi % 2
                    xi += 1
                    s0 = sg * 256 + st * 128
                    em.dma("sp", xr[r][:], xg[g][s0:s0 + 128, :], reads=["xg"], writes=[("xr", r)])
                    em.op("pool", lambda e: e.tensor_copy(out=gt[xb_][:, st:st + 1], in_=xr[r][:, DM + g:DM + g + 1]), reads=[("xr", r)], writes=[("gt", xb_)])
                    em.op("pool", lambda e: e.tensor_copy(out=tk[xb_][:, st:st + 1], in_=xr[r][:, DM + 16:DM + 17]), reads=[("xr", r)], writes=[("tk", xb_)])
                    for hf in range(2):
                        p = 2 + hf
                        for a in range(4):
                            kc = hf * 4 + a
                            em.op("pe", lambda e: e.transpose(out=ps[p][:, a * 128:(a + 1) * 128], in_=xr[r][:, kc * 128:(kc + 1) * 128], identity=ident),
                                  reads=[("xr", r), "mc"], writes=[("ps", p)])
                        o_ = xgT[xb_][:, hf * 4:hf * 4 + 4, st * 128:(st + 1) * 128]
                        i_ = ps[p][:, :].rearrange("p (a c) -> p a c", c=128)
                        if hf == 0:
                            em.op("dve", lambda e: e.tensor_copy(out=o_, in_=i_), reads=[("ps", p)], writes=[("xgT", xb_)])
                        else:
                            em.op("act", lambda e: e.copy(out=o_, in_=i_), reads=[("ps", p)], writes=[("xgT", xb_)])
                for fc in range(NFC):
                    ph = hi % 2
                    hi += 1
                    for (wb, tag, c0) in ((w1b, "w1b", 0), (w3b, "w3b", 256)):
                        for kc in range(8):
                            em.op("pe", lambda e: e.matmul(ps[ph][:, c0:c0 + 256], lhsT=wb[:, kc, fc * 128:(fc + 1) * 128], rhs=xgT[xb_][:, kc, :],
                                                           start=(kc == 0), stop=(kc == 7)), reads=[tag, ("xgT", xb_)], writes=[("ps", ph)])
                    sb_ = hi % 2
                    gb = gi % 3
                    gi += 1
                    em.op("act", lambda e: e.activation(out=sT[sb_][:], in_=ps[ph][:, 0:256], func=AF.Silu), reads=[("ps", ph)], writes=[("sT", sb_)])
                    em.op("dve", lambda e: e.tensor_tensor(out=gT[gb][:], in0=sT[sb_][:], in1=ps[ph][:, 256:512], op=ALU.mult),
                          reads=[("sT", sb_), ("ps", ph)], writes=[("gT", gb)])
                    for st in range(2):
                        for half in range(2):
                            py = 4 + st * 2 + half
                            em.op("pe", lambda e: e.matmul(ps[py][:, :], lhsT=gT[gb][:, st * 128:(st + 1) * 128], rhs=w2b[:, fc, half * 512:(half + 1) * 512],
                                                           start=(fc == 0), stop=(fc == NFC - 1)), reads=[("gT", gb), "w2b"], writes=[("ps", py)])
                for st in range(2):
                    for half in range(2):
                        py = 4 + st * 2 + half
                        o_ = yo[st][:, half * 512:(half + 1) * 512]
                        if half == 0:
                            em.op("act", lambda e: e.activation(out=o_, in_=ps[py][:, :], func=AF.Copy, scale=gt[xb_][:, st:st + 1]),
                                  reads=[("ps", py), ("gt", xb_)], writes=[("yo", st)])
                        else:
                            em.op("dve", lambda e: e.tensor_scalar_mul(out=o_, in0=ps[py][:, :], scalar1=gt[xb_][:, st:st + 1]),
                                  reads=[("ps", py), ("gt", xb_)], writes=[("yo", st)])
                    s0 = sg * 256 + st * 128
                    if moe is None:
                        em.dma("sp", yg[g][s0:s0 + 128, :], yo[st][:], reads=[("yo", st)], writes=["yg"])
                    else:
                        for t_ in prev_toks:
                            em._wait("pool", t_)
                        prev_toks = []
                        tok = em.dma("pool", None, None, reads=[("yo", st), ("tk", xb_)], writes=[],
                                     fn=lambda e: e.indirect_dma_start(out=moe, out_offset=bass.IndirectOffsetOnAxis(ap=tk[xb_][:, st:st + 1], axis=0),
                                                                       in_=yo[st][:], in_offset=None, bounds_check=bcm, oob_is_err=False,
                                                                       compute_op=ALU.add))
                        cur_toks.append(tok)
    em.barrier()
    P3.close()
    return Pc, idx_i, mc


def build_C(NB=2, NE=2, ntok=None):
    import concourse.bass as bass
    nc = bass.Bass("TRN2", target_bir_lowering=False)
    NG = NB * NE
    x1 = nc.dram_tensor("x1", [NB * SEQ, DM], F32, kind="ExternalInput").ap()
    affs = nc.dram_tensor("affs", [NG, SEQ], F32, kind="ExternalInput").ap()
    w1 = nc.dram_tensor("w1", [NE, DM, DFF], F32, kind="ExternalInput").ap()
    w3 = nc.dram_tensor("w3", [NE, DM, DFF], F32, kind="ExternalInput").ap()
    w2 = nc.dram_tensor("w2", [NE, DFF, DM], F32, kind="ExternalInput").ap()
    mcd = nc.dram_tensor("mc", [128, 4, 128], F32, kind="ExternalInput").ap()
    yg = nc.dram_tensor("yg", [NG, CAP, DM], F32, kind="ExternalOutput").ap()
    idxo = nc.dram_tensor("idxo", [NG, SEQ], I32, kind="ExternalOutput").ap()
    xg = [nc.dram_tensor("xg%d" % g, [CAP, DMA_W], F32).ap() for g in range(NG)]
    gated = nc.dram_tensor("gated", [NG, CAP, 1], F32).ap()
    em = EM(nc)
    P0 = Pool(nc)
    ps = [P0.ps("ps%d" % i, [128, 512], F32) for i in range(8)]
    Pc, _, _ = body_C(em, nc, x1, affs, w1, w3, w2, mcd, yg, idxo, xg, gated, ps, NB=NB, NE=NE)
    Pc.close()
    em.finish("sp")
    print("C instructions:", em.nins)
    return nc


def body_D(em, nc, idx_i, mc, ygs, x1, lng, lnb, x2o, x2To, ps, NG=16, alpha=1.0, moe=None):
    import concourse.bass as bass
    ident = mc[:, 2, :]
    P = Pool(nc)
    if moe is None:
        idxf = P.sb("d2_idxf", [128, NG, 128], F32)
        idxTf = P.sb("d2_idxTf", [128, NG, 128], F32)
        selT = P.sb("d2_selT", [128, NG, 128], F32)
        idxTi = P.sb("d2_idxTi", [128, NG, 128], I32)
    gB = P.sb("d2_gB", [128, DM], F32)
    bB = P.sb("d2_bB", [128, DM], F32)
    em.dma("sp", gB[:], bc_rows(lng, 128), writes=["gB"])
    em.dma("sp", bB[:], bc_rows(lnb, 128), writes=["bB"])
    if moe is None:
        em.op("dve", lambda e: e.tensor_copy(out=idxf[:], in_=idx_i[:]), reads=["idx"], writes=["idxf"])
        for g4 in range(NG // 4):
            p = g4 % 4
            for a_ in range(4):
                g = g4 * 4 + a_
                em.op("pe", lambda e: e.transpose(out=ps[p][:, a_ * 128:(a_ + 1) * 128], in_=idxf[:, g, :], identity=ident), reads=["idxf", "mc"], writes=[("ps", p)])
            em.op("dve", lambda e: e.tensor_copy(out=idxTf[:, g4 * 4:g4 * 4 + 4, :], in_=ps[p][:, :].rearrange("p (a c) -> p a c", c=128)),
                  reads=[("ps", p)], writes=["idxTf"])
        em.op("dve", lambda e: e.tensor_single_scalar(out=selT[:], in_=idxTf[:], scalar=float(CAP) - 0.5, op=ALU.is_lt), reads=["idxTf"], writes=["selT"])
        em.op("dve", lambda e: e.tensor_copy(out=idxTi[:], in_=idxTf[:]), reads=["idxTf"], writes=["idxTi"])
        NBUF = 6
        buf = [P.sb("d2_buf%d" % i, [128, DM], F32) for i in range(NBUF)]
        for i in range(NBUF):
            em.op("pool", lambda e: e.memset(buf[i][:], 0.0), writes=[("d2buf", i)])
    z = [P.sb("d2_z%d" % i, [128, DM], F32) for i in range(2)]
    tmp = P.sb("d2_tmp", [128, DM], F32)
    st = P.sb("d2_st", [128, 4], F32)
    xTs = [P.sb("d2_xTs%d" % i, [128, 8, 128], F32) for i in range(2)]
    bcreg = nc.gpsimd.to_reg(CAP - 1)
    bi = 0
    for n in range(SEQ // 128):
        zb = n % 2
        r0 = n * 128
        if moe is not None:
            em.dma("sp", z[zb][:], moe[r0:r0 + 128, :], reads=["moe"], writes=[("d2z", zb)])
        else:
            em.dma("sp", z[zb][:], x1[r0:r0 + 128, :], reads=["x1"], writes=[("d2z", zb)])
            em.op("act", lambda e: e.mul(out=z[zb][:], in_=z[zb][:], mul=float(alpha)), reads=[("d2z", zb)], writes=[("d2z", zb)])
        for g in range(NG if moe is None else 0):
            b_ = bi % NBUF
            bi += 1
            em.dma("pool", None, None, reads=["idxTi", "yg"], writes=[("d2buf", b_)],
                   fn=lambda e: e.indirect_dma_start(out=buf[b_][:], out_offset=None, in_=ygs[g],
                                                     in_offset=bass.IndirectOffsetOnAxis(ap=idxTi[:, g, n:n + 1], axis=0),
                                                     bounds_check=bcreg, oob_is_err=False))
            em.op("dve", lambda e: e.scalar_tensor_tensor(out=z[zb][:], in0=buf[b_][:], scalar=selT[:, g, n:n + 1], in1=z[zb][:],
                                                          op0=ALU.mult, op1=ALU.add), reads=[("d2buf", b_), "selT", ("d2z", zb)], writes=[("d2z", zb)])
        layer_norm_tile(em, nc, z[zb], gB, bB, tmp, st, z[zb], ("d2z", zb))
        em.dma("sp", x2o[r0:r0 + 128, :], z[zb][:], reads=[("d2z", zb)], writes=["x2o"])
        if x2To is not None:
            for kc in range(8):
                pt = 4 + (kc // 4)
                em.op("pe", lambda e: e.transpose(out=ps[pt][:, (kc % 4) * 128:(kc % 4 + 1) * 128], in_=z[zb][:, kc * 128:(kc + 1) * 128], identity=ident),
                      reads=[("d2z", zb), "mc"], writes=[("ps", pt)])
            em.op("act", lambda e: e.copy(out=xTs[zb][:, 0:4, :], in_=ps[4][:, :].rearrange("p (a c) -> p a c", c=128)), reads=[("ps", 4)], writes=[("xTs", zb)])
            em.op("dve", lambda e: e.tensor_copy(out=xTs[zb][:, 4:8, :], in_=ps[5][:, :].rearrange("p (a c) -> p a c", c=128)), reads=[("ps", 5)], writes=[("xTs", zb)])
            em.dma("sp", x2To.rearrange("(kc p) t -> p kc t", p=128)[:, :, r0:r0 + 128], xTs[zb][:], reads=[("xTs", zb)], writes=["x2To"])
    em.barrier()
    P.close()


import numpy as np

NL = 4


def build_full(nlayers=NL, nheads=4, nexp=16, dbg=False):
    import concourse.bass as bass
    nc = bass.Bass("TRN2", target_bir_lowering=False)
    I = lambda name, shape, dt=F32: nc.dram_tensor(name, list(shape), dt, kind="ExternalInput").ap()
    xT = I("xT", [DM, S])
    xtok = I("xtok", [S, DM])
    posd = I("pos", [1, S], I32)
    whall = I("whall", [NL, 4, DM, NCOL])
    pvall = I("pvall", [NL, 4, 64, NPV])
    lwall = I("lwall", [NL, 4, 64, 4, 64])
    wfall = I("wfall", [NL, 4, 64, 64])
    dscall = I("dscall", [NL, 4, 128, 4])
    fcd = I("fc", [128, 5, 128])
    f64d = I("f64", [64, 2, 64])
    identd = I("ident", [128, 128])
    maskd = I("amask", [128, 2, 128])
    seld = I("asel", [65, 64])
    dncd = I("dnc", [128, 5, 128])
    mcd = I("mc", [128, 4, 128])
    wo = I("wo", [NL, DM, DM])
    ln1g = I("ln1g", [NL, 1, DM])
    ln1b = I("ln1b", [NL, 1, DM])
    ln2g = I("ln2g", [NL, 1, DM])
    ln2b = I("ln2b", [NL, 1, DM])
    rw = I("rw", [NL, DM, 16])
    w1 = I("w1", [NL, 16, DM, DFF])
    w3 = I("w3", [NL, 16, DM, DFF])
    w2 = I("w2", [NL, 16, DFF, DM])
    out = nc.dram_tensor("out", [S, DM], F32, kind="ExternalOutput").ap()
    T = lambda name, shape, dt=F32: nc.dram_tensor(name, list(shape), dt).ap()
    hT = T("hT", [NCOL, S])
    vaug = T("vaug", [S + 2 * VPAD, 65], BF16)
    yT = T("yT", [DM, S])
    x1 = T("x1", [S, DM])
    affT = T("affT", [16, S])
    moe = T("moe", [S, DM])
    x2 = T("x2", [S, DM])
    x2T = T("x2T", [DM, S])
    xgs = [T("xg%d" % g, [CAP, DMA_W]) for g in range(16)]
    ygs = [T("yg%d" % g, [CAP, DM]) for g in range(16)]
    em = EM(nc)
    P0 = Pool(nc)
    ps = [P0.ps("ps%d" % i, [128, 512], F32) for i in range(8)]
    pv = P0.sb("pv_sb", [64, NPV], F32)
    for l in range(nlayers):
        xT_l = xT if l == 0 else x2T
        xtok_l = xtok if l == 0 else x2
        last = (l == nlayers - 1)
        for h in range(nheads):
            em.dma("sp", pv[:], pvall[l, h], writes=["pv"])
            P = Pool(nc)
            stage1(em, nc, P, xT_l, whall[l, h], hT, ps)
            em.barrier()
            P.close()
            mixer_fourier(em, nc, hT, yT, pv, fcd, f64d, wfall[l, h], ps, yr=0 * 256 + h * 64)
            P = Pool(nc)
            mixer_lru(em, nc, P, hT, yT, pv, lwall[l, h], ps, yr=1 * 256 + h * 64)
            em.barrier()
            P.close()
            mixer_attn(em, nc, hT, yT, pv, posd, identd, maskd, seld, vaug, ps, yr=2 * 256 + h * 64)
            mixer_dn(em, nc, hT, yT, pv, identd, dncd, dscall[l, h], ps, yr=3 * 256 + h * 64)
        body_B(em, nc, yT, xtok_l, wo[l], ln1g[l], ln1b[l], rw[l], identd, x1, None, ps, ntok=S, affT=affT, moe0=moe)
        Pc, idx_i, mc = body_C(em, nc, x1, affT, w1[l], w3[l], w2[l], mcd, ygs, None, xgs, None, ps, NB=1, NE=nexp, moe=moe)
        body_D(em, nc, idx_i, mc, ygs, x1, ln2g[l], ln2b[l], out if last else x2, None if last else x2T, ps, NG=nexp, alpha=ALPHA, moe=moe)
        Pc.close()
    em.finish("sp")
    print("FULL instructions:", em.nins)
    return nc


def prep_full(inp, b):
    f = lambda a: np.ascontiguousarray(a, dtype=np.float32)
    fc, f64 = fourier_consts()
    ident, mask, sel = attn_consts()
    whall = np.zeros((NL, 4, DM, NCOL), np.float32)
    pvall = np.zeros((NL, 4, 64, NPV), np.float32)
    lwall = np.zeros((NL, 4, 64, 4, 64), np.float32)
    wfall = np.zeros((NL, 4, 64, 64), np.float32)
    dscall = np.zeros((NL, 4, 128, 4), np.float32)
    for l in range(NL):
        for h in range(4):
            m = prep_A(inp, l, b, h, None)
            whall[l, h] = m["wh"]
            pvall[l, h] = m["pv"]
            lwall[l, h] = m["lw"]
            wfall[l, h] = m["wf"]
            dscall[l, h] = m["dsc"]
    return {
        "xT": f(inp['x'][b].T), "xtok": f(inp['x'][b]), "pos": np.ascontiguousarray(inp['positions'][b:b + 1]).astype(np.int32),
        "whall": whall, "pvall": pvall, "lwall": lwall, "wfall": wfall, "dscall": dscall,
        "fc": fc, "f64": f64, "ident": ident, "amask": mask, "asel": sel, "dnc": dn_consts(), "mc": moe_consts(),
        "wo": f(inp['w_out']), "ln1g": f(inp['ln1_g'])[:, None, :], "ln1b": f(inp['ln1_b'])[:, None, :],
        "ln2g": f(inp['ln2_g'])[:, None, :], "ln2b": f(inp['ln2_b'])[:, None, :], "rw": f(inp['router_w']),
        "w1": f(inp['exp_w1']), "w3": f(inp['exp_w3']), "w2": f(inp['exp_w2']),
    }


def kernel(**inputs):
    from concourse.bass_utils import run_bass_kernel_spmd
    inp = {k: np.asarray(v) for k, v in inputs.items()}
    nc = build_full()
    in_maps = [prep_full(inp, b) for b in range(2)]
    res = run_bass_kernel_spmd(nc, in_maps, core_ids=[0, 1])
    return np.stack([res.results[b]["out"] for b in range(2)], axis=0).astype(np.float32)
```

```python
import numpy as np
import concourse.bass as bass
import concourse.mybir as mybir

F32 = mybir.dt.float32
BF16 = mybir.dt.bfloat16
I32 = mybir.dt.int32
AF = mybir.ActivationFunctionType
ALU = mybir.AluOpType
AX = mybir.AxisListType


class Res:
    __slots__ = ("lastw", "rd_c", "rd_d")

    def __init__(self):
        self.lastw = None
        self.rd_c = {}
        self.rd_d = []


class EM:
    CE = ("pe", "act", "dve", "pool")

    def __init__(self, nc, ndma=96, same_eng_sync=True):
        self.nc = nc
        self.eng = dict(pe=nc.tensor, act=nc.scalar, dve=nc.vector, pool=nc.gpsimd, sp=nc.sync)
        self.sem = {e: nc.alloc_semaphore("cs_" + e) for e in self.CE}
        self.cnt = {e: 0 for e in self.CE}
        self.seen = {e: {f: 0 for f in self.CE} for e in self.eng}
        self.dsem = [nc.alloc_semaphore("ds%d" % i) for i in range(ndma)]
        self.dcnt = [0] * ndma
        self.dseen = {e: [0] * ndma for e in self.eng}
        self.drr = 0
        self.res = {}
        self.same = same_eng_sync
        self.nins = 0

    def _res(self, k):
        r = self.res.get(k)
        if r is None:
            r = self.res[k] = Res()
        return r

    def _wait(self, e, tok):
        if tok is None:
            return
        if tok[0] == "c":
            _, f, n = tok
            if f == e and (not self.same or e == "pe"):
                return
            if self.seen[e][f] >= n:
                return
            self.eng[e].wait_ge(self.sem[f], n)
            self.seen[e][f] = n
        else:
            _, si, v = tok
            if self.dseen[e][si] >= v:
                return
            self.eng[e].wait_ge(self.dsem[si], v)
            self.dseen[e][si] = v

    def _deps(self, e, reads, writes):
        for k in reads:
            self._wait(e, self._res(k).lastw)
        for k in writes:
            r = self._res(k)
            self._wait(e, r.lastw)
            for f, n in r.rd_c.items():
                self._wait(e, ("c", f, n))
            for t in r.rd_d:
                self._wait(e, t)

    def _record(self, tok, reads, writes):
        for k in reads:
            r = self._res(k)
            if tok[0] == "c":
                if r.rd_c.get(tok[1], 0) < tok[2]:
                    r.rd_c[tok[1]] = tok[2]
            else:
                r.rd_d.append(tok)
        for k in writes:
            r = self._res(k)
            r.lastw = tok
            r.rd_c = {}
            r.rd_d = []

    def op(self, e, fn, reads=(), writes=()):
        pr = [k for k in reads if isinstance(k, tuple) and isinstance(k[0], str) and k[0].startswith("ps")]
        if pr:
            writes = list(writes) + pr
        self._deps(e, reads, writes)
        ins = fn(self.eng[e])
        self.cnt[e] += 1
        ins.then_inc(self.sem[e], 1)
        tok = ("c", e, self.cnt[e])
        self._record(tok, reads, writes)
        self.nins += 1
        return tok

    def dma(self, q, out, in_, reads=(), writes=(), fn=None, **kw):
        si = self.drr
        self.drr = (self.drr + 1) % len(self.dsem)
        if self.dcnt[si] > 0:
            self._wait(q, ("d", si, 16 * self.dcnt[si]))
        self._deps(q, reads, writes)
        if fn is None:
            ins = self.eng[q].dma_start(out=out, in_=in_, **kw)
        else:
            ins = fn(self.eng[q])
        self.dcnt[si] += 1
        ins.then_inc(self.dsem[si], 16)
        tok = ("d", si, 16 * self.dcnt[si])
        self._record(tok, reads, writes)
        self.nins += 1
        return tok

    def barrier(self):
        for e in self.eng:
            self.finish(e)

    def finish(self, e="sp"):
        for si in range(len(self.dsem)):
            if self.dcnt[si] > 0:
                self._wait(e, ("d", si, 16 * self.dcnt[si]))
        for f in self.CE:
            if self.cnt[f] > 0:
                self._wait(e, ("c", f, self.cnt[f]))


class Pool:
    def __init__(self, nc):
        import contextlib
        self.nc = nc
        self.st = contextlib.ExitStack()

    _ctr = [0]

    def sb(self, name, shape, dt=F32):
        Pool._ctr[0] += 1
        return self.st.enter_context(self.nc.sbuf_tensor("%s_%d" % (name, Pool._ctr[0]), list(shape), dt))

    def ps(self, name, shape, dt=F32):
        Pool._ctr[0] += 1
        return self.st.enter_context(self.nc.psum_tensor("%s_%d" % (name, Pool._ctr[0]), list(shape), dt))

    def close(self):
        self.st.close()


def rev_ap(ap):
    (ps, pn), (fs, fn) = ap.ap
    return bass.AP(ap.tensor, ap.offset + fs * (fn - 1), [[ps, pn], [-fs, fn]])


import numpy as np

S = 16384
DM = 1024
NCOL = 772
C_FA, C_LX, C_LG, C_Q, C_QR, C_K, C_KR, C_V, C_DQ, C_DK, C_DV, C_DG, C_BD = [64 * i for i in range(13)]
PV_LCW = 0
PV_LCB = 4
PV_BA = 5
PV_BX = 7
PV_LAM = 9
PV_FB = 11
PV_DCW = 12
PV_DCB = 24
PV_DNW = 27
PV_INV = 28
PV_SGN = 29
NPV = 32


def stage1(em, nc, P, xT, wh, hT, ps):
    wb = P.sb("s1_wb", [128, 8, NCOL], BF16)
    wtmp = [P.sb("s1_wtmp%d" % i, [128, NCOL], F32) for i in range(2)]
    x32 = [P.sb("s1_x32_%d" % i, [128, 8, 512], F32) for i in range(2)]
    xb = [P.sb("s1_xb_%d" % i, [128, 8, 512], BF16) for i in range(2)]
    ho = [P.sb("s1_ho_%d" % i, [128, 512], F32) for i in range(4)]
    whv = wh.rearrange("(kc p) c -> p kc c", p=128)
    for kc in range(8):
        b = kc % 2
        em.dma("sp", wtmp[b][:], whv[:, kc, :], writes=[("wtmp", b)])
        em.op("dve", lambda e: e.tensor_copy(out=wb[:, kc, :], in_=wtmp[b][:]), reads=[("wtmp", b)], writes=["wb"])
    xTv = xT.rearrange("(kc p) t -> p kc t", p=128)
    groups = [(g * 128, min(128, NCOL - g * 128)) for g in range((NCOL + 127) // 128)]
    pi = 0
    hi = 0
    for tt in range(S // 512):
        b = tt % 2
        em.dma("sp", x32[b][:], xTv[:, :, tt * 512:(tt + 1) * 512], writes=[("x32", b)])
        for kc in range(8):
            eng = ("dve", "pool", "act")[kc % 3] if False else ("dve" if kc % 2 == 0 else "pool")
            em.op(eng, lambda e: e.tensor_copy(out=xb[b][:, kc, :], in_=x32[b][:, kc, :]),
                  reads=[("x32", b)], writes=[("xb", b, kc)])
        for (c0, m) in groups:
            p = pi % 8
            pi += 1
            for kc in range(8):
                em.op("pe", lambda e: e.matmul(ps[p][0:m, :], lhsT=wb[:, kc, c0:c0 + m], rhs=xb[b][:, kc, :],
                                               start=(kc == 0), stop=(kc == 7)),
                      reads=["wb", ("xb", b, kc)], writes=[("ps", p)])
            hb = hi % 4
            hi += 1
            if hi % 2 == 0:
                em.op("act", lambda e: e.copy(out=ho[hb][0:m, :], in_=ps[p][0:m, :]), reads=[("ps", p)], writes=[("ho", hb)])
            else:
                em.op("dve", lambda e: e.tensor_copy(out=ho[hb][0:m, :], in_=ps[p][0:m, :]), reads=[("ps", p)], writes=[("ho", hb)])
            em.dma("sp", hT[c0:c0 + m, tt * 512:(tt + 1) * 512], ho[hb][0:m, :], reads=[("ho", hb)], writes=["hT"])


def mixer_lru(em, nc, P, hT, yT, pv, lw, ps, yr=64):
    CH = 2048
    NCH = S // CH
    lws = P.sb("lru_w", [64, 4, 64], F32)
    em.dma("sp", lws[:], lw, writes=["lru_w"])
    cneg = P.sb("lru_cneg", [64, 4], F32)
    em.op("act", lambda e: e.activation(out=cneg[:, 0:2], in_=pv[:, PV_LAM:PV_LAM + 2], func=AF.Exp, scale=-1.0),
          reads=["pv"], writes=["cneg"])
    em.op("act", lambda e: e.activation(out=cneg[:, 0:2], in_=cneg[:, 0:2], func=AF.Ln, bias=1.0, scale=1.0),
          reads=["cneg"], writes=["cneg"])
    em.op("dve", lambda e: e.tensor_scalar_mul(out=cneg[:, 2:4], in0=cneg[:, 0:2], scalar1=-16.0), reads=["cneg"], writes=["cneg"])
    em.op("dve", lambda e: e.tensor_scalar_mul(out=cneg[:, 0:2], in0=cneg[:, 0:2], scalar1=-8.0), reads=["cneg"], writes=["cneg"])
    hf = P.sb("lru_hf", [64, S], F32)
    xin = [P.sb("lru_xin%d" % i, [64, CH + 3], F32) for i in range(2)]
    gin = [P.sb("lru_gin%d" % i, [64, CH], F32) for i in range(2)]
    xc = P.sb("lru_xc", [64, CH], F32)
    rr = P.sb("lru_r", [64, CH], F32)
    ii = P.sb("lru_i", [64, CH], F32)
    aa = P.sb("lru_a", [64, CH], F32)
    uu = P.sb("lru_u", [64, CH], F32)
    hb = P.sb("lru_hb", [64, CH], F32)
    carry = P.sb("lru_carry", [64, 1], F32)
    li = 0
    for d in range(2):
        order = range(NCH) if d == 0 else range(NCH - 1, -1, -1)
        for ci, c in enumerate(order):
            t0 = c * CH
            b = li % 2
            li += 1
            lo = max(t0 - 2, 0)
            hi_ = min(t0 + CH + 1, S)
            if lo > t0 - 2:
                em.op("pool", lambda e: e.memset(xin[b][:, 0:2], 0.0), writes=[("xin", b)])
            if hi_ < t0 + CH + 1:
                em.op("pool", lambda e: e.memset(xin[b][:, CH + 2:CH + 3], 0.0), writes=[("xin", b)])
            em.dma("sp", xin[b][:, lo - (t0 - 2):hi_ - (t0 - 2)], hT[C_LX:C_LX + 64, lo:hi_], reads=["hT"], writes=[("xin", b)])
            if d == 1:
                em.dma("sp", gin[b][:], hT[C_LG:C_LG + 64, t0:t0 + CH], reads=["hT"], writes=[("gin", b)])
            em.op("dve", lambda e: e.tensor_scalar(out=xc[:], in0=xin[b][:, 0:CH], scalar1=pv[:, PV_LCW:PV_LCW + 1],
                                                   scalar2=pv[:, PV_LCB:PV_LCB + 1], op0=ALU.mult, op1=ALU.add),
                  reads=[("xin", b), "pv"], writes=["xc"])
            for j in range(1, 4):
                em.op("dve", lambda e: e.scalar_tensor_tensor(out=xc[:], in0=xin[b][:, j:j + CH], scalar=pv[:, PV_LCW + j:PV_LCW + j + 1],
                                                              in1=xc[:], op0=ALU.mult, op1=ALU.add),
                      reads=[("xin", b), "pv", "xc"], writes=["xc"])
            for q in range(CH // 512):
                sl = slice(q * 512, (q + 1) * 512)
                pa = (2 * q) % 8
                px = (2 * q + 1) % 8
                em.op("pe", lambda e: e.matmul(ps[pa][0:64, :], lhsT=lws[:, d, :], rhs=xc[:, sl], start=True, stop=True),
                      reads=["lru_w", "xc"], writes=[("ps", pa)])
                em.op("pe", lambda e: e.matmul(ps[px][0:64, :], lhsT=lws[:, 2 + d, :], rhs=xc[:, sl], start=True, stop=True),
                      reads=["lru_w", "xc"], writes=[("ps", px)])
                em.op("act", lambda e: e.activation(out=rr[:, sl], in_=ps[pa][0:64, :], func=AF.Sigmoid,
                                                    bias=pv[:, PV_BA + d:PV_BA + d + 1], scale=1.0),
                      reads=[("ps", pa), "pv"], writes=["rr"])
                em.op("act", lambda e: e.activation(out=ii[:, sl], in_=ps[px][0:64, :], func=AF.Sigmoid,
                                                    bias=pv[:, PV_BX + d:PV_BX + d + 1], scale=1.0),
                      reads=[("ps", px), "pv"], writes=["ii"])
            em.op("act", lambda e: e.activation(out=aa[:], in_=rr[:], func=AF.Exp, scale=cneg[:, d:d + 1]),
                  reads=["rr", "cneg"], writes=["aa"])
            em.op("act", lambda e: e.activation(out=rr[:], in_=rr[:], func=AF.Exp, scale=cneg[:, 2 + d:3 + d]),
                  reads=["rr", "cneg"], writes=["rr"])
            em.op("act", lambda e: e.activation(out=rr[:], in_=rr[:], func=AF.Sqrt, bias=1.0, scale=-1.0),
                  reads=["rr"], writes=["rr"])
            em.op("dve", lambda e: e.tensor_tensor(out=uu[:], in0=rr[:], in1=ii[:], op=ALU.mult), reads=["rr", "ii"], writes=["uu"])
            em.op("dve", lambda e: e.tensor_tensor(out=uu[:], in0=uu[:], in1=xc[:], op=ALU.mult), reads=["uu", "xc"], writes=["uu"])
            if d == 0:
                init = 0.0 if ci == 0 else hf[:, t0 - 1:t0]
                em.op("dve", lambda e: e.tensor_tensor_scan(out=hf[:, t0:t0 + CH], data0=aa[:], data1=uu[:], initial=init,
                                                            op0=ALU.mult, op1=ALU.add),
                      reads=["aa", "uu", "hf"], writes=["hf"])
            else:
                init = 0.0 if ci == 0 else carry[:, 0:1]
                em.op("dve", lambda e: e.tensor_tensor_scan(out=rev_ap(hb[:]), data0=rev_ap(aa[:]), data1=rev_ap(uu[:]), initial=init,
                                                            op0=ALU.mult, op1=ALU.add),
                      reads=["aa", "uu", "carry"], writes=["hb"])
                em.op("dve", lambda e: e.tensor_copy(out=carry[:], in_=hb[:, 0:1]), reads=["hb"], writes=["carry"])
                em.op("act", lambda e: e.activation(out=gin[b][:], in_=gin[b][:], func=AF.Gelu), reads=[("gin", b)], writes=[("gin", b)])
                em.op("pool", lambda e: e.tensor_tensor(out=hb[:], in0=hb[:], in1=hf[:, t0:t0 + CH], op=ALU.add), reads=["hb", "hf"], writes=["hb"])
                em.op("pool", lambda e: e.tensor_tensor(out=gin[b][:], in0=hb[:], in1=gin[b][:], op=ALU.mult), reads=["hb", ("gin", b)], writes=[("gin", b)])
                em.dma("sp", yT[yr:yr + 64, t0:t0 + CH], gin[b][:], reads=[("gin", b)], writes=["yT"])


def head_cols(h):
    cols = []
    seg = lambda k: list(range(k * 256 + h * 64, k * 256 + h * 64 + 64))
    q = seg(3)
    k = seg(4)
    cols += seg(0) + seg(1) + seg(2) + q + q[32:] + q[:32] + k + k[32:] + k[:32] + seg(5) + seg(6) + seg(7) + seg(8) + seg(9)
    cols += [2560 + h, 2560 + 4 + h, 2568 + h, 2568 + 4 + h]
    return np.array(cols)


def prep_A(inp, l, b, h, xT_b):
    f = lambda a: np.ascontiguousarray(a, dtype=np.float32)
    hs = slice(h * 64, h * 64 + 64)
    pv = np.zeros((64, NPV), np.float32)
    pv[:, PV_LCW:PV_LCW + 4] = inp['lru_conv_w'][l][:, 64 + 0 * 0 + h * 64 - 64 + 0:][:, 0:0].T if False else inp['lru_conv_w'][l][:, hs].T
    pv[:, PV_LCB] = inp['lru_conv_b'][l][hs]
    pv[:, PV_BA:PV_BA + 2] = inp['lru_ba'][l][:, hs].T
    pv[:, PV_BX:PV_BX + 2] = inp['lru_bx'][l][:, hs].T
    pv[:, PV_LAM:PV_LAM + 2] = inp['lru_lam'][l][:, hs].T
    pv[:, PV_FB] = inp['fno_b'][l][h]
    for j in range(3):
        pv[:, PV_DCW + 4 * j:PV_DCW + 4 * j + 4] = inp['dn_conv_w'][l][:, j * 256 + h * 64:j * 256 + h * 64 + 64].T
        pv[:, PV_DCB + j] = inp['dn_conv_b'][l][j * 256 + h * 64:j * 256 + h * 64 + 64]
    pv[:, PV_DNW] = inp['dn_norm_w'][l]
    lw = np.stack([inp['lru_wa'][l][0, h], inp['lru_wa'][l][1, h], inp['lru_wx'][l][0, h], inp['lru_wx'][l][1, h]], axis=1)
    fc, f64 = fourier_consts()
    ident, mask, sel = attn_consts()
    inv = np.power(np.float32(10000.0), -(np.arange(32, dtype=np.float32) / np.float32(32))).astype(np.float32)
    pv[:, PV_INV] = np.concatenate([inv, inv])
    pv[:, PV_INV + 2] = np.pi / 2
    pv[0:32, PV_SGN] = -1.0
    pv[32:64, PV_SGN] = 1.0
    return {"xT": xT_b, "wh": f(inp['w_in'][l][:, head_cols(h)]), "pv": pv, "lw": f(lw),
            "fc": fc, "f64": f64, "wf": f(inp['fno_w'][l][h]),
            "pos": np.ascontiguousarray(inp['positions'][b:b + 1]).astype(np.int32), "ident": ident, "amask": mask, "asel": sel, "dnc": dn_consts(),
            "dsc": np.tile(np.concatenate([inp['dn_dt_bias'][l][:, h], inp['dn_a_log'][l][:, h]])[None, :], (128, 1)).astype(np.float32)}


def build_A(debug=False, mixers="b"):
    import concourse.bass as bass
    nc = bass.Bass("TRN2", target_bir_lowering=False)
    xT = nc.dram_tensor("xT", [DM, S], F32, kind="ExternalInput").ap()
    wh = nc.dram_tensor("wh", [DM, NCOL], F32, kind="ExternalInput").ap()
    pvd = nc.dram_tensor("pv", [64, NPV], F32, kind="ExternalInput").ap()
    lw = nc.dram_tensor("lw", [64, 4, 64], F32, kind="ExternalInput").ap()
    fcd = nc.dram_tensor("fc", [128, 5, 128], F32, kind="ExternalInput").ap()
    f64d = nc.dram_tensor("f64", [64, 2, 64], F32, kind="ExternalInput").ap()
    wfd = nc.dram_tensor("wf", [64, 64], F32, kind="ExternalInput").ap()
    posd = nc.dram_tensor("pos", [1, S], I32, kind="ExternalInput").ap()
    identd = nc.dram_tensor("ident", [128, 128], F32, kind="ExternalInput").ap()
    maskd = nc.dram_tensor("amask", [128, 2, 128], F32, kind="ExternalInput").ap()
    seld = nc.dram_tensor("asel", [65, 64], F32, kind="ExternalInput").ap()
    dncd = nc.dram_tensor("dnc", [128, 5, 128], F32, kind="ExternalInput").ap()
    dscd = nc.dram_tensor("dsc", [128, 4], F32, kind="ExternalInput").ap()
    vaug = nc.dram_tensor("vaug", [S + 2 * VPAD, 65], BF16).ap()
    yT = nc.dram_tensor("yT", [256, S], F32, kind="ExternalOutput").ap()
    hT = nc.dram_tensor("hT", [NCOL, S], F32, kind="ExternalOutput" if debug else "Internal").ap()
    em = EM(nc)
    P0 = Pool(nc)
    ps = [P0.ps("ps%d" % i, [128, 512], F32) for i in range(8)]
    pv = P0.sb("pv_sb", [64, NPV], F32)
    em.dma("sp", pv[:], pvd, writes=["pv"])
    P = Pool(nc)
    stage1(em, nc, P, xT, wh, hT, ps)
    em.barrier()
    P.close()
    if "a" in mixers:
        mixer_fourier(em, nc, hT, yT, pv, fcd, f64d, wfd, ps)
    if "c" in mixers:
        mixer_attn(em, nc, hT, yT, pv, posd, identd, maskd, seld, vaug, ps)
    if "d" in mixers:
        mixer_dn(em, nc, hT, yT, pv, identd, dncd, dscd, ps)
    if "b" in mixers:
        P = Pool(nc)
        mixer_lru(em, nc, P, hT, yT, pv, lw, ps)
        em.barrier()
        P.close()
    em.finish("sp")
    print("A instructions:", em.nins)
    return nc


def fourier_consts():
    j = np.arange(128)
    th = 2 * np.pi * np.outer(j, j) / 128.0
    tw = 2 * np.pi * np.outer(j, j) / float(S)
    c = np.arange(64)
    t64 = 2 * np.pi * np.outer(c, c) / 64.0
    fc = np.zeros((128, 5, 128), np.float64)
    fc[:, 0] = np.cos(th)
    fc[:, 1] = np.sin(th)
    fc[:, 2] = -np.sin(th)
    fc[:, 3] = np.cos(tw)
    fc[:, 4] = np.sin(tw)
    f64 = np.zeros((64, 2, 64), np.float64)
    f64[:, 0] = np.cos(t64) / 8.0
    f64[:, 1] = -np.sin(t64) / 8.0
    return fc.astype(np.float32), f64.astype(np.float32)


def mixer_fourier(em, nc, hT, yT, pv, fcd, f64d, wfd, ps, yr=0):
    Pc = Pool(nc)
    fc = Pc.sb("f_fc", [128, 5, 128], F32)
    f64 = Pc.sb("f_f64", [64, 2, 64], F32)
    wf = Pc.sb("f_wf", [64, 64], F32)
    G = Pc.sb("f_G", [64, 128], F32)
    em.dma("sp", fc[:], fcd, writes=["fc"])
    em.dma("sp", f64[:], f64d, writes=["f64"])
    em.dma("sp", wf[:], wfd, writes=["wf"])
    for r in range(2):
        em.op("pe", lambda e: e.matmul(ps[0][0:64, r * 64:(r + 1) * 64], lhsT=f64[:, r, :], rhs=wf[:], start=True, stop=True),
              reads=["f64", "wf"], writes=[("ps", 0)])
    em.op("dve", lambda e: e.tensor_copy(out=G[:], in_=ps[0][0:64, 0:128]), reads=[("ps", 0)], writes=["G"])
    PZ = Pool(nc)
    Z = PZ.sb("f_Z", [128, 128, 128], F32)
    P1 = Pool(nc)
    ha = P1.sb("f_ha", [64, S], F32)
    for q in range(4):
        em.dma("sp", ha[:, q * 4096:(q + 1) * 4096], hT[C_FA:C_FA + 64, q * 4096:(q + 1) * 4096], reads=["hT"], writes=["ha"])
    hav = ha[:].rearrange("c (s1 s2) -> c s2 s1", s2=128)
    pi = 0
    for sb in range(32):
        p = pi % 8
        pi += 1
        pv4 = ps[p][:].rearrange("p (a n) -> p a n", n=128)
        for a in range(4):
            s2 = sb * 4 + a
            em.op("pe", lambda e: e.matmul(pv4[:, a, :], lhsT=hav[:, s2, :], rhs=G[:], start=True, stop=True),
                  reads=["ha", "G"], writes=[("ps", p)])
        eng = "dve" if sb % 2 == 0 else "act"
        if eng == "dve":
            em.op("dve", lambda e: e.tensor_copy(out=Z[:, sb * 4:sb * 4 + 4, :], in_=pv4), reads=[("ps", p)], writes=["Z"])
        else:
            em.op("act", lambda e: e.copy(out=Z[:, sb * 4:sb * 4 + 4, :], in_=pv4), reads=[("ps", p)], writes=["Z"])
    em.barrier()
    P1.close()
    PA = Pool(nc)
    A = PA.sb("f_A", [128, 64, 2, 128], F32)
    for db in range(32):
        p = pi % 8
        pi += 1
        pv4 = ps[p][:].rearrange("p (a n) -> p a n", n=128)
        for dd in range(2):
            d = db * 2 + dd
            zre = Z[:, :, d]
            zim = Z[:, :, 64 + d]
            em.op("pe", lambda e: e.matmul(pv4[:, dd * 2, :], lhsT=zre, rhs=fc[:, 0, :], start=True, stop=False), reads=["Z", "fc"], writes=[("ps", p)])
            em.op("pe", lambda e: e.matmul(pv4[:, dd * 2, :], lhsT=zim, rhs=fc[:, 1, :], start=False, stop=True), reads=["Z", "fc"], writes=[("ps", p)])
            em.op("pe", lambda e: e.matmul(pv4[:, dd * 2 + 1, :], lhsT=zim, rhs=fc[:, 0, :], start=True, stop=False), reads=["Z", "fc"], writes=[("ps", p)])
            em.op("pe", lambda e: e.matmul(pv4[:, dd * 2 + 1, :], lhsT=zre, rhs=fc[:, 2, :], start=False, stop=True), reads=["Z", "fc"], writes=[("ps", p)])
        outv = A[:, db * 2:db * 2 + 2, :, :].rearrange("p d r k -> p (d r) k")
        if db % 2 == 0:
            em.op("dve", lambda e: e.tensor_copy(out=outv, in_=pv4), reads=[("ps", p)], writes=["A"])
        else:
            em.op("act", lambda e: e.copy(out=outv, in_=pv4), reads=[("ps", p)], writes=["A"])
    em.barrier()
    PT = Pool(nc)
    t1 = PT.sb("f_t1", [128, 32, 128], F32)
    t2 = PT.sb("f_t2", [128, 32, 128], F32)
    tc_b = fc[:, 3:4, :].to_broadcast([128, 32, 128])
    ts_b = fc[:, 4:5, :].to_broadcast([128, 32, 128])
    for hh in range(2):
        are = A[:, hh * 32:(hh + 1) * 32, 0, :]
        aim = A[:, hh * 32:(hh + 1) * 32, 1, :]
        em.op("dve", lambda e: e.tensor_tensor(out=t1[:], in0=are, in1=ts_b, op=ALU.mult), reads=["A", "fc"], writes=["t1"])
        em.op("pool", lambda e: e.tensor_tensor(out=t2[:], in0=aim, in1=ts_b, op=ALU.mult), reads=["A", "fc"], writes=["t2"])
        em.op("dve", lambda e: e.tensor_tensor(out=are, in0=are, in1=tc_b, op=ALU.mult), reads=["A", "fc", "t1"], writes=[("A", "re")])
        em.op("pool", lambda e: e.tensor_tensor(out=aim, in0=aim, in1=tc_b, op=ALU.mult), reads=["A", "fc", "t2"], writes=[("A", "im")])
        em.op("dve", lambda e: e.tensor_tensor(out=are, in0=are, in1=t2[:], op=ALU.add), reads=[("A", "re"), "t2"], writes=[("A", "re"), "A"])
        em.op("pool", lambda e: e.tensor_tensor(out=aim, in0=aim, in1=t1[:], op=ALU.subtract), reads=[("A", "im"), "t1"], writes=[("A", "im"), "A"])
    em.barrier()
    PT.close()
    ya = Z[0:64, :, :].rearrange("p a b -> p (a b)")
    yav = ya.rearrange("d (k2 k1) -> d k1 k2", k1=128)
    for kb in range(32):
        p = pi % 8
        pi += 1
        pv4 = ps[p][0:64, :].rearrange("p (a n) -> p a n", n=128)
        for a in range(4):
            k1 = kb * 4 + a
            em.op("pe", lambda e: e.matmul(pv4[:, a, :], lhsT=A[:, :, 0, k1], rhs=fc[:, 0, :], start=True, stop=False), reads=["A", "fc"], writes=[("ps", p)])
            em.op("pe", lambda e: e.matmul(pv4[:, a, :], lhsT=A[:, :, 1, k1], rhs=fc[:, 1, :], start=False, stop=True), reads=["A", "fc"], writes=[("ps", p)])
        em.op("act", lambda e: e.activation(out=yav[:, kb * 4:kb * 4 + 4, :], in_=pv4, func=AF.Identity,
                                            bias=pv[:, PV_FB:PV_FB + 1], scale=1.0 / 128.0),
              reads=[("ps", p), "pv"], writes=["Z"])
    for q in range(4):
        em.dma("sp", yT[yr:yr + 64, q * 4096:(q + 1) * 4096], ya[:, q * 4096:(q + 1) * 4096], reads=["Z"], writes=["yT"])
    em.barrier()
    PA.close()
    PZ.close()
    Pc.close()


VPAD = 1024
TWO_PI = float(np.float32(2 * np.pi))


def attn_consts():
    ident = np.eye(128, dtype=np.float32)
    j = np.arange(128)[:, None]
    i = np.arange(128)[None, :]
    mask = np.zeros((128, 2, 128), np.float32)
    mask[:, 0, :] = (j >= i)
    mask[:, 1, :] = (j <= i)
    sel = np.zeros((65, 64), np.float32)
    sel[64, :] = 1.0
    return ident, mask, sel


def bc_ap(ap, nparts):
    import concourse.bass as bass
    a = ap.ap
    return bass.AP(ap.tensor, ap.offset, [[0, nparts], [a[-1][0], a[-1][1]]])


def mixer_attn(em, nc, hT, yT, pv, posd, identd, maskd, seld, vaug, ps, yr=128):
    Pc = Pool(nc)
    ident = Pc.sb("a_ident", [128, 128], F32)
    mask32 = Pc.sb("a_mask32", [128, 2, 128], F32)
    mask = Pc.sb("a_mask", [128, 2, 128], BF16)
    sel = Pc.sb("a_sel", [65, 64], F32)
    em.dma("sp", ident[:], identd, writes=["ident"])
    em.dma("sp", mask32[:], maskd, writes=["mask32"])
    em.dma("sp", sel[:], seld, writes=["sel"])
    em.op("dve", lambda e: e.tensor_copy(out=mask[:], in_=mask32[:]), reads=["mask32"], writes=["mask"])
    qb = Pc.sb("a_qb", [64, S], BF16)
    kb = Pc.sb("a_kb", [64, S + 2 * VPAD], BF16)
    acc = Pc.sb("a_acc", [65, S], F32)
    em.op("pool", lambda e: e.memset(kb[:, 0:VPAD], 0.0), writes=["kb"])
    em.op("pool", lambda e: e.memset(kb[:, VPAD + S:], 0.0), writes=["kb"])
    P0 = Pool(nc)
    zt = P0.sb("a_zt", [128, 8, 65], BF16)
    em.op("pool", lambda e: e.memset(zt[:], 0.0), writes=["zt"])
    em.dma("sp", vaug[0:VPAD, :].rearrange("(p a) c -> p a c", a=8), zt[:], reads=["zt"], writes=["vaug"])
    em.dma("sp", vaug[VPAD + S:, :].rearrange("(p a) c -> p a c", a=8), zt[:], reads=["zt"], writes=["vaug"])
    v32 = [P0.sb("a_v32_%d" % i, [64, 512], F32) for i in range(2)]
    vt = [P0.sb("a_vt_%d" % i, [128, 4, 65], BF16) for i in range(2)]
    for i in range(2):
        em.op("pool", lambda e: e.memset(vt[i][:], 1.0), writes=[("vt", i)])
    for tt in range(S // 512):
        b = tt % 2
        p = tt % 8
        em.dma("sp", v32[b][:], hT[C_V:C_V + 64, tt * 512:(tt + 1) * 512], reads=["hT"], writes=[("v32", b)])
        pv4 = ps[p][:, 0:256].rearrange("p (a c) -> p a c", c=64)
        for a in range(4):
            em.op("pe", lambda e: e.transpose(out=pv4[:, a, :], in_=v32[b][:, a * 128:(a + 1) * 128], identity=ident[0:64, 0:64]),
                  reads=[("v32", b), "ident"], writes=[("ps", p)])
        em.op("dve", lambda e: e.tensor_copy(out=vt[b][:, :, 0:64], in_=pv4), reads=[("ps", p)], writes=[("vt", b)])
        em.dma("sp", vaug[VPAD + tt * 512:VPAD + (tt + 1) * 512, :].rearrange("(a p) c -> p a c", p=128), vt[b][:],
               reads=[("vt", b)], writes=["vaug"])
    em.barrier()
    P0.close()
    P1 = Pool(nc)
    CH = 1024
    posi = P1.sb("a_posi", [64, CH], I32)
    ang = P1.sb("a_ang", [64, CH], F32)
    cs = P1.sb("a_cs", [64, CH], F32)
    sn = P1.sb("a_sn", [64, CH], F32)
    xin = [P1.sb("a_xin%d" % i, [64, CH], F32) for i in range(4)]
    t1 = P1.sb("a_t1", [64, CH], F32)
    t2 = P1.sb("a_t2", [64, CH], F32)
    for c in range(S // CH):
        t0 = c * CH
        em.dma("sp", posi[:], bc_ap(posd[0:1, t0:t0 + CH], 64), writes=["posi"])
        for i, row in enumerate((C_Q, C_QR, C_K, C_KR)):
            em.dma("sp", xin[i][:], hT[row:row + 64, t0:t0 + CH], reads=["hT"], writes=[("xin", i)])
        em.op("dve", lambda e: e.tensor_copy(out=ang[:], in_=posi[:]), reads=["posi"], writes=["ang"])
        em.op("dve", lambda e: e.tensor_scalar_mul(out=ang[:], in0=ang[:], scalar1=pv[:, PV_INV:PV_INV + 1]), reads=["ang", "pv"], writes=["ang"])
        em.op("dve", lambda e: e.tensor_scalar(out=cs[:], in0=ang[:], scalar1=float(1.0 / (2 * np.pi)), scalar2=12582912.0,
                                               op0=ALU.mult, op1=ALU.add), reads=["ang"], writes=["cs"])
        em.op("dve", lambda e: e.tensor_scalar_add(out=cs[:], in0=cs[:], scalar1=-12582912.0), reads=["cs"], writes=["cs"])
        em.op("dve", lambda e: e.scalar_tensor_tensor(out=ang[:], in0=cs[:], scalar=-6.28125, in1=ang[:], op0=ALU.mult, op1=ALU.add),
              reads=["cs", "ang"], writes=["ang"])
        em.op("dve", lambda e: e.scalar_tensor_tensor(out=ang[:], in0=cs[:], scalar=-0.0019353071795864769, in1=ang[:], op0=ALU.mult, op1=ALU.add),
              reads=["cs", "ang"], writes=["ang"])
        em.op("dve", lambda e: e.tensor_scalar(out=ang[:], in0=ang[:], scalar1=3.1415925, scalar2=-3.1415925, op0=ALU.min, op1=ALU.max),
              reads=["ang"], writes=["ang"])
        em.op("act", lambda e: e.activation(out=sn[:], in_=ang[:], func=AF.Sin), reads=["ang"], writes=["sn"])
        em.op("dve", lambda e: e.scalar_tensor_tensor(out=ang[:], in0=ang[:], scalar=-1.0, in1=ang[:], op0=ALU.mult, op1=ALU.max), reads=["ang", "sn"], writes=["ang"])
        em.op("act", lambda e: e.activation(out=cs[:], in_=ang[:], func=AF.Sin, bias=pv[:, PV_INV + 2:PV_INV + 3], scale=-1.0),
              reads=["ang", "pv"], writes=["cs"])
        em.op("dve", lambda e: e.tensor_scalar_mul(out=sn[:], in0=sn[:], scalar1=pv[:, PV_SGN:PV_SGN + 1]), reads=["sn", "pv"], writes=["sn"])
        for which, dst in ((0, qb[:, t0:t0 + CH]), (1, kb[:, VPAD + t0:VPAD + t0 + CH])):
            em.op("dve", lambda e: e.tensor_tensor(out=t1[:], in0=xin[2 * which][:], in1=cs[:], op=ALU.mult),
                  reads=[("xin", 2 * which), "cs"], writes=["t1"])
            em.op("pool", lambda e: e.tensor_tensor(out=t2[:], in0=xin[2 * which + 1][:], in1=sn[:], op=ALU.mult),
                  reads=[("xin", 2 * which + 1), "sn"], writes=["t2"])
            em.op("dve", lambda e: e.tensor_tensor(out=dst, in0=t1[:], in1=t2[:], op=ALU.add), reads=["t1", "t2"],
                  writes=["qb" if which == 0 else "kb"])
    em.barrier()
    P1.close()
    P2 = Pool(nc)
    vg = [P2.sb("a_vg%d" % i, [128, 5, 65], BF16) for i in range(3)]
    pT = [P2.sb("a_pT%d" % i, [128, 2, 128], BF16) for i in range(4)]
    import concourse.bass as bass
    gi = 0
    bi = 0
    pending = []
    for (win, d) in ((128, 1), (512, 4), (2048, 16)):
        L = S // d
        for r in range(d):
            for g in range(L // 512):
                vb = gi % 3
                po = 4 + gi % 4
                gi += 1
                row0 = VPAD + r + d * (128 * 4 * g - 64)
                src = bass.AP(vaug.tensor, vaug.offset + row0 * 65, [[d * 65, 128], [128 * d * 65, 5], [1, 65]])
                em.dma("sp", vg[vb][:], src, reads=["vaug"], writes=[("vg", vb)])
                for a in range(4):
                    n = 4 * g + a
                    psi = bi % 4
                    pb = bi % 4
                    bi += 1
                    sv = ps[psi][:, 0:256].rearrange("p (a c) -> p a c", c=128)
                    q0 = r + d * 128 * n
                    qv = qb[:, q0:q0 + d * 127 + 1:d]
                    for tl in range(2):
                        k0 = VPAD + r + d * (128 * (n + tl) - 64)
                        kv = kb[:, k0:k0 + d * 127 + 1:d]
                        em.op("pe", lambda e: e.matmul(sv[:, tl, :], lhsT=kv, rhs=qv, start=True, stop=True),
                              reads=["qb", "kb"], writes=[("ps", psi)])
                    em.op("act", lambda e: e.activation(out=pT[pb][:], in_=sv, func=AF.Exp, scale=0.125),
                          reads=[("ps", psi)], writes=[("pT", pb)])
                    em.op("pool", lambda e: e.tensor_tensor(out=pT[pb][:], in0=pT[pb][:], in1=mask[:], op=ALU.mult),
                          reads=[("pT", pb), "mask"], writes=[("pT", pb)])

                    def stage_b(a=a, vb=vb, pb=pb, po=po, d=d, r=r, g=g):
                        for tl in range(2):
                            em.op("pe", lambda e: e.matmul(ps[po][0:65, a * 128:(a + 1) * 128], lhsT=vg[vb][:, a + tl, :], rhs=pT[pb][:, tl, :],
                                                           start=(tl == 0), stop=(tl == 1)),
                                  reads=[("vg", vb), ("pT", pb)], writes=[("ps", po)])
                        if a == 3:
                            a0 = r + d * 512 * g
                            av = acc[:, a0:a0 + d * 511 + 1:d]
                            if d == 1:
                                em.op("dve", lambda e: e.tensor_copy(out=av, in_=ps[po][0:65, :]), reads=[("ps", po)], writes=["acc"])
                            else:
                                em.op("dve", lambda e: e.tensor_tensor(out=av, in0=av, in1=ps[po][0:65, :], op=ALU.add),
                                      reads=[("ps", po), "acc"], writes=["acc"])

                    pending.append(stage_b)
                    if len(pending) > 2:
                        pending.pop(0)()
    while pending:
        pending.pop(0)()
    em.barrier()
    P2.close()
    P3 = Pool(nc)
    rec = [P3.sb("a_rec%d" % i, [64, 512], F32) for i in range(2)]
    yo = [P3.sb("a_yo%d" % i, [64, 512], F32) for i in range(2)]
    for tt in range(S // 512):
        b = tt % 2
        p = tt % 8
        sl = slice(tt * 512, (tt + 1) * 512)
        em.op("pe", lambda e: e.matmul(ps[p][0:64, :], lhsT=sel[:], rhs=acc[:, sl], start=True, stop=True),
              reads=["sel", "acc"], writes=[("ps", p)])
        em.op("dve", lambda e: e.reciprocal(out=rec[b][:], in_=ps[p][0:64, :]), reads=[("ps", p)], writes=[("rec", b)])
        em.op("pool", lambda e: e.tensor_tensor(out=yo[b][:], in0=acc[0:64, sl], in1=rec[b][:], op=ALU.mult),
              reads=["acc", ("rec", b)], writes=[("yo", b)])
        em.dma("sp", yT[yr:yr + 64, sl], yo[b][:], reads=[("yo", b)], writes=["yT"])
    em.barrier()
    P3.close()
    Pc.close()


def dn_consts():
    p = np.arange(128)[:, None]
    f = np.arange(128)[None, :]
    c = np.zeros((128, 5, 128), np.float32)
    c[:, 0] = 1.0
    c[:, 1] = (f >= p)
    c[:, 2] = (f > p)
    c[:, 3] = (f <= p)
    c[:, 4] = (f < p)
    return c


def mixer_dn(em, nc, hT, yT, pv, identd, dncd, dscd, ps, yr=192):
    C = 128
    NCH = S // C
    Pc = Pool(nc)
    ident = Pc.sb("d_ident", [128, 128], F32)
    dnc = Pc.sb("d_dnc", [128, 5, 128], F32)
    dsc = Pc.sb("d_dsc", [128, 4], F32)
    em.dma("sp", ident[:], identd, writes=["d_ident"])
    em.dma("sp", dnc[:], dncd, writes=["dnc"])
    em.dma("sp", dsc[:], dscd, writes=["dsc"])
    ones = dnc[:, 0, :]
    Qn = Pc.sb("d_Qn", [128, NCH, 64], F32)
    Kn = Pc.sb("d_Kn", [128, NCH, 64], F32)
    Vt = Pc.sb("d_Vt", [128, NCH, 64], F32)
    Os = Pc.sb("d_Os", [128, NCH, 64], F32)
    bd = Pc.sb("d_bd", [128, NCH, 4], F32)
    beta = Pc.sb("d_beta", [128, 2, NCH], F32)
    gg = Pc.sb("d_g", [128, 2, NCH], F32)
    gam = Pc.sb("d_gam", [128, 2, NCH], F32)
    egam = Pc.sb("d_egam", [128, 2, NCH], F32)
    etg = Pc.sb("d_etg", [128, 2, NCH], F32)
    etot = Pc.sb("d_etot", [128, 2, NCH], F32)
    bsc = Pc.sb("d_bsc", [128, 2, NCH], F32)
    nA = Pc.sb("d_nA", [128, 2], F32)
    em.op("act", lambda e: e.activation(out=nA[:], in_=dsc[:, 2:4], func=AF.Exp), reads=["dsc"], writes=["nA"])
    em.op("dve", lambda e: e.tensor_scalar_mul(out=nA[:], in0=nA[:], scalar1=-1.0), reads=["nA"], writes=["nA"])
    P1 = Pool(nc)
    CH = 2048
    xin = [P1.sb("d_xin%d" % i, [64, CH + 3], F32) for i in range(3)]
    xc = [P1.sb("d_xc%d" % i, [64, CH], F32) for i in range(3)]
    bdin = P1.sb("d_bdin", [4, CH], F32)
    sq = P1.sb("d_sq", [128, 16, 64], F32)
    ss = P1.sb("d_ss", [128, 2, 16], F32)
    dst = (Qn, Kn, Vt)
    for c in range(S // CH):
        t0 = c * CH
        lo = max(t0 - 2, 0)
        hi_ = min(t0 + CH + 1, S)
        for j, row in enumerate((C_DQ, C_DK, C_DV)):
            if lo > t0 - 2:
                em.op("pool", lambda e: e.memset(xin[j][:, 0:2], 0.0), writes=[("dxin", j)])
            if hi_ < t0 + CH + 1:
                em.op("pool", lambda e: e.memset(xin[j][:, CH + 2:CH + 3], 0.0), writes=[("dxin", j)])
            em.dma("sp", xin[j][:, lo - (t0 - 2):hi_ - (t0 - 2)], hT[row:row + 64, lo:hi_], reads=["hT"], writes=[("dxin", j)])
            eng = "dve"
            em.op(eng, lambda e: e.tensor_scalar(out=xc[j][:], in0=xin[j][:, 0:CH], scalar1=pv[:, PV_DCW + 4 * j:PV_DCW + 4 * j + 1],
                                                 scalar2=pv[:, PV_DCB + j:PV_DCB + j + 1], op0=ALU.mult, op1=ALU.add),
                  reads=[("dxin", j), "pv"], writes=[("dxc", j)])
            for t in range(1, 4):
                em.op(eng, lambda e: e.scalar_tensor_tensor(out=xc[j][:], in0=xin[j][:, t:t + CH],
                                                            scalar=pv[:, PV_DCW + 4 * j + t:PV_DCW + 4 * j + t + 1],
                                                            in1=xc[j][:], op0=ALU.mult, op1=ALU.add),
                      reads=[("dxin", j), "pv", ("dxc", j)], writes=[("dxc", j)])
            em.op("act", lambda e: e.activation(out=xc[j][:], in_=xc[j][:], func=AF.Silu), reads=[("dxc", j)], writes=[("dxc", j)])
        em.dma("sp", bdin[:], hT[C_BD:C_BD + 4, t0:t0 + CH], reads=["hT"], writes=["bdin"])
        for j in range(3):
            for half in range(2):
                p = (2 * j + half) % 6
                pv8 = ps[p][:].rearrange("p (a c) -> p a c", c=64)
                for a in range(8):
                    blk = half * 8 + a
                    em.op("pe", lambda e: e.transpose(out=pv8[:, a, :], in_=xc[j][:, blk * 128:(blk + 1) * 128], identity=ident[0:64, 0:64]),
                          reads=[("dxc", j), "d_ident"], writes=[("ps", p)])
                o_ = dst[j][:, c * 16 + half * 8:c * 16 + half * 8 + 8, :]
                if half == 0:
                    em.op("dve", lambda e: e.tensor_copy(out=o_, in_=pv8), reads=[("ps", p)], writes=[("dst", j)])
                else:
                    em.op("act", lambda e: e.copy(out=o_, in_=pv8), reads=[("ps", p)], writes=[("dst", j)])
        pb = ps[6][:, 0:64].rearrange("p (a c) -> p a c", c=4)
        for a in range(16):
            em.op("pe", lambda e: e.transpose(out=pb[:, a, :], in_=bdin[:, a * 128:(a + 1) * 128], identity=ident[0:4, 0:4]),
                  reads=["bdin", "d_ident"], writes=[("ps", 6)])
        em.op("dve", lambda e: e.tensor_copy(out=bd[:, c * 16:(c + 1) * 16, :], in_=pb), reads=[("ps", 6)], writes=["bd"])
        for j in range(2):
            blkv = dst[j][:, c * 16:(c + 1) * 16, :]
            em.op("pool", lambda e: e.tensor_tensor(out=sq[:], in0=blkv, in1=blkv, op=ALU.mult), reads=[("dst", j)], writes=["dsq"])
            em.op("dve", lambda e: e.reduce_sum(out=ss[:, j, :], in_=sq[:], axis=AX.X), reads=["dsq"], writes=["dss"])
            em.op("act", lambda e: e.activation(out=ss[:, j, :], in_=ss[:, j, :], func=AF.Sqrt, bias=1e-6, scale=1.0), reads=["dss"], writes=["dss"])
            em.op("dve", lambda e: e.reciprocal(out=ss[:, j, :], in_=ss[:, j, :]), reads=["dss"], writes=["dss"])
            if j == 0:
                em.op("dve", lambda e: e.tensor_scalar_mul(out=ss[:, j, :], in0=ss[:, j, :], scalar1=0.125), reads=["dss"], writes=["dss"])
            em.op("dve", lambda e: e.tensor_tensor(out=blkv, in0=blkv, in1=ss[:, j, :].unsqueeze(2).to_broadcast([128, 16, 64]), op=ALU.mult),
                  reads=[("dst", j), "dss"], writes=[("dst", j)])
    em.barrier()
    P1.close()
    STOP = 99
    if STOP <= 1:
        Pc.close()
        return
    _sub = [0]
    SUBLIM = 999
    def _ok():
        _sub[0] += 1
        return _sub[0] <= SUBLIM
    for dr in range(2):
        _ok() and em.op("act", lambda e: e.activation(out=beta[:, dr, :], in_=bd[:, :, dr], func=AF.Sigmoid), reads=["bd"], writes=["beta"])
        _ok() and em.op("act", lambda e: e.activation(out=gg[:, dr, :], in_=bd[:, :, 2 + dr], func=AF.Exp, bias=dsc[:, dr:dr + 1], scale=1.0),
              reads=["bd", "dsc"], writes=["gg"])
        _ok() and em.op("act", lambda e: e.activation(out=gg[:, dr, :], in_=gg[:, dr, :], func=AF.Ln, bias=1.0, scale=1.0), reads=["gg"], writes=["gg"])
        _ok() and em.op("dve", lambda e: e.tensor_scalar_mul(out=gg[:, dr, :], in0=gg[:, dr, :], scalar1=nA[:, dr:dr + 1]), reads=["gg", "nA"], writes=["gg"])
        tri = dnc[:, 1, :] if dr == 0 else dnc[:, 3, :]
        _ok() and em.op("pe", lambda e: e.matmul(ps[0][:, 0:NCH], lhsT=tri, rhs=gg[:, dr, :], start=True, stop=True), reads=["dnc", "gg"], writes=[("ps", 0)])
        _ok() and em.op("pe", lambda e: e.matmul(ps[0][:, NCH:2 * NCH], lhsT=ones, rhs=gg[:, dr, :], start=True, stop=True), reads=["dnc", "gg"], writes=[("ps", 0)])
        _ok() and em.op("dve", lambda e: e.tensor_copy(out=gam[:, dr, :], in_=ps[0][:, 0:NCH]), reads=[("ps", 0)], writes=["gam"])
        _ok() and em.op("act", lambda e: e.activation(out=egam[:, dr, :], in_=ps[0][:, 0:NCH], func=AF.Exp), reads=[("ps", 0)], writes=["egam"])
        _ok() and em.op("act", lambda e: e.activation(out=etot[:, dr, :], in_=ps[0][:, NCH:2 * NCH], func=AF.Exp), reads=[("ps", 0)], writes=["etot"])
        _ok() and em.op("dve", lambda e: e.tensor_tensor(out=etg[:, dr, :], in0=ps[0][:, NCH:2 * NCH], in1=gam[:, dr, :], op=ALU.subtract),
              reads=[("ps", 0), "gam"], writes=["etg"])
        _ok() and em.op("act", lambda e: e.activation(out=etg[:, dr, :], in_=etg[:, dr, :], func=AF.Exp), reads=["etg"], writes=["etg"])
        _ok() and em.op("dve", lambda e: e.tensor_tensor(out=bsc[:, dr, :], in0=beta[:, dr, :], in1=egam[:, dr, :], op=ALU.mult),
              reads=["beta", "egam"], writes=["bsc"])
    if STOP <= 2:
        em.barrier()
        Pc.close()
        return
    P2 = Pool(nc)
    KI = 3
    T = lambda n, shape=(128, 128), m=1: [P2.sb("d_%s%d" % (n, i), list(shape), F32) for i in range(KI * m)]
    dgb = T("dgb", (128, 256))
    Dm = T("Dm")
    LTi = T("LTi")
    LTs = T("LTs")
    KQT = T("KQT", (64, 256))
    AT = T("AT", m=2)
    nk = [T("nk%d" % k) for k in range(7)]
    nkT = [T("nkT%d" % k) for k in range(7)]
    Y = [T("Y%d" % k) for k in range(2)]
    X = T("X", m=2)
    UT = T("UT", (64, 128), m=2)
    Qd = T("Qd", (128, 64))
    QdT = T("QdT", (64, 128), m=2)
    Kd = T("Kd", (128, 64), m=2)
    Vn = T("Vn", (128, 64), m=2)
    St = P2.sb("d_St", [64, 64], F32)

    def prep(dr, ch, b, b2):
        K = lambda n: (n, b)
        K2 = lambda n: (n, b2)
        gam_c = gam[:, dr, ch:ch + 1]
        beta_c = beta[:, dr, ch:ch + 1]
        mi = dnc[:, 1, :] if dr == 0 else dnc[:, 3, :]
        ms = dnc[:, 2, :] if dr == 0 else dnc[:, 4, :]
        pR = pK = 2 * b
        pA = pB = pT_ = 2 * b + 1
        em.op("dve", lambda e: e.tensor_scalar_mul(out=dgb[b][:, 0:128], in0=ident[:], scalar1=gam_c), reads=["d_ident", "gam"], writes=[K("dgb")])
        yield
        em.op("pool", lambda e: e.tensor_scalar_mul(out=dgb[b][:, 128:256], in0=ident[:], scalar1=beta_c), reads=["d_ident", "beta"], writes=[K("dgb")])
        yield
        em.op("pe", lambda e: e.matmul(ps[pR][:, 0:256], lhsT=ones, rhs=dgb[b][:], start=True, stop=True), reads=["dnc", K("dgb")], writes=[("ps", pR)])
        yield
        em.op("dve", lambda e: e.tensor_scalar(out=Dm[b][:], in0=ps[pR][:, 0:128], scalar1=gam_c, scalar2=0.0, op0=ALU.subtract, op1=ALU.min),
              reads=[("ps", pR), "gam"], writes=[K("Dm")])
        yield
        em.op("act", lambda e: e.activation(out=Dm[b][:], in_=Dm[b][:], func=AF.Exp), reads=[K("Dm")], writes=[K("Dm")])
        yield
        em.op("pool", lambda e: e.tensor_tensor(out=LTi[b][:], in0=Dm[b][:], in1=mi, op=ALU.mult), reads=[K("Dm"), "dnc"], writes=[K("LTi")])
        yield
        em.op("pool", lambda e: e.tensor_tensor(out=LTs[b][:], in0=Dm[b][:], in1=ms, op=ALU.mult), reads=[K("Dm"), "dnc"], writes=[K("LTs")])
        yield
        em.op("pe", lambda e: e.transpose(out=ps[pT_][0:64, 0:128], in_=Kn[:, ch, :], identity=ident[:]), reads=[("dst", 1), "d_ident"], writes=[("ps", pT_)])
        yield
        em.op("pe", lambda e: e.transpose(out=ps[pT_][0:64, 128:256], in_=Qn[:, ch, :], identity=ident[:]), reads=[("dst", 0), "d_ident"], writes=[("ps", pT_)])
        yield
        em.op("act", lambda e: e.copy(out=KQT[b][:], in_=ps[pT_][0:64, 0:256]), reads=[("ps", pT_)], writes=[K("KQT")])
        yield
        em.op("pe", lambda e: e.matmul(ps[pK][:, 256:384], lhsT=KQT[b][:, 0:128], rhs=KQT[b][:, 0:128], start=True, stop=True), reads=[K("KQT")], writes=[("ps", pK)])
        yield
        em.op("pe", lambda e: e.matmul(ps[pK][:, 384:512], lhsT=KQT[b][:, 0:128], rhs=KQT[b][:, 128:256], start=True, stop=True), reads=[K("KQT")], writes=[("ps", pK)])
        yield
        em.op("dve", lambda e: e.tensor_tensor(out=AT[b2][:], in0=ps[pK][:, 384:512], in1=LTi[b][:], op=ALU.mult), reads=[("ps", pK), K("LTi")], writes=[K2("AT")])
        yield
        em.op("dve", lambda e: e.tensor_tensor(out=nkT[0][b][:], in0=ps[pK][:, 256:384], in1=LTs[b][:], op=ALU.mult), reads=[("ps", pK), K("LTs")], writes=[K("nkT0")])
        yield
        em.op("dve", lambda e: e.tensor_tensor(out=nkT[0][b][:], in0=nkT[0][b][:], in1=ps[pR][:, 128:256], op=ALU.mult), reads=[("ps", pR), K("nkT0")], writes=[K("nkT0")])
        yield
        em.op("pe", lambda e: e.transpose(out=ps[pT_][:, 384:512], in_=nkT[0][b][:], identity=ident[:]), reads=[K("nkT0"), "d_ident"], writes=[("ps", pT_)])
        yield
        em.op("act", lambda e: e.copy(out=nk[0][b][:], in_=ps[pT_][:, 384:512]), reads=[("ps", pT_)], writes=[K("nk0")])
        yield
        for k in range(1, 7):
            em.op("pe", lambda e: e.matmul(ps[pA][:, 0:128], lhsT=nkT[k - 1][b][:], rhs=nk[k - 1][b][:], start=True, stop=True),
                  reads=[K("nkT%d" % (k - 1)), K("nk%d" % (k - 1))], writes=[("ps", pA)])
            yield
            em.op("pe", lambda e: e.matmul(ps[pB][:, 128:256], lhsT=nk[k - 1][b][:], rhs=nkT[k - 1][b][:], start=True, stop=True),
                  reads=[K("nkT%d" % (k - 1)), K("nk%d" % (k - 1))], writes=[("ps", pB)])
            yield
            if k < 6:
                em.op("dve", lambda e: e.tensor_copy(out=nk[k][b][:], in_=ps[pA][:, 0:128]), reads=[("ps", pA)], writes=[K("nk%d" % k)])
                yield
            em.op("act", lambda e: e.copy(out=nkT[k][b][:], in_=ps[pB][:, 128:256]), reads=[("ps", pB)], writes=[K("nkT%d" % k)])
            yield
        yb = 0
        em.op("pool", lambda e: e.tensor_scalar_mul(out=Y[yb][b][:, 0:64], in0=Vt[:, ch, :], scalar1=beta_c), reads=[("dst", 2), "beta"], writes=[K("Y%d" % yb)])
        yield
        em.op("pool", lambda e: e.tensor_scalar_mul(out=Y[yb][b][:, 64:128], in0=Kn[:, ch, :], scalar1=bsc[:, dr, ch:ch + 1]), reads=[("dst", 1), "bsc"], writes=[K("Y%d" % yb)])
        yield
        for k in range(6, -1, -1):
            pp = pA if k % 2 == 0 else pB
            em.op("pe", lambda e: e.matmul(ps[pp][:, 256:384], lhsT=nkT[k][b][:], rhs=Y[yb][b][:], start=True, stop=True),
                  reads=[K("nkT%d" % k), K("Y%d" % yb)], writes=[("ps", pp)])
            yield
            if k > 0:
                em.op("dve", lambda e: e.tensor_tensor(out=Y[1 - yb][b][:], in0=Y[yb][b][:], in1=ps[pp][:, 256:384], op=ALU.add),
                      reads=[("ps", pp), K("Y%d" % yb)], writes=[K("Y%d" % (1 - yb))])
                yield
                yb = 1 - yb
            else:
                em.op("dve", lambda e: e.tensor_tensor(out=X[b2][:], in0=Y[yb][b][:], in1=ps[pp][:, 256:384], op=ALU.subtract),
                      reads=[("ps", pp), K("Y%d" % yb)], writes=[K2("X")])
                yield
        em.op("pe", lambda e: e.transpose(out=ps[pT_][0:64, 384:512], in_=X[b2][:, 64:128], identity=ident[:]), reads=[K2("X"), "d_ident"], writes=[("ps", pT_)])
        yield
        em.op("act", lambda e: e.copy(out=UT[b2][:], in_=ps[pT_][0:64, 384:512]), reads=[("ps", pT_)], writes=[K2("UT")])
        yield
        em.op("pool", lambda e: e.tensor_scalar_mul(out=Qd[b][:], in0=Qn[:, ch, :], scalar1=egam[:, dr, ch:ch + 1]), reads=[("dst", 0), "egam"], writes=[K("Qd")])
        yield
        em.op("pe", lambda e: e.transpose(out=ps[pT_][0:64, 0:128], in_=Qd[b][:], identity=ident[:]), reads=[K("Qd"), "d_ident"], writes=[("ps", pT_)])
        yield
        em.op("act", lambda e: e.copy(out=QdT[b2][:], in_=ps[pT_][0:64, 0:128]), reads=[("ps", pT_)], writes=[K2("QdT")])
        yield
        em.op("pool", lambda e: e.tensor_scalar_mul(out=Kd[b2][:], in0=Kn[:, ch, :], scalar1=etg[:, dr, ch:ch + 1]), reads=[("dst", 1), "etg"], writes=[K2("Kd")])
        yield

    def seq(dr, ch, b, first):
        K = lambda n: (n, b)
        p7 = 7
        if first:
            em.op("pool", lambda e: e.memset(St[:], 0.0), writes=["St"])
            yield
        em.op("pe", lambda e: e.matmul(ps[p7][:, 0:64], lhsT=UT[b][:], rhs=St[:], start=True, stop=True), reads=[K("UT"), "St"], writes=[("ps", 7)])
        yield
        em.op("dve", lambda e: e.tensor_tensor(out=Vn[b][:], in0=X[b][:, 0:64], in1=ps[p7][:, 0:64], op=ALU.subtract),
              reads=[("ps", 7), K("X")], writes=[K("Vn")])
        yield
        em.op("pe", lambda e: e.matmul(ps[p7][:, 64:128], lhsT=QdT[b][:], rhs=St[:], start=True, stop=False), reads=[K("QdT"), "St"], writes=[("ps", 7)])
        yield
        em.op("pe", lambda e: e.matmul(ps[p7][:, 64:128], lhsT=AT[b][:], rhs=Vn[b][:], start=False, stop=True), reads=[K("AT"), K("Vn")], writes=[("ps", 7)])
        yield
        em.op("pe", lambda e: e.matmul(ps[p7][0:64, 128:192], lhsT=Kd[b][:], rhs=Vn[b][:], start=True, stop=True), reads=[K("Kd"), K("Vn")], writes=[("ps", 7)])
        yield
        em.op("dve", lambda e: e.scalar_tensor_tensor(out=St[:], in0=St[:], scalar=etot[0:64, dr, ch:ch + 1], in1=ps[p7][0:64, 128:192],
                                                      op0=ALU.mult, op1=ALU.add), reads=[("ps", 7), "St", "etot"], writes=["St"])
        yield
        if dr == 0:
            em.op("act", lambda e: e.copy(out=Os[:, ch, :], in_=ps[p7][:, 64:128]), reads=[("ps", 7)], writes=["Os"])
            yield
        else:
            em.op("dve", lambda e: e.tensor_tensor(out=Os[:, ch, :], in0=Os[:, ch, :], in1=ps[p7][:, 64:128], op=ALU.add), reads=[("ps", 7), "Os"], writes=["Os"])
            yield

    def run_il(gens):
        gens = list(gens)
        while gens:
            for g_ in list(gens):
                try:
                    next(g_)
                except StopIteration:
                    gens.remove(g_)

    def seq_chain(dr, grp, par, first_group):
        for s_, ch in enumerate(grp):
            yield from seq(dr, ch, par * KI + s_, first_group and s_ == 0)

    for dr in range(2):
        order = list(range(NCH)) if dr == 0 else list(range(NCH - 1, -1, -1))
        groups = [order[n0:n0 + KI] for n0 in range(0, NCH, KI)]
        run_il([prep(dr, ch, s_, s_) for s_, ch in enumerate(groups[0])])
        for gi_, grp in enumerate(groups):
            par = gi_ % 2
            gens = [seq_chain(dr, grp, par, gi_ == 0)]
            if gi_ + 1 < len(groups):
                gens += [prep(dr, ch, s_, (1 - par) * KI + s_) for s_, ch in enumerate(groups[gi_ + 1])]
            run_il(gens)
    em.barrier()
    P2.close()
    P3 = Pool(nc)
    ssn = P3.sb("d_ssn", [128, NCH], F32)
    sg = [P3.sb("d_sg%d" % i, [64, 512], F32) for i in range(2)]
    yo = [P3.sb("d_yo%d" % i, [64, 512], F32) for i in range(2)]
    em.op("pool", lambda e: e.tensor_tensor(out=Vt[:], in0=Os[:], in1=Os[:], op=ALU.mult), reads=["Os"], writes=[("dst", 2)])
    em.op("dve", lambda e: e.reduce_sum(out=ssn[:], in_=Vt[:], axis=AX.X), reads=[("dst", 2)], writes=["ssn"])
    em.op("act", lambda e: e.activation(out=ssn[:], in_=ssn[:], func=AF.Sqrt, bias=1e-6, scale=1.0 / 64.0), reads=["ssn"], writes=["ssn"])
    em.op("dve", lambda e: e.reciprocal(out=ssn[:], in_=ssn[:]), reads=["ssn"], writes=["ssn"])
    em.op("dve", lambda e: e.tensor_tensor(out=Os[:], in0=Os[:], in1=ssn[:].unsqueeze(2).to_broadcast([128, NCH, 64]), op=ALU.mult),
          reads=["Os", "ssn"], writes=["Os"])
    for tt in range(S // 512):
        b = tt % 2
        p = tt % 6
        sl = slice(tt * 512, (tt + 1) * 512)
        em.dma("sp", sg[b][:], hT[C_DG:C_DG + 64, sl], reads=["hT"], writes=[("sg", b)])
        em.op("act", lambda e: e.activation(out=sg[b][:], in_=sg[b][:], func=AF.Silu), reads=[("sg", b)], writes=[("sg", b)])
        for a in range(4):
            em.op("pe", lambda e: e.transpose(out=ps[p][0:64, a * 128:(a + 1) * 128], in_=Os[:, tt * 4 + a, :], identity=ident[:]),
                  reads=["Os", "d_ident"], writes=[("ps", p)])
        em.op("dve", lambda e: e.scalar_tensor_tensor(out=yo[b][:], in0=ps[p][0:64, :], scalar=pv[:, PV_DNW:PV_DNW + 1], in1=sg[b][:],
                                                      op0=ALU.mult, op1=ALU.mult), reads=[("ps", p), "pv", ("sg", b)], writes=[("yo", b)])
        em.dma("sp", yT[yr:yr + 64, sl], yo[b][:], reads=[("yo", b)], writes=["yT"])
    em.barrier()
    P3.close()
    Pc.close()


import numpy as np

DM = 1024
TPC = 4096
ALPHA = float((2.0 * 4) ** 0.25)
LN_EPS = 1e-5


def layer_norm_gen(em, nc, z, gB, bB, tmp, st, out, tagz, sfx=""):
    em.op("dve", lambda e: e.reduce_sum(out=st[:, 0:1], in_=z[:], axis=AX.X), reads=[tagz], writes=["ln_st" + sfx])
    yield
    em.op("dve", lambda e: e.tensor_scalar_mul(out=st[:, 0:1], in0=st[:, 0:1], scalar1=-1.0 / DM), reads=["ln_st" + sfx], writes=["ln_st" + sfx])
    yield
    em.op("act", lambda e: e.activation(out=z[:], in_=z[:], func=AF.Identity, bias=st[:, 0:1], scale=1.0), reads=[tagz, "ln_st" + sfx], writes=[tagz])
    yield
    em.op("pool", lambda e: e.tensor_tensor(out=tmp[:], in0=z[:], in1=z[:], op=ALU.mult), reads=[tagz], writes=["ln_tmp" + sfx])
    yield
    em.op("dve", lambda e: e.reduce_sum(out=st[:, 1:2], in_=tmp[:], axis=AX.X), reads=["ln_tmp" + sfx], writes=["ln_st" + sfx])
    yield
    em.op("act", lambda e: e.activation(out=st[:, 1:2], in_=st[:, 1:2], func=AF.Sqrt, bias=LN_EPS, scale=1.0 / DM), reads=["ln_st" + sfx], writes=["ln_st" + sfx])
    yield
    em.op("dve", lambda e: e.reciprocal(out=st[:, 1:2], in_=st[:, 1:2]), reads=["ln_st" + sfx], writes=["ln_st" + sfx])
    yield
    em.op("dve", lambda e: e.scalar_tensor_tensor(out=out[:], in0=z[:], scalar=st[:, 1:2], in1=gB[:], op0=ALU.mult, op1=ALU.mult),
          reads=[tagz, "ln_st" + sfx, "gB"], writes=[tagz])
    yield
    em.op("pool", lambda e: e.tensor_tensor(out=out[:], in0=out[:], in1=bB[:], op=ALU.add), reads=[tagz, "bB"], writes=[tagz])
    yield

def layer_norm_tile(em, nc, z, gB, bB, tmp, st, out, tagz, sfx=""):
    for _ in layer_norm_gen(em, nc, z, gB, bB, tmp, st, out, tagz, sfx):
        pass


def run_il_b(gens):
    gens = list(gens)
    while gens:
        for g_ in list(gens):
            try:
                next(g_)
            except StopIteration:
                gens.remove(g_)


def bc_rows(ap, nparts):
    import concourse.bass as bass
    a = ap.ap
    return bass.AP(ap.tensor, ap.offset, [[0, nparts], [a[-1][0], a[-1][1]]])


def body_B(em, nc, yTs, xs, wo, lng, lnb, rw, identd, x1o, affo, ps, ntok=TPC, affT=None, moe0=None):
    P = Pool(nc)
    wob = P.sb("b_wob", [128, 8, DM], BF16)
    wtmp = [P.sb("b_wtmp%d" % i, [128, DM], F32) for i in range(2)]
    rws = P.sb("b_rw", [128, 8, 16], F32)
    gB = P.sb("b_gB", [128, DM], F32)
    bB = P.sb("b_bB", [128, DM], F32)
    ident = P.sb("b_ident", [128, 128], F32)
    em.dma("sp", ident[:], identd, writes=["b_ident"])
    em.dma("sp", gB[:], bc_rows(lng, 128), writes=["gB"])
    em.dma("sp", bB[:], bc_rows(lnb, 128), writes=["bB"])
    em.dma("sp", rws[:], rw.rearrange("(kc p) e -> p kc e", p=128), writes=["rws"])
    wov = wo.rearrange("(kc p) c -> p kc c", p=128)
    for kc in range(8):
        b = kc % 2
        em.dma("sp", wtmp[b][:], wov[:, kc, :], writes=[("bwtmp", b)])
        em.op("dve", lambda e: e.tensor_copy(out=wob[:, kc, :], in_=wtmp[b][:]), reads=[("bwtmp", b)], writes=["wob"])
    y32 = [P.sb("b_y32_%d" % i, [128, 8, 512], F32) for i in range(2)]
    yb = [P.sb("b_yb_%d" % i, [128, 8, 512], BF16) for i in range(2)]
    xt = [P.sb("b_xt%d" % i, [128, DM], F32) for i in range(2)]
    z = [P.sb("b_z%d" % i, [128, DM], F32) for i in range(2)]
    tmp_ = [P.sb("b_tmp%d" % i, [128, DM], F32) for i in range(2)]
    st_ = [P.sb("b_st%d" % i, [128, 4], F32) for i in range(2)]
    x1T_ = [P.sb("b_x1T%d" % i, [128, 8, 128], F32) for i in range(2)]
    lg_ = [P.sb("b_lg%d" % i, [128, 16], F32) for i in range(2)]
    sm_ = [P.sb("b_sm%d" % i, [128, 4], F32) for i in range(2)]
    yTv = yTs.rearrange("(kc p) t -> p kc t", p=128)
    pi = 0
    afT_ = [P.sb("b_afT%d" % i, [16, 128], F32) for i in range(2)]
    for tt in range(ntok // 512):
        b = tt % 2
        em.dma("sp", y32[b][:], yTv[:, :, tt * 512:(tt + 1) * 512], writes=[("y32", b)])
        for kc in range(8):
            eng = "dve" if kc % 2 == 0 else "pool"
            em.op(eng, lambda e: e.tensor_copy(out=yb[b][:, kc, :], in_=y32[b][:, kc, :]), reads=[("y32", b)], writes=[("yb", b, kc)])
        def tile_gen(a, p0, p1):
            ti = tt * 4 + a
            zb = ti % 2
            r0 = ti * 128
            tmp, st, x1T, lg, sm = tmp_[zb], st_[zb], x1T_[zb], lg_[zb], sm_[zb]
            sfx = str(zb)
            afT = afT_[zb]
            em.dma("sp", xt[zb][:], xs[r0:r0 + 128, :], writes=[("xt", zb)])
            yield
            for half in range(2):
                p = p0 if half == 0 else p1
                for kc in range(8):
                    em.op("pe", lambda e: e.matmul(ps[p][:, :], lhsT=yb[b][:, kc, a * 128:(a + 1) * 128], rhs=wob[:, kc, half * 512:(half + 1) * 512],
                                                   start=(kc == 0), stop=(kc == 7)), reads=[("yb", b, kc), "wob"], writes=[("ps", p)])
                    yield
                em.op("dve", lambda e: e.scalar_tensor_tensor(out=z[zb][:, half * 512:(half + 1) * 512], in0=xt[zb][:, half * 512:(half + 1) * 512],
                                                              scalar=ALPHA, in1=ps[p][:, :], op0=ALU.mult, op1=ALU.add),
                      reads=[("xt", zb), ("ps", p)], writes=[("z", zb)])
                yield
            yield from layer_norm_gen(em, nc, z[zb], gB, bB, tmp, st, z[zb], ("z", zb), sfx=sfx)
            em.dma("sp", x1o[r0:r0 + 128, :], z[zb][:], reads=[("z", zb)], writes=["x1o"])
            yield
            if moe0 is not None:
                em.op("act", lambda e: e.mul(out=tmp[:], in_=z[zb][:], mul=ALPHA), reads=[("z", zb)], writes=["ln_tmp" + sfx])
                yield
                em.dma("sp", moe0[r0:r0 + 128, :], tmp[:], reads=["ln_tmp" + sfx], writes=["moe0"])
                yield
            for kc in range(8):
                pt = 4 + 2 * zb + (kc // 4)
                em.op("pe", lambda e: e.transpose(out=ps[pt][:, (kc % 4) * 128:(kc % 4 + 1) * 128], in_=z[zb][:, kc * 128:(kc + 1) * 128], identity=ident[:]),
                      reads=[("z", zb), "b_ident"], writes=[("ps", pt)])
                yield
            em.op("act", lambda e: e.copy(out=x1T[:, 0:4, :], in_=ps[4 + 2 * zb][:, :].rearrange("p (a c) -> p a c", c=128)), reads=[("ps", 4 + 2 * zb)], writes=[("x1T", zb)])
            yield
            em.op("dve", lambda e: e.tensor_copy(out=x1T[:, 4:8, :], in_=ps[5 + 2 * zb][:, :].rearrange("p (a c) -> p a c", c=128)), reads=[("ps", 5 + 2 * zb)], writes=[("x1T", zb)])
            yield
            for kc in range(8):
                em.op("pe", lambda e: e.matmul(ps[p0][:, 0:16], lhsT=x1T[:, kc, :], rhs=rws[:, kc, :], start=(kc == 0), stop=(kc == 7)),
                      reads=[("x1T", zb), "rws"], writes=[("ps", p0)])
                yield
            em.op("dve", lambda e: e.reduce_max(out=sm[:, 0:1], in_=ps[p0][:, 0:16], axis=AX.X), reads=[("ps", p0)], writes=[("sm", zb)])
            yield
            em.op("dve", lambda e: e.tensor_scalar_mul(out=sm[:, 0:1], in0=sm[:, 0:1], scalar1=-1.0), reads=[("sm", zb)], writes=[("sm", zb)])
            yield
            em.op("act", lambda e: e.activation(out=lg[:], in_=ps[p0][:, 0:16], func=AF.Exp, bias=sm[:, 0:1], scale=1.0), reads=[("ps", p0), ("sm", zb)], writes=[("lg", zb)])
            yield
            em.op("dve", lambda e: e.reduce_sum(out=sm[:, 1:2], in_=lg[:], axis=AX.X), reads=[("lg", zb)], writes=[("sm", zb)])
            yield
            em.op("dve", lambda e: e.reciprocal(out=sm[:, 1:2], in_=sm[:, 1:2]), reads=[("sm", zb)], writes=[("sm", zb)])
            yield
            em.op("dve", lambda e: e.tensor_scalar_mul(out=lg[:], in0=lg[:], scalar1=sm[:, 1:2]), reads=[("lg", zb), ("sm", zb)], writes=[("lg", zb)])
            yield
            if affo is not None:
                em.dma("sp", affo[r0:r0 + 128, :], lg[:], reads=[("lg", zb)], writes=["affo"])
                yield
            if affT is not None:
                em.op("pe", lambda e: e.transpose(out=ps[p1][0:16, 0:128], in_=lg[:], identity=ident[:]), reads=[("lg", zb), "b_ident"], writes=[("ps", p1)])
                yield
                em.op("act", lambda e: e.copy(out=afT[:], in_=ps[p1][0:16, 0:128]), reads=[("ps", p1)], writes=[("afT", zb)])
                yield
                em.dma("sp", affT[:, r0:r0 + 128], afT[:], reads=[("afT", zb)], writes=["affT"])
                yield

        run_il_b([tile_gen(0, 0, 1), tile_gen(1, 2, 3)])
        run_il_b([tile_gen(2, 0, 1), tile_gen(3, 2, 3)])
    em.barrier()
    P.close()


def build_B():
    import concourse.bass as bass
    nc = bass.Bass("TRN2", target_bir_lowering=False)
    yTs = nc.dram_tensor("yTs", [DM, TPC], F32, kind="ExternalInput").ap()
    xs = nc.dram_tensor("xs", [TPC, DM], F32, kind="ExternalInput").ap()
    wo = nc.dram_tensor("wo", [DM, DM], F32, kind="ExternalInput").ap()
    lng = nc.dram_tensor("lng", [1, DM], F32, kind="ExternalInput").ap()
    lnb = nc.dram_tensor("lnb", [1, DM], F32, kind="ExternalInput").ap()
    rw = nc.dram_tensor("rw", [DM, 16], F32, kind="ExternalInput").ap()
    identd = nc.dram_tensor("ident", [128, 128], F32, kind="ExternalInput").ap()
    x1o = nc.dram_tensor("x1o", [TPC, DM], F32, kind="ExternalOutput").ap()
    affo = nc.dram_tensor("affo", [TPC, 16], F32, kind="ExternalOutput").ap()
    em = EM(nc)
    P0 = Pool(nc)
    ps = [P0.ps("ps%d" % i, [128, 512], F32) for i in range(8)]
    body_B(em, nc, yTs, xs, wo, lng, lnb, rw, identd, x1o, affo, ps)
    em.finish("sp")
    print("B instructions:", em.nins)
    return nc


import numpy as np

DM = 1024
DFF = 2688
NFC = DFF // 128
SEQ = 16384
CAP = 2048
DMA_W = DM + 24
NITER = 30


def moe_consts():
    p = np.arange(128)[:, None]
    f = np.arange(128)[None, :]
    c = np.zeros((128, 4, 128), np.float32)
    c[:, 3] = p * 128 + f
    c[:, 0] = 1.0
    c[:, 1] = (p < f)
    c[:, 2] = np.eye(128)
    return c


def body_C(em, nc, x1, affs, w1, w3, w2, mcd, yg, idxo, xg, gated, ps, NB=2, NE=2, moe=None):
    import concourse.bass as bass
    NG = NE * NB
    Pc = Pool(nc)
    mc = Pc.sb("c_mc", [128, 4, 128], F32)
    em.dma("sp", mc[:], mcd, writes=["mc"])
    ones = mc[:, 0, :]
    lstr = mc[:, 1, :]
    ident = mc[:, 2, :]
    aff = Pc.sb("c_aff", [128, NG, 128], F32)
    em.dma("sp", aff[:], affs.rearrange("g (p j) -> p g j", j=128), writes=["aff"])
    idx_i = Pc.sb("c_idx", [128, NG, 128], I32)
    P1 = Pool(nc)
    cmp_ = P1.sb("c_cmp", [128, NG, 128], F32)
    lo = P1.sb("c_lo", [128, NG], F32)
    mid = P1.sb("c_mid", [128, NG], F32)
    cnt = P1.sb("c_cnt", [128, NG], F32)
    fl = P1.sb("c_fl", [128, NG], F32)
    em.op("dve", lambda e: e.memset(lo[:], 0.0), writes=["lo"])
    w = 0.5
    for it in range(NITER):
        em.op("dve", lambda e: e.tensor_scalar_add(out=mid[:], in0=lo[:], scalar1=float(w)), reads=["lo"], writes=["mid"])
        em.op("dve", lambda e: e.tensor_tensor(out=cmp_[:], in0=aff[:], in1=mid[:].unsqueeze(2).to_broadcast([128, NG, 128]), op=ALU.is_ge),
              reads=["aff", "mid"], writes=["cmp"])
        em.op("dve", lambda e: e.reduce_sum(out=cnt[:], in_=cmp_[:], axis=AX.X), reads=["cmp"], writes=["cnt"])
        em.op("pe", lambda e: e.matmul(ps[0][:, 0:NG], lhsT=ones, rhs=cnt[:], start=True, stop=True), reads=["mc", "cnt"], writes=[("ps", 0)])
        em.op("dve", lambda e: e.tensor_single_scalar(out=fl[:], in_=ps[0][:, 0:NG], scalar=float(CAP) - 0.5, op=ALU.is_ge), reads=[("ps", 0)], writes=["fl"])
        em.op("dve", lambda e: e.scalar_tensor_tensor(out=lo[:], in0=fl[:], scalar=float(w), in1=lo[:], op0=ALU.mult, op1=ALU.add),
              reads=["fl", "lo"], writes=["lo"])
        w *= 0.5
    cs = P1.sb("c_cs", [128, NG, 128], F32)
    onesj = P1.sb("c_onesj", [128, 128], F32)
    off = P1.sb("c_off", [128, NG], F32)
    em.op("pool", lambda e: e.memset(onesj[:], 1.0), writes=["onesj"])
    em.op("dve", lambda e: e.tensor_tensor(out=cmp_[:], in0=aff[:], in1=lo[:].unsqueeze(2).to_broadcast([128, NG, 128]), op=ALU.is_ge),
          reads=["aff", "lo"], writes=["cmp"])
    for g in range(NG):
        em.op("dve", lambda e: e.tensor_tensor_scan(out=cs[:, g, :], data0=onesj[:], data1=cmp_[:, g, :], initial=0.0, op0=ALU.mult, op1=ALU.add),
              reads=["cmp", "onesj"], writes=["cs"])
    em.op("dve", lambda e: e.tensor_copy(out=cnt[:], in_=cs[:, :, 127]), reads=["cs"], writes=["cnt"])
    em.op("pe", lambda e: e.matmul(ps[0][:, 0:NG], lhsT=lstr, rhs=cnt[:], start=True, stop=True), reads=["mc", "cnt"], writes=[("ps", 0)])
    em.op("dve", lambda e: e.tensor_scalar_add(out=off[:], in0=ps[0][:, 0:NG], scalar1=-1.0), reads=[("ps", 0)], writes=["off"])
    em.op("dve", lambda e: e.tensor_tensor(out=cs[:], in0=cs[:], in1=off[:].unsqueeze(2).to_broadcast([128, NG, 128]), op=ALU.add),
          reads=["cs", "off"], writes=["cs"])
    em.op("dve", lambda e: e.tensor_scalar(out=cmp_[:], in0=cmp_[:], scalar1=-100000.0, scalar2=100000.0, op0=ALU.mult, op1=ALU.add),
          reads=["cmp"], writes=["cmp"])
    em.op("dve", lambda e: e.tensor_tensor(out=cs[:], in0=cs[:], in1=cmp_[:], op=ALU.add), reads=["cs", "cmp"], writes=["cs"])
    em.op("dve", lambda e: e.tensor_copy(out=idx_i[:], in_=cs[:]), reads=["cs"], writes=["idx"])
    if idxo is not None:
        em.dma("sp", idxo.rearrange("g (p j) -> p g j", j=128), idx_i[:], reads=["idx"], writes=["idxo"])
    em.barrier()
    P1.close()
    P2 = Pool(nc)
    xt = [P2.sb("c_xt%d" % i, [128, DMA_W], F32) for i in range(3)]
    bcreg = nc.gpsimd.to_reg(CAP - 1)
    for b in range(NB):
        for j in range(128):
            xb_ = j % 3
            src = bass.AP(x1.tensor, x1.offset + (b * SEQ + j) * DM, [[128 * DM, 128], [1, DM]])
            em.dma("sp", xt[xb_][:, 0:DM], src, writes=[("cxt", xb_)])
            em.op("pool", lambda e: e.tensor_copy(out=xt[xb_][:, DM:DM + NG], in_=aff[:, :, j]), reads=["aff"], writes=[("cxt", xb_)])
            em.op("pool", lambda e: e.tensor_copy(out=xt[xb_][:, DM + 16:DM + 17], in_=mc[:, 3, j:j + 1]), reads=["mc"], writes=[("cxt", xb_)])
            for i in range(NE):
                g = i * NB + b
                em.dma("pool", None, None, reads=[("cxt", xb_), "idx"], writes=["xg"],
                       fn=lambda e: e.indirect_dma_start(out=xg[g], out_offset=bass.IndirectOffsetOnAxis(ap=idx_i[:, g, j:j + 1], axis=0),
                                                         in_=xt[xb_][:], in_offset=None, bounds_check=bcreg, oob_is_err=False))
    em.barrier()
    P2.close()
    P3 = Pool(nc)
    w1b = P3.sb("c_w1b", [128, 8, DFF], BF16)
    w3b = P3.sb("c_w3b", [128, 8, DFF], BF16)
    w2b = P3.sb("c_w2b", [128, NFC, DM], BF16)
    NSTG = 3
    stg = [P3.sb("c_stg%d" % i, [128, DFF], F32) for i in range(NSTG)]
    xr = [P3.sb("c_xr%d" % i, [128, DMA_W], F32) for i in range(2)]
    xgT = [P3.sb("c_xgT%d" % i, [128, 8, 256], BF16) for i in range(2)]
    gt = [P3.sb("c_gt%d" % i, [128, 2], F32) for i in range(2)]
    tk = [P3.sb("c_tk%d" % i, [128, 2], I32) for i in range(2)]
    bcm = nc.gpsimd.to_reg(SEQ - 1)
    prev_toks = []
    cur_toks = []
    sT = [P3.sb("c_sT%d" % i, [128, 256], F32) for i in range(2)]
    gT = [P3.sb("c_gT%d" % i, [128, 256], BF16) for i in range(3)]
    yo = [P3.sb("c_yo%d" % i, [128, DM], F32) for i in range(2)]
    si = 0
    ceng = ("dve", "pool", "act")

    def cast(eng, out, in_, reads, writes):
        if eng == "act":
            em.op("act", lambda e: e.copy(out=out, in_=in_), reads=reads, writes=writes)
        else:
            em.op(eng, lambda e: e.tensor_copy(out=out, in_=in_), reads=reads, writes=writes)

    xi = 0
    hi = 0
    gi = 0
    for i in range(NE):
        prev_toks = cur_toks
        cur_toks = []
        for (wsrc, wdst, tag) in ((w1, w1b, "w1b"), (w3, w3b, "w3b")):
            wv = wsrc[i].rearrange("(kc p) f -> p kc f", p=128)
            for kc in range(8):
                s = si % NSTG
                si += 1
                em.dma("sp", stg[s][:], wv[:, kc, :], writes=[("stg", s)])
                cast(ceng[si % 3], wdst[:, kc, :], stg[s][:], [("stg", s)], [tag])
        w2v = w2[i].rearrange("(fc p) d -> p fc d", p=128)
        for f2 in range(0, NFC, 2):
            n = min(2, NFC - f2)
            s = si % NSTG
            si += 1
            sv = stg[s][:, 0:n * DM].rearrange("p (a d) -> p a d", d=DM)
            em.dma("sp", sv, w2v[:, f2:f2 + n, :], writes=[("stg", s)])
            cast(ceng[si % 3], w2b[:, f2:f2 + n, :], sv, [("stg", s)], ["w2b"])
        for b in range(NB):
            g = i * NB + b
            for sg in range(CAP // 256):
                xb_ = sg % 2
                for st in range(2):
                    r = xi % 2
                    xi += 1
                    s0 = sg * 256 + st * 128
                    em.dma("sp", xr[r][:], xg[g][s0:s0 + 128, :], reads=["xg"], writes=[("xr", r)])
                    em.op("pool", lambda e: e.tensor_copy(out=gt[xb_][:, st:st + 1], in_=xr[r][:, DM + g:DM + g + 1]), reads=[("xr", r)], writes=[("gt", xb_)])
                    em.op("pool", lambda e: e.tensor_copy(out=tk[xb_][:, st:st + 1], in_=xr[r][:, DM + 16:DM + 17]), reads=[("xr", r)], writes=[("tk", xb_)])
                    for hf in range(2):
                        p = 2 + hf
                        for a in range(4):
                            kc = hf * 4 + a
                            em.op("pe", lambda e: e.transpose(out=ps[p][:, a * 128:(a + 1) * 128], in_=xr[r][:, kc * 128:(kc + 1) * 128], identity=ident),
                                  reads=[("xr", r), "mc"], writes=[("ps", p)])
                        o_ = xgT[xb_][:, hf * 4:hf * 4 + 4, st * 128:(st + 1) * 128]
                        i_ = ps[p][:, :].rearrange("p (a c) -> p a c", c=128)
                        if hf == 0:
                            em.op("dve", lambda e: e.tensor_copy(out=o_, in_=i_), reads=[("ps", p)], writes=[("xgT", xb_)])
                        else:
                            em.op("act", lambda e: e.copy(out=o_, in_=i_), reads=[("ps", p)], writes=[("xgT", xb_)])
                for fc in range(NFC):
                    ph = hi % 2
                    hi += 1
                    for (wb, tag, c0) in ((w1b, "w1b", 0), (w3b, "w3b", 256)):
                        for kc in range(8):
                            em.op("pe", lambda e: e.matmul(ps[ph][:, c0:c0 + 256], lhsT=wb[:, kc, fc * 128:(fc + 1) * 128], rhs=xgT[xb_][:, kc, :],
                                                           start=(kc == 0), stop=(kc == 7)), reads=[tag, ("xgT", xb_)], writes=[("ps", ph)])
                    sb_ = hi % 2
                    gb = gi % 3
                    gi += 1
                    em.op("act", lambda e: e.activation(out=sT[sb_][:], in_=ps[ph][:, 0:256], func=AF.Silu), reads=[("ps", ph)], writes=[("sT", sb_)])
                    em.op("dve", lambda e: e.tensor_tensor(out=gT[gb][:], in0=sT[sb_][:], in1=ps[ph][:, 256:512], op=ALU.mult),
                          reads=[("sT", sb_), ("ps", ph)], writes=[("gT", gb)])
                    for st in range(2):
                        for half in range(2):
                            py = 4 + st * 2 + half
                            em.op("pe", lambda e: e.matmul(ps[py][:, :], lhsT=gT[gb][:, st * 128:(st + 1) * 128], rhs=w2b[:, fc, half * 512:(half + 1) * 512],
                                                           start=(fc == 0), stop=(fc == NFC - 1)), reads=[("gT", gb), "w2b"], writes=[("ps", py)])
                for st in range(2):
                    for half in range(2):
                        py = 4 + st * 2 + half
                        o_ = yo[st][:, half * 512:(half + 1) * 512]
                        if half == 0:
                            em.op("act", lambda e: e.activation(out=o_, in_=ps[py][:, :], func=AF.Copy, scale=gt[xb_][:, st:st + 1]),
                                  reads=[("ps", py), ("gt", xb_)], writes=[("yo", st)])
                        else:
                            em.op("dve", lambda e: e.tensor_scalar_mul(out=o_, in0=ps[py][:, :], scalar1=gt[xb_][:, st:st + 1]),
                                  reads=[("ps", py), ("gt", xb_)], writes=[("yo", st)])
                    s0 = sg * 256 + st * 128
                    if moe is None:
                        em.dma("sp", yg[g][s0:s0 + 128, :], yo[st][:], reads=[("yo", st)], writes=["yg"])
                    else:
                        for t_ in prev_toks:
                            em._wait("pool", t_)
                        prev_toks = []
                        tok = em.dma("pool", None, None, reads=[("yo", st), ("tk", xb_)], writes=[],
                                     fn=lambda e: e.indirect_dma_start(out=moe, out_offset=bass.IndirectOffsetOnAxis(ap=tk[xb_][:, st:st + 1], axis=0),
                                                                       in_=yo[st][:], in_offset=None, bounds_check=bcm, oob_is_err=False,
                                                                       compute_op=ALU.add))
                        cur_toks.append(tok)
    em.barrier()
    P3.close()
    return Pc, idx_i, mc


def build_C(NB=2, NE=2, ntok=None):
    import concourse.bass as bass
    nc = bass.Bass("TRN2", target_bir_lowering=False)
    NG = NB * NE
    x1 = nc.dram_tensor("x1", [NB * SEQ, DM], F32, kind="ExternalInput").ap()
    affs = nc.dram_tensor("affs", [NG, SEQ], F32, kind="ExternalInput").ap()
    w1 = nc.dram_tensor("w1", [NE, DM, DFF], F32, kind="ExternalInput").ap()
    w3 = nc.dram_tensor("w3", [NE, DM, DFF], F32, kind="ExternalInput").ap()
    w2 = nc.dram_tensor("w2", [NE, DFF, DM], F32, kind="ExternalInput").ap()
    mcd = nc.dram_tensor("mc", [128, 4, 128], F32, kind="ExternalInput").ap()
    yg = nc.dram_tensor("yg", [NG, CAP, DM], F32, kind="ExternalOutput").ap()
    idxo = nc.dram_tensor("idxo", [NG, SEQ], I32, kind="ExternalOutput").ap()
    xg = [nc.dram_tensor("xg%d" % g, [CAP, DMA_W], F32).ap() for g in range(NG)]
    gated = nc.dram_tensor("gated", [NG, CAP, 1], F32).ap()
    em = EM(nc)
    P0 = Pool(nc)
    ps = [P0.ps("ps%d" % i, [128, 512], F32) for i in range(8)]
    Pc, _, _ = body_C(em, nc, x1, affs, w1, w3, w2, mcd, yg, idxo, xg, gated, ps, NB=NB, NE=NE)
    Pc.close()
    em.finish("sp")
    print("C instructions:", em.nins)
    return nc


def body_D(em, nc, idx_i, mc, ygs, x1, lng, lnb, x2o, x2To, ps, NG=16, alpha=1.0, moe=None):
    import concourse.bass as bass
    ident = mc[:, 2, :]
    P = Pool(nc)
    if moe is None:
        idxf = P.sb("d2_idxf", [128, NG, 128], F32)
        idxTf = P.sb("d2_idxTf", [128, NG, 128], F32)
        selT = P.sb("d2_selT", [128, NG, 128], F32)
        idxTi = P.sb("d2_idxTi", [128, NG, 128], I32)
    gB = P.sb("d2_gB", [128, DM], F32)
    bB = P.sb("d2_bB", [128, DM], F32)
    em.dma("sp", gB[:], bc_rows(lng, 128), writes=["gB"])
    em.dma("sp", bB[:], bc_rows(lnb, 128), writes=["bB"])
    if moe is None:
        em.op("dve", lambda e: e.tensor_copy(out=idxf[:], in_=idx_i[:]), reads=["idx"], writes=["idxf"])
        for g4 in range(NG // 4):
            p = g4 % 4
            for a_ in range(4):
                g = g4 * 4 + a_
                em.op("pe", lambda e: e.transpose(out=ps[p][:, a_ * 128:(a_ + 1) * 128], in_=idxf[:, g, :], identity=ident), reads=["idxf", "mc"], writes=[("ps", p)])
            em.op("dve", lambda e: e.tensor_copy(out=idxTf[:, g4 * 4:g4 * 4 + 4, :], in_=ps[p][:, :].rearrange("p (a c) -> p a c", c=128)),
                  reads=[("ps", p)], writes=["idxTf"])
        em.op("dve", lambda e: e.tensor_single_scalar(out=selT[:], in_=idxTf[:], scalar=float(CAP) - 0.5, op=ALU.is_lt), reads=["idxTf"], writes=["selT"])
        em.op("dve", lambda e: e.tensor_copy(out=idxTi[:], in_=idxTf[:]), reads=["idxTf"], writes=["idxTi"])
        NBUF = 6
        buf = [P.sb("d2_buf%d" % i, [128, DM], F32) for i in range(NBUF)]
        for i in range(NBUF):
            em.op("pool", lambda e: e.memset(buf[i][:], 0.0), writes=[("d2buf", i)])
    z = [P.sb("d2_z%d" % i, [128, DM], F32) for i in range(2)]
    tmp = P.sb("d2_tmp", [128, DM], F32)
    st = P.sb("d2_st", [128, 4], F32)
    xTs = [P.sb("d2_xTs%d" % i, [128, 8, 128], F32) for i in range(2)]
    bcreg = nc.gpsimd.to_reg(CAP - 1)
    bi = 0
    for n in range(SEQ // 128):
        zb = n % 2
        r0 = n * 128
        if moe is not None:
            em.dma("sp", z[zb][:], moe[r0:r0 + 128, :], reads=["moe"], writes=[("d2z", zb)])
        else:
            em.dma("sp", z[zb][:], x1[r0:r0 + 128, :], reads=["x1"], writes=[("d2z", zb)])
            em.op("act", lambda e: e.mul(out=z[zb][:], in_=z[zb][:], mul=float(alpha)), reads=[("d2z", zb)], writes=[("d2z", zb)])
        for g in range(NG if moe is None else 0):
            b_ = bi % NBUF
            bi += 1
            em.dma("pool", None, None, reads=["idxTi", "yg"], writes=[("d2buf", b_)],
                   fn=lambda e: e.indirect_dma_start(out=buf[b_][:], out_offset=None, in_=ygs[g],
                                                     in_offset=bass.IndirectOffsetOnAxis(ap=idxTi[:, g, n:n + 1], axis=0),
                                                     bounds_check=bcreg, oob_is_err=False))
            em.op("dve", lambda e: e.scalar_tensor_tensor(out=z[zb][:], in0=buf[b_][:], scalar=selT[:, g, n:n + 1], in1=z[zb][:],
                                                          op0=ALU.mult, op1=ALU.add), reads=[("d2buf", b_), "selT", ("d2z", zb)], writes=[("d2z", zb)])
        layer_norm_tile(em, nc, z[zb], gB, bB, tmp, st, z[zb], ("d2z", zb))
        em.dma("sp", x2o[r0:r0 + 128, :], z[zb][:], reads=[("d2z", zb)], writes=["x2o"])
        if x2To is not None:
            for kc in range(8):
                pt = 4 + (kc // 4)
                em.op("pe", lambda e: e.transpose(out=ps[pt][:, (kc % 4) * 128:(kc % 4 + 1) * 128], in_=z[zb][:, kc * 128:(kc + 1) * 128], identity=ident),
                      reads=[("d2z", zb), "mc"], writes=[("ps", pt)])
            em.op("act", lambda e: e.copy(out=xTs[zb][:, 0:4, :], in_=ps[4][:, :].rearrange("p (a c) -> p a c", c=128)), reads=[("ps", 4)], writes=[("xTs", zb)])
            em.op("dve", lambda e: e.tensor_copy(out=xTs[zb][:, 4:8, :], in_=ps[5][:, :].rearrange("p (a c) -> p a c", c=128)), reads=[("ps", 5)], writes=[("xTs", zb)])
            em.dma("sp", x2To.rearrange("(kc p) t -> p kc t", p=128)[:, :, r0:r0 + 128], xTs[zb][:], reads=[("xTs", zb)], writes=["x2To"])
    em.barrier()
    P.close()


import numpy as np

NL = 4


def build_full(nlayers=NL, nheads=4, nexp=16, dbg=False):
    import concourse.bass as bass
    nc = bass.Bass("TRN2", target_bir_lowering=False)
    I = lambda name, shape, dt=F32: nc.dram_tensor(name, list(shape), dt, kind="ExternalInput").ap()
    xT = I("xT", [DM, S])
    xtok = I("xtok", [S, DM])
    posd = I("pos", [1, S], I32)
    whall = I("whall", [NL, 4, DM, NCOL])
    pvall = I("pvall", [NL, 4, 64, NPV])
    lwall = I("lwall", [NL, 4, 64, 4, 64])
    wfall = I("wfall", [NL, 4, 64, 64])
    dscall = I("dscall", [NL, 4, 128, 4])
    fcd = I("fc", [128, 5, 128])
    f64d = I("f64", [64, 2, 64])
    identd = I("ident", [128, 128])
    maskd = I("amask", [128, 2, 128])
    seld = I("asel", [65, 64])
    dncd = I("dnc", [128, 5, 128])
    mcd = I("mc", [128, 4, 128])
    wo = I("wo", [NL, DM, DM])
    ln1g = I("ln1g", [NL, 1, DM])
    ln1b = I("ln1b", [NL, 1, DM])
    ln2g = I("ln2g", [NL, 1, DM])
    ln2b = I("ln2b", [NL, 1, DM])
    rw = I("rw", [NL, DM, 16])
    w1 = I("w1", [NL, 16, DM, DFF])
    w3 = I("w3", [NL, 16, DM, DFF])
    w2 = I("w2", [NL, 16, DFF, DM])
    out = nc.dram_tensor("out", [S, DM], F32, kind="ExternalOutput").ap()
    T = lambda name, shape, dt=F32: nc.dram_tensor(name, list(shape), dt).ap()
    hT = T("hT", [NCOL, S])
    vaug = T("vaug", [S + 2 * VPAD, 65], BF16)
    yT = T("yT", [DM, S])
    x1 = T("x1", [S, DM])
    affT = T("affT", [16, S])
    moe = T("moe", [S, DM])
    x2 = T("x2", [S, DM])
    x2T = T("x2T", [DM, S])
    xgs = [T("xg%d" % g, [CAP, DMA_W]) for g in range(16)]
    ygs = [T("yg%d" % g, [CAP, DM]) for g in range(16)]
    em = EM(nc)
    P0 = Pool(nc)
    ps = [P0.ps("ps%d" % i, [128, 512], F32) for i in range(8)]
    pv = P0.sb("pv_sb", [64, NPV], F32)
    for l in range(nlayers):
        xT_l = xT if l == 0 else x2T
        xtok_l = xtok if l == 0 else x2
        last = (l == nlayers - 1)
        for h in range(nheads):
            em.dma("sp", pv[:], pvall[l, h], writes=["pv"])
            P = Pool(nc)
            stage1(em, nc, P, xT_l, whall[l, h], hT, ps)
            em.barrier()
            P.close()
            mixer_fourier(em, nc, hT, yT, pv, fcd, f64d, wfall[l, h], ps, yr=0 * 256 + h * 64)
            P = Pool(nc)
            mixer_lru(em, nc, P, hT, yT, pv, lwall[l, h], ps, yr=1 * 256 + h * 64)
            em.barrier()
            P.close()
            mixer_attn(em, nc, hT, yT, pv, posd, identd, maskd, seld, vaug, ps, yr=2 * 256 + h * 64)
            mixer_dn(em, nc, hT, yT, pv, identd, dncd, dscall[l, h], ps, yr=3 * 256 + h * 64)
        body_B(em, nc, yT, xtok_l, wo[l], ln1g[l], ln1b[l], rw[l], identd, x1, None, ps, ntok=S, affT=affT, moe0=moe)
        Pc, idx_i, mc = body_C(em, nc, x1, affT, w1[l], w3[l], w2[l], mcd, ygs, None, xgs, None, ps, NB=1, NE=nexp, moe=moe)
        body_D(em, nc, idx_i, mc, ygs, x1, ln2g[l], ln2b[l], out if last else x2, None if last else x2T, ps, NG=nexp, alpha=ALPHA, moe=moe)
        Pc.close()
    em.finish("sp")
    print("FULL instructions:", em.nins)
    return nc


def prep_full(inp, b):
    f = lambda a: np.ascontiguousarray(a, dtype=np.float32)
    fc, f64 = fourier_consts()
    ident, mask, sel = attn_consts()
    whall = np.zeros((NL, 4, DM, NCOL), np.float32)
    pvall = np.zeros((NL, 4, 64, NPV), np.float32)
    lwall = np.zeros((NL, 4, 64, 4, 64), np.float32)
    wfall = np.zeros((NL, 4, 64, 64), np.float32)
    dscall = np.zeros((NL, 4, 128, 4), np.float32)
    for l in range(NL):
        for h in range(4):
            m = prep_A(inp, l, b, h, None)
            whall[l, h] = m["wh"]
            pvall[l, h] = m["pv"]
            lwall[l, h] = m["lw"]
            wfall[l, h] = m["wf"]
            dscall[l, h] = m["dsc"]
    return {
        "xT": f(inp['x'][b].T), "xtok": f(inp['x'][b]), "pos": np.ascontiguousarray(inp['positions'][b:b + 1]).astype(np.int32),
        "whall": whall, "pvall": pvall, "lwall": lwall, "wfall": wfall, "dscall": dscall,
        "fc": fc, "f64": f64, "ident": ident, "amask": mask, "asel": sel, "dnc": dn_consts(), "mc": moe_consts(),
        "wo": f(inp['w_out']), "ln1g": f(inp['ln1_g'])[:, None, :], "ln1b": f(inp['ln1_b'])[:, None, :],
        "ln2g": f(inp['ln2_g'])[:, None, :], "ln2b": f(inp['ln2_b'])[:, None, :], "rw": f(inp['router_w']),
        "w1": f(inp['exp_w1']), "w3": f(inp['exp_w3']), "w2": f(inp['exp_w2']),
    }


def kernel(**inputs):
    from concourse.bass_utils import run_bass_kernel_spmd
    inp = {k: np.asarray(v) for k, v in inputs.items()}
    nc = build_full()
    in_maps = [prep_full(inp, b) for b in range(2)]
    res = run_bass_kernel_spmd(nc, in_maps, core_ids=[0, 1])
    return np.stack([res.results[b]["out"] for b in range(2)], axis=0).astype(np.float32)
```
